# Optimizing a Trainium2 kernel written in Bass

```python
import math
import jax, jax.numpy as jnp
from jax import lax
import numpy as np

D_MODEL = 1024
BATCH = 4
SEQ = 4096
DEPTH = 4

CHUNK = 64
Q_BLOCK = 128
N_MIXERS = 2
N_RET_LAYERS = (DEPTH + 1) // 2
N_DIFF_LAYERS = DEPTH // 2
MIX_WIDTH = D_MODEL
MEM_LEN = 256
MEM_HEADS = 4
MEM_HEAD_DIM = 64
MEM_WIDTH = MEM_HEADS * MEM_HEAD_DIM
MAIN_WIDTH = MIX_WIDTH - MEM_WIDTH
RET_HEADS = 6
RET_V_DIM = MAIN_WIDTH // RET_HEADS
RET_QK_DIM = RET_V_DIM // 2
DIFF_HEADS = 6
DIFF_HEAD_DIM = MAIN_WIDTH // (2 * DIFF_HEADS)
DIFF_V_DIM = 2 * DIFF_HEAD_DIM
IN_WIDTH = 3 * MAIN_WIDTH + MEM_WIDTH
FFN_DIM = -(-8 * D_MODEL // (3 * 256)) * 256
EPS = 1e-6

kernel_name = 'hybrid_retention_diffattn_block'


def rms_norm(x, w):
    x32 = x.astype(jnp.float32)
    y = x32 * lax.rsqrt(jnp.mean(x32 * x32, axis=-1, keepdims=True) + EPS)
    return (y * w.astype(jnp.float32)).astype(x.dtype)


def retention(q, k, v):
    B, S, H, dk = q.shape
    dv = v.shape[-1]
    nc = S // CHUNK
    dt = q.dtype
    log_gamma = jnp.log(1.0 - 2.0 ** (-5.0 - jnp.arange(H, dtype=jnp.float32)))
    k = k * (dk ** -0.5)
    q = q.reshape(B, nc, CHUNK, H, dk)
    k = k.reshape(B, nc, CHUNK, H, dk)
    v = v.reshape(B, nc, CHUNK, H, dv)
    pos = jnp.arange(CHUNK, dtype=jnp.float32)
    d_in = jnp.exp(log_gamma[:, None, None] * jnp.abs(pos[:, None] - pos[None, :])).astype(dt)
    s_in = jnp.einsum('bnihd,bnjhd->bnhij', q, k) * d_in
    o_in = jnp.einsum('bnhij,bnjhe->bnihe', s_in, v)
    k_dec = jnp.exp(log_gamma[:, None] * (CHUNK - 1 - pos)[None, :]).astype(dt)
    kv = jnp.einsum('bnjhd,hj,bnjhe->bnhde', k, k_dec, v)
    chunk_dec = jnp.exp(log_gamma * CHUNK).astype(dt)[:, None, None]

    def step(state, kv_c):
        return state * chunk_dec + kv_c, state

    init = jnp.zeros((B, H, dk, dv), dtype=kv.dtype)
    _, prev = lax.scan(step, init, jnp.moveaxis(kv, 1, 0))
    prev = jnp.moveaxis(prev, 0, 1)
    q_dec = jnp.exp(log_gamma[:, None] * (pos + 1.0)[None, :]).astype(dt)
    o_x = jnp.einsum('bnihd,hi,bnhde->bnihe', q, q_dec, prev)
    return (o_in + o_x).reshape(B, S, H, dv)


def diff_attention(q, k, v, lam):
    B, S, H, _, d = q.shape
    nb = S // Q_BLOCK
    slopes = 2.0 ** (-8.0 * jnp.arange(1, H + 1, dtype=jnp.float32) / H)
    s_pos = jnp.arange(S)
    s_chunk = s_pos // CHUNK
    qb = jnp.moveaxis(q.reshape(B, nb, Q_BLOCK, H, 2, d), 1, 0)
    starts = jnp.arange(nb, dtype=jnp.int32) * Q_BLOCK
    scale = d ** -0.5

    def block(args):
        qblk, t0 = args
        t = t0 + jnp.arange(Q_BLOCK)
        sc = jnp.einsum('bihcd,bjhcd->bhcij', qblk, k).astype(jnp.float32) * scale
        dist = jnp.abs(t[:, None] - s_pos[None, :]).astype(jnp.float32)
        bias = -slopes[:, None, None] * dist[None]
        allowed = s_chunk[None, :] <= (t // CHUNK)[:, None]
        sc = jnp.where(allowed, sc + bias[None, :, None], -jnp.inf)
        p = jax.nn.softmax(sc, axis=-1)
        a = p[:, :, 0] - lam * p[:, :, 1]
        return jnp.einsum('bhij,bjhe->bihe', a.astype(v.dtype), v)

    o = lax.map(block, (qb, starts))
    return jnp.moveaxis(o, 0, 1).reshape(B, S, H, v.shape[-1])


def memory_attention(qc, mk, mv):
    sc = jnp.einsum('bshd,bmhd->bhsm', qc, mk).astype(jnp.float32) * (qc.shape[-1] ** -0.5)
    p = jax.nn.softmax(sc, axis=-1).astype(mv.dtype)
    return jnp.einsum('bhsm,bmhd->bshd', p, mv)


def setup_inputs(seed: int = 0) -> dict:
    key = jax.random.key(seed)
    ks = jax.random.split(key, 24)
    n = jax.random.normal
    f = jnp.float32

    def gain(k, shape):
        return 1.0 + 0.02 * n(k, shape, f)

    return {
        'x': n(ks[0], (BATCH, SEQ, D_MODEL), f),
        'mem': n(ks[1], (BATCH, MEM_LEN, D_MODEL), f),
        'attn_norm_w': gain(ks[2], (DEPTH, D_MODEL)),
        'w_in': n(ks[3], (DEPTH, D_MODEL, IN_WIDTH), f) * D_MODEL ** -0.5,
        'w_out': n(ks[4], (DEPTH, MIX_WIDTH, D_MODEL), f) * MIX_WIDTH ** -0.5,
        'mem_norm_w': gain(ks[5], (D_MODEL,)),
        'w_mem_kv': n(ks[6], (DEPTH, D_MODEL, 2 * MEM_WIDTH), f) * D_MODEL ** -0.5,
        'mem_q_norm_w': gain(ks[7], (DEPTH, MEM_HEAD_DIM)),
        'mem_k_norm_w': gain(ks[8], (DEPTH, MEM_HEAD_DIM)),
        'ret_gn_w': gain(ks[9], (N_RET_LAYERS, RET_HEADS, RET_V_DIM)),
        'diff_q_norm_w': gain(ks[10], (N_DIFF_LAYERS, DIFF_HEAD_DIM)),
        'diff_k_norm_w': gain(ks[11], (N_DIFF_LAYERS, DIFF_HEAD_DIM)),
        'diff_lambda_q1': 0.1 * n(ks[12], (N_DIFF_LAYERS, DIFF_HEAD_DIM), f),
        'diff_lambda_k1': 0.1 * n(ks[13], (N_DIFF_LAYERS, DIFF_HEAD_DIM), f),
        'diff_lambda_q2': 0.1 * n(ks[14], (N_DIFF_LAYERS, DIFF_HEAD_DIM), f),
        'diff_lambda_k2': 0.1 * n(ks[15], (N_DIFF_LAYERS, DIFF_HEAD_DIM), f),
        'diff_subln_w': gain(ks[16], (N_DIFF_LAYERS, DIFF_V_DIM)),
        'ffn_norm_w': gain(ks[17], (DEPTH, D_MODEL)),
        'w_gate_up': n(ks[18], (DEPTH, D_MODEL, 2 * FFN_DIM), f) * D_MODEL ** -0.5,
        'w_down': n(ks[19], (DEPTH, FFN_DIM, D_MODEL), f) * FFN_DIM ** -0.5,
    }


def reference(x, mem, attn_norm_w, w_in, w_out, mem_norm_w, w_mem_kv, mem_q_norm_w,
              mem_k_norm_w, ret_gn_w, diff_q_norm_w, diff_k_norm_w, diff_lambda_q1,
              diff_lambda_k1, diff_lambda_q2, diff_lambda_k2, diff_subln_w, ffn_norm_w,
              w_gate_up, w_down):
    B, S, _ = x.shape
    M = mem.shape[1]
    memn = rms_norm(mem, mem_norm_w)
    for layer in range(DEPTH):
        h = rms_norm(x, attn_norm_w[layer])
        u = h @ w_in[layer]
        j = layer // N_MIXERS
        if layer % N_MIXERS == 0:
            nqk = RET_HEADS * RET_QK_DIM
            q = u[..., :nqk].reshape(B, S, RET_HEADS, RET_QK_DIM)
            k = u[..., nqk:2 * nqk].reshape(B, S, RET_HEADS, RET_QK_DIM)
            v = u[..., 2 * nqk:2 * nqk + MAIN_WIDTH].reshape(B, S, RET_HEADS, RET_V_DIM)
            g = u[..., 2 * nqk + MAIN_WIDTH:3 * MAIN_WIDTH]
            o = rms_norm(retention(q, k, v), ret_gn_w[j])
            o_main = jax.nn.silu(g) * o.reshape(B, S, MAIN_WIDTH)
        else:
            q = u[..., :MAIN_WIDTH].reshape(B, S, DIFF_HEADS, 2, DIFF_HEAD_DIM)
            k = u[..., MAIN_WIDTH:2 * MAIN_WIDTH].reshape(B, S, DIFF_HEADS, 2, DIFF_HEAD_DIM)
            v = u[..., 2 * MAIN_WIDTH:3 * MAIN_WIDTH].reshape(B, S, DIFF_HEADS, DIFF_V_DIM)
            q = rms_norm(q, diff_q_norm_w[j])
            k = rms_norm(k, diff_k_norm_w[j])
            lambda_init = 0.8 - 0.6 * math.exp(-0.3 * layer)
            lq1 = diff_lambda_q1[j].astype(jnp.float32)
            lk1 = diff_lambda_k1[j].astype(jnp.float32)
            lq2 = diff_lambda_q2[j].astype(jnp.float32)
            lk2 = diff_lambda_k2[j].astype(jnp.float32)
            lam = jnp.exp(jnp.sum(lq1 * lk1)) - jnp.exp(jnp.sum(lq2 * lk2)) + lambda_init
            o = rms_norm(diff_attention(q, k, v, lam), diff_subln_w[j]) * (1.0 - lambda_init)
            o_main = o.reshape(B, S, MAIN_WIDTH)
        qc = rms_norm(u[..., 3 * MAIN_WIDTH:].reshape(B, S, MEM_HEADS, MEM_HEAD_DIM), mem_q_norm_w[layer])
        mkv = memn @ w_mem_kv[layer]
        mk = rms_norm(mkv[..., :MEM_WIDTH].reshape(B, M, MEM_HEADS, MEM_HEAD_DIM), mem_k_norm_w[layer])
        mv = mkv[..., MEM_WIDTH:].reshape(B, M, MEM_HEADS, MEM_HEAD_DIM)
        o_mem = memory_attention(qc, mk, mv).reshape(B, S, MEM_WIDTH)
        x = x + jnp.concatenate([o_main, o_mem], axis=-1) @ w_out[layer]
        h = rms_norm(x, ffn_norm_w[layer])
        gu = h @ w_gate_up[layer]
        x = x + (jax.nn.silu(gu[..., :FFN_DIM]) * gu[..., FFN_DIM:]) @ w_down[layer]
    return x
```

```python
import math
import os
import contextlib
import numpy as np
import ml_dtypes
import concourse.bass as bass
import concourse.mybir as mybir
from concourse.bass_utils import run_bass_kernel_spmd

F32 = mybir.dt.float32
BF16 = mybir.dt.bfloat16
AF = mybir.ActivationFunctionType
ALU = mybir.AluOpType

D = 1024
S = 4096
TOK = 2048
NG = 4
DEPTH = 4
FFN = 2816
NFC = 22
EPS = 1e-6
NEG = -30000.0


class TB:
    __slots__ = ("name", "w", "r", "x")

    def __init__(self, name="", x=False):
        self.name = name
        self.w = None
        self.r = {}
        self.x = x


CE = ("pe", "act", "dve", "pool")
STREAMS = ("pe", "act", "dve", "pool", "sp")
NDSEM = 8
STRICT_SAME = os.environ.get('KSTRICT', '1') == '1'


class Prog:
    def __init__(self):
        self.ops = {s: [] for s in STREAMS}
        self.seen = {s: {} for s in STREAMS}
        self.marked = {e: set() for e in CE}
        self.dma_cnt = {}
        self.dma_rr = {"sp": 0, "pool": 0}
        self.ncc = 0
        self.last = {}

    def _need(self, stream, tok, waits):
        if tok is None:
            return
        key, val = tok
        if key == stream and (stream == "pe" or not STRICT_SAME):
            return
        if self.seen[stream].get(key, 0) >= val:
            return
        if waits.get(key, 0) < val:
            waits[key] = val

    def _commit(self, stream, waits):
        for key, val in waits.items():
            self.seen[stream][key] = val
            if key in CE:
                self.marked[key].add(val)

    def op(self, stream, fn, reads=(), writes=(), dma=False, cc=False):
        waits = {}
        for b in reads:
            self._need(stream, b.w, waits)
            if b.x:
                for t in b.r.values():
                    self._need(stream, t, waits)
        for b in writes:
            self._need(stream, b.w, waits)
            for t in b.r.values():
                self._need(stream, t, waits)
        if cc:
            key = ("cc", self.ncc)
            self.ncc += 1
            self.dma_cnt[key] = 1
            tok = (key, 1)
            kind = 2
        elif dma:
            k = self.dma_rr[stream] % NDSEM
            self.dma_rr[stream] += 1
            key = ("dma", stream, k)
            n = self.dma_cnt.get(key, 0)
            if n > 0:
                self._need(stream, (key, 16 * n), waits)
            self.dma_cnt[key] = n + 1
            tok = (key, 16 * (n + 1))
            kind = 1
        else:
            tok = (stream, len(self.ops[stream]) + 1)
            kind = 0
        self._commit(stream, waits)
        self.ops[stream].append((waits, fn, tok, kind))
        self.last[tok[0]] = tok
        for b in reads:
            b.r[tok[0]] = tok
        for b in writes:
            b.w = tok
            b.r = {}
        return tok

    def wait_all(self, stream, bufs):
        waits = {}
        for b in bufs:
            self._need(stream, b.w, waits)
            for t in b.r.values():
                self._need(stream, t, waits)
        self._commit(stream, waits)
        self.ops[stream].append((waits, None, None, 0))

    def barrier(self):
        toks = [t for t in self.last.values() if not (isinstance(t[0], tuple) and t[0][0] == "cc")]
        for s in STREAMS:
            waits = {}
            for t in toks:
                self._need(s, t, waits)
            self._commit(s, waits)
            self.ops[s].append((waits, None, None, 0))

    def emit(self, nc, block, stack):
        sems = {}
        for e in CE:
            sems[e] = stack.enter_context(nc.semaphore("s_" + e))
        for key in self.dma_cnt:
            sems[key] = stack.enter_context(nc.semaphore("d_" + "_".join(str(k) for k in key)))
        rank = {}
        for e in CE:
            m = sorted(self.marked[e])
            rank[e] = {v: i + 1 for i, v in enumerate(m)}
        marked = self.marked

        def runner(stream):
            ops = self.ops[stream]

            def body(eng):
                for i, (waits, fn, tok, kind) in enumerate(ops, 1):
                    for key, val in waits.items():
                        v = rank[key][val] if key in CE else val
                        eng.wait_ge(sems[key], v)
                    if fn is None:
                        continue
                    ins = fn(eng)
                    if kind == 1:
                        ins.then_inc(sems[tok[0]], 16)
                    elif kind == 2:
                        ins.then_inc(sems[tok[0]], 1)
                    elif stream in CE and i in marked[stream]:
                        ins.then_inc(sems[stream], 1)
            return body

        block.tensor(runner("pe"))
        block.scalar(runner("act"))
        block.vector(runner("dve"))
        block.gpsimd(runner("pool"))
        block.sync(runner("sp"))


RET_H = 6
GAM = [1.0 - 2.0 ** (-5.0 - h) for h in range(RET_H)]
SLOPES = [2.0 ** (-8.0 * (h + 1) / 6.0) for h in range(6)]


def lambda_init(layer):
    return 0.8 - 0.6 * math.exp(-0.3 * layer)


class CMap:
    def __init__(self):
        self.n = 0
        self.m = {}

    def add(self, name, w):
        self.m[name] = (self.n, w)
        self.n += w

    def sl(self, name):
        a, w = self.m[name]
        return slice(a, a + w)


def build_cmap():
    c = CMap()
    for l in range(DEPTH):
        c.add("anw%d" % l, 8)
        c.add("fnw%d" % l, 8)
        c.add("mqw%d" % l, 1)
        c.add("mkw%d" % l, 1)
    c.add("mnw", 8)
    for j in range(2):
        c.add("dqw%d" % j, 1)
        c.add("dkw%d" % j, 1)
        for h in range(6):
            c.add("gnw%d_%d" % (j, h), 1)
    for p in range(3):
        c.add("g128_%d" % p, 1)
    c.add("flag", 1)
    c.add("bown", 6 * 16)
    c.add("bprev", 6 * 32)
    c.add("identf", 128)
    c.add("Dm", 6 * 128)
    c.add("dectab", 384)
    c.add("qdtab", 3 * 512)
    c.add("subw", 2 * 128)
    c.add("lamv", 8 * 64)
    return c


CM = build_cmap()


def host_consts(inputs, half):
    c = np.zeros((128, CM.n), np.float32)
    p = np.arange(128)

    def colvec(v):
        return np.asarray(v, np.float32).reshape(8, 128).T

    for l in range(DEPTH):
        c[:, CM.sl("anw%d" % l)] = colvec(inputs["attn_norm_w"][l])
        c[:, CM.sl("fnw%d" % l)] = colvec(inputs["ffn_norm_w"][l])
        c[:, CM.sl("mqw%d" % l)] = inputs["mem_q_norm_w"][l][p % 64][:, None]
        c[:, CM.sl("mkw%d" % l)] = inputs["mem_k_norm_w"][l][p % 64][:, None]
    c[:, CM.sl("mnw")] = colvec(inputs["mem_norm_w"])
    for j in range(2):
        c[:, CM.sl("dqw%d" % j)] = inputs["diff_q_norm_w"][j][p % 64][:, None]
        c[:, CM.sl("dkw%d" % j)] = inputs["diff_k_norm_w"][j][p % 64][:, None]
        for h in range(6):
            c[:, CM.sl("gnw%d_%d" % (j, h))] = inputs["ret_gn_w"][j][h][:, None]
    g = np.array(GAM, np.float64)
    for pr in range(3):
        c[:, CM.sl("g128_%d" % pr)] = (g[2 * pr + p // 64] ** 128).astype(np.float32)[:, None]
    c[:, CM.sl("flag")] = float(half)
    bo = np.zeros((6, 16), np.float32)
    bp = np.zeros((6, 32), np.float32)
    for h in range(6):
        for m in range(-3, 13):
            bo[h, m + 3] = -SLOPES[h] * 128.0 * m
        for m in range(32):
            bp[h, m] = (-SLOPES[h] * 128.0 * m) if half == 1 else NEG
    c[:, CM.sl("bown")] = bo.reshape(1, -1)
    c[:, CM.sl("bprev")] = bp.reshape(1, -1)
    c[:, CM.sl("identf")] = np.eye(128, dtype=np.float32)
    j_ = np.arange(128)[:, None]
    i_ = np.arange(128)[None, :]
    Dm = np.zeros((128, 6, 128), np.float64)
    for h in range(6):
        Dm[:, h, :] = np.where((j_ // 64) <= (i_ // 64), g[h] ** np.abs(i_ - j_), 0.0) / 8.0
    c[:, CM.sl("Dm")] = Dm.reshape(128, -1).astype(np.float32)
    dec = np.zeros((128, 6, 64), np.float64)
    for h in range(6):
        dec[:, h, :] = (g[h] ** (127 - np.arange(128)))[:, None] / 8.0
    c[:, CM.sl("dectab")] = dec.reshape(128, -1).astype(np.float32)
    qd = np.zeros((128, 3, 512), np.float64)
    for pr in range(3):
        gg = g[2 * pr + p // 64]
        qd[:, pr, :] = gg[:, None] ** ((np.arange(512) % 128) + 1)[None, :]
    c[:, CM.sl("qdtab")] = qd.reshape(128, -1).astype(np.float32)
    sw = np.zeros((128, 2, 128), np.float32)
    for j in range(2):
        sw[:, j, :] = inputs["diff_subln_w"][j][None, :]
    c[:, CM.sl("subw")] = sw.reshape(128, -1)
    lv = np.zeros((128, 8, 64), np.float32)
    for j in range(2):
        for k, nm in enumerate(["diff_lambda_q1", "diff_lambda_k1", "diff_lambda_q2", "diff_lambda_k2"]):
            lv[:, j * 4 + k, :] = inputs[nm][j][None, :]
    c[:, CM.sl("lamv")] = lv.reshape(128, -1)
    return c


def host_tables():
    tb = np.zeros((128, 3, 128), np.float32)
    tb[:, 0, :] = 1.0
    tb[:, 1, :] = ((np.arange(128)[:, None] // 64) == (np.arange(128)[None, :] // 64)).astype(np.float32)
    tb[:, 2, :] = np.eye(128, dtype=np.float32)
    tb = tb.astype(ml_dtypes.bfloat16)
    j_ = np.arange(128)[:, None].astype(np.float64)
    i5 = np.arange(512)[None, :].astype(np.float64)
    i1 = np.arange(128)[None, :].astype(np.float64)
    T = np.zeros((6, 128, 512), np.float32)
    Tg = np.zeros((6, 128, 128), np.float32)
    for h in range(6):
        T[h] = (-SLOPES[h] * (i5 - j_)).astype(np.float32)
        allowed = (j_ // 64) <= (i1 // 64)
        Tg[h] = np.where(allowed, -SLOPES[h] * np.abs(i1 - j_), NEG).astype(np.float32)
    return tb, T, Tg


class Alloc:
    def __init__(self, nc, base, limit, tag):
        self.nc = nc
        self.off = base
        self.limit = limit
        self.tag = tag
        self.i = 0

    def t(self, name, shape, dt):
        esz = 2 if dt == BF16 else 4
        n = 1
        for s in shape[1:]:
            n *= s
        nb = (n * esz + 63) // 64 * 64
        assert self.off + nb <= self.limit, (name, self.off, nb, self.limit)
        h = self.nc.alloc_sbuf_tensor_at("%s_%s_%d" % (self.tag, name, self.i), list(shape), dt, offset=self.off)
        self.i += 1
        self.off += nb
        return h


SB_BASE = 16384 + 512
SB_LIMIT = 224 * 1024 - 512


def build_program(NL):
    nc = bass.Bass("TRN2", target_bir_lowering=False)
    P = Prog()
    dram = {}

    def din(name, shape, dt=F32):
        dram[name] = nc.dram_tensor(name, list(shape), dt, kind="ExternalInput")
        return dram[name].ap()

    x_d = din("x", [TOK, D])
    mem_d = din("mem", [256, D])
    w_in_d = din("w_in", [NL, D, 2560])
    w_out_d = din("w_out", [NL, D, D])
    w_mkv_d = din("w_mem_kv", [NL, D, 512])
    w_gu_d = din("w_gate_up", [NL, D, 2 * FFN])
    w_dn_d = din("w_down", [NL, FFN, D])
    cst_d = din("cst", [128, CM.n])
    tb_d = din("tb", [128, 3, 128], BF16)
    T_d = din("Ttab", [6, 128, 512])
    Tg_d = din("Tgtab", [6, 128, 128])
    out_d = nc.dram_tensor("out", [TOK, D], F32, kind="ExternalOutput").ap()

    qT_s = nc.dram_tensor("qT_s", [768, TOK], BF16, kind="ExternalOutput").ap()
    qdT_s = nc.dram_tensor("qdT_s", [384, TOK], BF16, kind="ExternalOutput").ap()
    kT_s2 = [nc.dram_tensor("kT_s%d" % i, [384, TOK], BF16).ap() for i in range(2)]
    kT_g2 = [nc.dram_tensor("kT_g%d" % i, [768, TOK], BF16).ap() for i in range(2)]
    v_s2 = [nc.dram_tensor("v_s%d" % i, [1024, 768], BF16).ap() for i in range(2)]
    v_g2 = [nc.dram_tensor("v_g%d" % i, [2048, 768], BF16).ap() for i in range(2)]

    def kT_rows(r0):
        return kT_s2[r0 // 384][r0 % 384:r0 % 384 + 128, :]

    def v_rows(tok0):
        return v_s2[tok0 // 1024][tok0 % 1024:tok0 % 1024 + 128, :]
    gT_s = nc.dram_tensor("gT_s", [768, TOK], BF16, kind="ExternalOutput").ap()
    kt_s = nc.dram_tensor("kt_s", [TOK, 384], BF16, kind="ExternalOutput").ap()
    st_l = nc.dram_tensor("st_l", [384, 256], F32).ap()
    st_g = nc.dram_tensor("st_g", [768, 256], F32).ap()
    NCORES = int(os.environ.get('KNC', '8'))
    RG = [[2 * i, 2 * i + 1] for i in range(NCORES // 2)]

    pa = Alloc(nc, SB_BASE, SB_LIMIT, "p")
    xT = pa.t("xT", [128, 8, TOK], F32)
    cst = pa.t("cst", [128, CM.n], F32)
    tbs = pa.t("tb", [128, 3, 128], BF16)
    memnT = pa.t("memnT", [128, 8, 256], BF16)
    ARENA = pa.off
    ones_b = tbs[:, 0, :]
    bd64_b = tbs[:, 1, :]
    id_b = tbs[:, 2, :]
    id_f = cst[:, CM.sl("identf")]
    B_xT = [[TB("xT%d_%d" % (kc, g)) for g in range(NG)] for kc in range(8)]
    B_cst = TB("cst")
    B_tb = TB("tb")
    B_memnT = TB("memnT")

    def C(name):
        return cst[:, CM.sl(name)]

    es = contextlib.ExitStack()
    psn = ["pA0", "pA1", "pA2", "pB", "pS0", "pS1", "pO0", "pO1"]
    ps = {n: es.enter_context(nc.psum_tensor(n, [128, 512], F32)) for n in psn}
    B_ps = {n: TB(n, x=True) for n in psn}
    rrA = [0]

    A_BANKS = [["pA0", "pA1", "pA2"]]

    def nextA():
        lst = A_BANKS[0]
        n = lst[rrA[0] % len(lst)]
        rrA[0] += 1
        return ps[n], B_ps[n]

    rot = {}

    def rr(key, n):
        v = rot.get(key, 0)
        rot[key] = v + 1
        return v % n

    def dma(stream, out, in_, reads, writes):
        return P.op(stream, lambda e: e.dma_start(out=out, in_=in_), reads=reads, writes=writes, dma=True)

    def mm(out, lhsT, rhs, start, stop, reads, writes, skip=False):
        if skip:
            return P.op("pe", lambda e: e.matmul(out, lhsT=lhsT, rhs=rhs, start=start, stop=stop, skip_group_check=True), reads=reads, writes=writes)
        return P.op("pe", lambda e: e.matmul(out, lhsT=lhsT, rhs=rhs, start=start, stop=stop), reads=reads, writes=writes)

    def act(out, in_, func, reads, writes, bias=0.0, scale=1.0, accum_out=None):
        if accum_out is None:
            return P.op("act", lambda e: e.activation(out=out, in_=in_, func=func, bias=bias, scale=scale), reads=reads, writes=writes)
        return P.op("act", lambda e: e.activation(out=out, in_=in_, func=func, bias=bias, scale=scale, accum_out=accum_out), reads=reads, writes=writes)

    def dve(fn, reads, writes):
        return P.op("dve", fn, reads=reads, writes=writes)

    def rstd_from(psum_ap, out_ap, n, reads, writes, eps=EPS, extra=1.0):
        act(out_ap, psum_ap, AF.Sqrt, reads, writes, bias=eps / (extra * extra), scale=1.0 / (n * extra * extra))
        dve(lambda e: e.reciprocal(out=out_ap, in_=out_ap), writes, writes)

    wbf_in = [nc.dram_tensor("wbf_in%d" % l, [D, 2560], BF16).ap() for l in range(NL)]
    wbf_mkv = [nc.dram_tensor("wbf_mkv%d" % l, [D, 512], BF16).ap() for l in range(NL)]
    wbf_out = [nc.dram_tensor("wbf_out%d" % l, [D, D], BF16).ap() for l in range(NL)]
    B_wbf = [{"in": [TB("wbi%d_%d" % (l, k)) for k in range(8)], "mkv": [TB("wbm%d_%d" % (l, k)) for k in range(2)], "out": [TB("wbo%d_%d" % (l, k)) for k in range(4)]} for l in range(NL)]
    for l in range(NL):
        for kc in range(8):
            rsl = slice(kc * 128, (kc + 1) * 128)
            P.op("pool", lambda e, l=l, rsl=rsl: e.dma_start(out=wbf_in[l][rsl, :], in_=w_in_d[l, rsl, :]), reads=[], writes=[B_wbf[l]["in"][rsl.start // 128]], dma=True)
        for kc in range(0, 8, 4):
            rsl = slice(kc * 128, (kc + 4) * 128)
            P.op("pool", lambda e, l=l, rsl=rsl: e.dma_start(out=wbf_mkv[l][rsl, :], in_=w_mkv_d[l, rsl, :]), reads=[], writes=[B_wbf[l]["mkv"][rsl.start // 512]], dma=True)
        for kc in range(0, 8, 2):
            rsl = slice(kc * 128, (kc + 2) * 128)
            P.op("pool", lambda e, l=l, rsl=rsl: e.dma_start(out=wbf_out[l][rsl, :], in_=w_out_d[l, rsl, :]), reads=[], writes=[B_wbf[l]["out"][rsl.start // 256]], dma=True)

    dma("sp", cst[:], cst_d, [], [B_cst])
    dma("sp", tbs[:], tb_d, [], [B_tb])

    a0 = Alloc(nc, ARENA, SB_LIMIT, "ld")
    xin = [a0.t("xin%d" % i, [128, D], F32) for i in range(2)]
    B_xin = [TB("xin0"), TB("xin1")]
    memin = a0.t("memin", [128, 2, D], F32)
    B_memin = TB("memin")
    memT = a0.t("memT", [128, 8, 256], F32)
    B_memT = TB("memT")
    sqm = a0.t("sqm", [128, 256], BF16)
    B_sqm = TB("sqm")
    rsm = a0.t("rsm", [128, 256], F32)
    B_rsm = TB("rsm")
    for t in range(16):
        xi, bx = xin[t % 2], B_xin[t % 2]
        dma("sp", xi[:], x_d[t * 128:(t + 1) * 128, :], [], [bx])
        for hf in range(2):
            pt, bp = nextA()
            for k4 in range(4):
                kc = hf * 4 + k4
                P.op("pe", lambda e, pt=pt, xi=xi, k4=k4, kc=kc: e.transpose(pt[:, k4 * 128:(k4 + 1) * 128], xi[:, kc * 128:(kc + 1) * 128], id_f),
                     reads=[bx, B_cst], writes=[bp])
            g = t // 4
            wr = [B_xT[hf * 4 + k4][g] for k4 in range(4)]
            outap = xT[:, hf * 4:hf * 4 + 4, t * 128:(t + 1) * 128]
            inap = pt[:, :].rearrange("p (k n) -> p k n", k=4)
            if hf == 0:
                dve(lambda e, o=outap, i=inap: e.tensor_copy(out=o, in_=i), [bp], wr)
            else:
                P.op("act", lambda e, o=outap, i=inap: e.copy(out=o, in_=i), reads=[bp], writes=wr)
    dma("sp", memin[:], mem_d.rearrange("(t p) d -> p t d", p=128), [], [B_memin])
    for mt in range(2):
        for hf in range(2):
            pt, bp = nextA()
            for k4 in range(4):
                kc = hf * 4 + k4
                P.op("pe", lambda e, pt=pt, k4=k4, kc=kc, mt=mt: e.transpose(pt[:, k4 * 128:(k4 + 1) * 128], memin[:, mt, kc * 128:(kc + 1) * 128], id_f),
                     reads=[B_memin, B_cst], writes=[bp])
            dve(lambda e, pt=pt, hf=hf, mt=mt: e.tensor_copy(out=memT[:, hf * 4:hf * 4 + 4, mt * 128:(mt + 1) * 128], in_=pt[:, :].rearrange("p (k n) -> p k n", k=4)),
                [bp], [B_memT])
    for kc in range(8):
        act(sqm[:], memT[:, kc, :], AF.Square, [B_memT], [B_sqm])
        mm(ps["pB"][:, 0:256], ones_b, sqm[:], kc == 0, kc == 7, [B_sqm, B_tb], [B_ps["pB"]])
    rstd_from(ps["pB"][:, 0:256], rsm[:], D, [B_ps["pB"]], [B_rsm])
    for kc in range(8):
        dve(lambda e, kc=kc: e.scalar_tensor_tensor(out=memnT[:, kc, :], in0=memT[:, kc, :], scalar=C("mnw")[:, kc:kc + 1], in1=rsm[:], op0=ALU.mult, op1=ALU.mult),
            [B_memT, B_rsm, B_cst], [B_memnT])
    P.barrier()

    def norm_group(g, wname, hT, B_h, sq, B_sq, rs, B_rs):
        cols = slice(g * 512, (g + 1) * 512)
        for kc in range(8):
            s, bs = sq[kc % 2], B_sq[kc % 2]
            act(s[:], xT[:, kc, cols], AF.Square, [B_xT[kc][g]], [bs])
            mm(ps["pB"][:, :], ones_b, s[:], kc == 0, kc == 7, [bs, B_tb], [B_ps["pB"]])
        rstd_from(ps["pB"][:, :], rs[:], D, [B_ps["pB"]], [B_rs])
        for kc in range(8):
            dve(lambda e, kc=kc: e.scalar_tensor_tensor(out=hT[:, kc, :], in0=xT[:, kc, cols], scalar=C(wname)[:, kc:kc + 1], in1=rs[:], op0=ALU.mult, op1=ALU.mult),
                [B_xT[kc][g], B_rs, B_cst], [B_h])

    def fm_proj(w_ap_fn, M, hT, B_h, B_w, ncols=512, hcols=None):
        pt, bp = nextA()
        for kc in range(8):
            rhs = hT[:, kc, :] if hcols is None else hT[:, kc, hcols]
            mm(pt[0:M, 0:ncols], w_ap_fn(kc), rhs, kc == 0, kc == 7, [B_h, B_w], [bp])
        return pt, bp

    def qknorm_evac(pt, bp, N, wcol, out_ap, B_out, sql, B_sql, rsl, B_rsl):
        i_ = rr("qkn", len(sql))
        sq, B_sq, rs, B_rs = sql[i_], B_sql[i_], rsl[i_], B_rsl[i_]
        sb_ = ["pS0", "pS1"][rr("qknb", 2)]
        act(sq[:, 0:N], pt[:, 0:N], AF.Square, [bp], [B_sq])
        mm(ps[sb_][:, 0:N], bd64_b, sq[:, 0:N], True, True, [B_sq, B_tb], [B_ps[sb_]])
        rstd_from(ps[sb_][:, 0:N], rs[:, 0:N], 64, [B_ps[sb_]], [B_rs])
        dve(lambda e: e.scalar_tensor_tensor(out=out_ap, in0=pt[:, 0:N], scalar=wcol, in1=rs[:, 0:N], op0=ALU.mult, op1=ALU.mult),
            [bp, B_rs, B_cst], [B_out])

    def stt(out, in0, scalar, in1, op0, op1, reads, writes):
        return P.op("dve", lambda e: e.scalar_tensor_tensor(out=out, in0=in0, scalar=scalar, in1=in1, op0=op0, op1=op1), reads=reads, writes=writes)

    def tt(out, in0, in1, op, reads, writes):
        return P.op("dve", lambda e: e.tensor_tensor(out=out, in0=in0, in1=in1, op=op), reads=reads, writes=writes)

    def tsc(out, in0, scalar1, op0, reads, writes):
        return P.op("dve", lambda e: e.tensor_scalar(out=out, in0=in0, scalar1=scalar1, scalar2=None, op0=op0), reads=reads, writes=writes)

    def vcopy(out, in_, reads, writes):
        return P.op("dve", lambda e: e.tensor_copy(out=out, in_=in_), reads=reads, writes=writes)

    def vrecip(out, in_, reads, writes):
        return P.op("dve", lambda e: e.reciprocal(out=out, in_=in_), reads=reads, writes=writes)

    def acopy(out, in_, reads, writes):
        return P.op("act", lambda e: e.copy(out=out, in_=in_), reads=reads, writes=writes)

    def tpose(out, in_, ident, reads, writes):
        return P.op("pe", lambda e: e.transpose(out, in_, ident), reads=reads, writes=writes)

    def wslice(w, c0, M=128):
        return lambda kc: w[:, kc, c0:c0 + M]

    KSTOP = int(os.environ.get('KSTOP', '99'))
    for layer in range(NL):
        if KSTOP <= 0:
            break
        is_ret = (layer % 2 == 0)
        jj = layer // 2
        a1 = Alloc(nc, ARENA, SB_LIMIT, "ip%d" % layer)
        qcnT = a1.t("qcnT", [128, 2, TOK], BF16)
        B_qcn = [TB("qcn%d" % g) for g in range(NG)]
        mkT = a1.t("mkT", [128, 2, 256], BF16)
        B_mkT = TB("mkT")
        mv = a1.t("mv", [128, 2, 4, 65], BF16)
        B_mv = TB("mv")
        AR2 = a1.off
        w_in = a1.t("w_in", [128, 8, 2560], BF16)
        B_win = TB("w_in")
        w_mkv = a1.t("w_mkv", [128, 8, 512], BF16)
        B_wmkv = TB("w_mkv")
        hT = [a1.t("hT%d" % i, [128, 8, 512], BF16) for i in range(2)]
        B_hT = [TB("hT0"), TB("hT1")]
        sq = [a1.t("sq%d" % i, [128, 512], BF16) for i in range(2)]
        B_sq = [TB("sq0"), TB("sq1")]
        rs = a1.t("rs", [128, 512], F32)
        B_rs = TB("rs")
        sq2 = [a1.t("sq2_%d" % i, [128, 512], BF16) for i in range(3)]
        B_sq2 = [TB("sq2_%d" % i) for i in range(3)]
        rs2 = [a1.t("rs2_%d" % i, [128, 512], F32) for i in range(3)]
        B_rs2 = [TB("rs2_%d" % i) for i in range(3)]
        A_BANKS[0] = ["pA0", "pA1", "pA2", "pO0", "pO1"]
        NST = 4
        stg = [a1.t("stg%d" % i, [128, 512], BF16) for i in range(NST)]
        B_stg = [TB("stg%d" % i) for i in range(NST)]
        stk = [a1.t("stk%d" % i, [128, 1152], BF16) for i in range(2)]
        B_stk = [TB("stk%d" % i) for i in range(2)]

        for kc in range(8):
            for q2 in range(2):
                dma("sp", w_in[:, kc, q2 * 1280:(q2 + 1) * 1280], wbf_in[layer][kc * 128:(kc + 1) * 128, q2 * 1280:(q2 + 1) * 1280], [B_wbf[layer]["in"][kc]], [B_win])
            dma("sp", w_mkv[:, kc, :], wbf_mkv[layer][kc * 128:(kc + 1) * 128, :], [B_wbf[layer]["mkv"][kc // 4]], [B_wmkv])

        P.op("pool", lambda e, mv=mv: e.memset(mv[:], 1.0), reads=[], writes=[B_mv])
        for c2 in range(2):
            pt, bp = fm_proj(wslice(w_mkv, c2 * 128), 128, memnT, B_memnT, B_wmkv, ncols=256)
            qknorm_evac(pt, bp, 256, C("mkw%d" % layer), mkT[:, c2, :], B_mkT, sq2, B_sq2, rs2, B_rs2)
        for mt in range(2):
            pt, bp = nextA()
            for kc in range(8):
                mm(pt[:, 0:256], memnT[:, kc, mt * 128:(mt + 1) * 128], w_mkv[:, kc, 256:512], kc == 0, kc == 7, [B_memnT, B_wmkv], [bp])
            acopy(mv[:, mt, :, 0:64], pt[:, 0:256].rearrange("p (h d) -> p h d", h=4), [bp], [B_mv])

        B_scr = {n: TB(n) for n in ["qT", "qdT", "kT", "gT", "kt", "v"]}
        scr_lists = {n: [] for n in B_scr}

        def scr_w(name):
            t_ = TB(name)
            scr_lists[name].append(t_)
            return t_

        def stage_out(pt_ap, bp, dst_ap, kind, scr, mul_ap=None):
            if os.environ.get('KNOSTAGE') == '1':
                return
            if os.environ.get('KONLY') and os.environ.get('KONLY') != kind:
                return
            i = rr("stg", NST)
            st, bst = stg[i], B_stg[i]
            if kind == "copy":
                acopy(st[:], pt_ap, [bp], [bst])
            elif kind == "silu":
                act(st[:], pt_ap, AF.Silu, [bp], [bst])
            elif kind == "mul":
                tt(st[:], pt_ap, mul_ap, ALU.mult, [bp, B_cst], [bst])
            if os.environ.get('KNODMA') != '1':
                dma("sp", dst_ap, st[:], [bst], [scr_w(scr)])

        KSUB = int(os.environ.get('KSUB', '99'))
        for g in range(NG):
            if KSUB <= 1:
                break
            cols = slice(g * 512, (g + 1) * 512)
            h_, bh = hT[g % 2], B_hT[g % 2]
            norm_group(g, "anw%d" % layer, h_, bh, sq, B_sq, rs, B_rs)
            if KSUB <= 2:
                continue
            if is_ret:
                for pr in range(3):
                    pt, bp = fm_proj(wslice(w_in, pr * 128), 128, h_, bh, B_win)
                    stage_out(pt[:, :], bp, qT_s[pr * 128:(pr + 1) * 128, cols], "copy", "qT")
                    stage_out(pt[:, :], bp, qdT_s[pr * 128:(pr + 1) * 128, cols], "mul", "qdT", mul_ap=C("qdtab")[:, pr * 512:(pr + 1) * 512])
                for pr in range(3):
                    pt, bp = fm_proj(wslice(w_in, 384 + pr * 128), 128, h_, bh, B_win)
                    stage_out(pt[:, :], bp, kT_rows(pr * 128)[:, cols], "copy", "kT")
                for hh in range(6):
                    pt, bp = fm_proj(wslice(w_in, 1536 + hh * 128), 128, h_, bh, B_win)
                    stage_out(pt[:, :], bp, gT_s[hh * 128:(hh + 1) * 128, cols], "silu", "gT")
            else:
                for hh in range(6):
                    pt, bp = fm_proj(wslice(w_in, hh * 128), 128, h_, bh, B_win)
                    i = rr("stg", NST)
                    qknorm_evac(pt, bp, 512, C("dqw%d" % jj), stg[i][:], B_stg[i], sq2, B_sq2, rs2, B_rs2)
                    dma("sp", qT_s[hh * 128:(hh + 1) * 128, cols], stg[i][:], [B_stg[i]], [scr_w("qT")])
                for hh in range(6):
                    pt, bp = fm_proj(wslice(w_in, 768 + hh * 128), 128, h_, bh, B_win)
                    i = rr("stg", NST)
                    qknorm_evac(pt, bp, 512, C("dkw%d" % jj), stg[i][:], B_stg[i], sq2, B_sq2, rs2, B_rs2)
                    dma("sp", kT_rows(hh * 128)[:, cols], stg[i][:], [B_stg[i]], [scr_w("kT")])
            for c2 in range(2):
                pt, bp = fm_proj(wslice(w_in, 2304 + c2 * 128), 128, h_, bh, B_win)
                qknorm_evac(pt, bp, 512, C("mqw%d" % layer), qcnT[:, c2, cols], B_qcn[g], sq2, B_sq2, rs2, B_rs2)
            for t4 in range(4):
                if KSUB <= 3:
                    break
                tok0 = g * 512 + t4 * 128
                i = rr("stk", 2)
                sk, bsk = stk[i], B_stk[i]
                nparts = 3 if is_ret else 2
                cbase = 384 if is_ret else 1536
                for part in range(nparts):
                    pt, bp = nextA()
                    c0 = cbase + part * 384
                    for kc in range(8):
                        mm(pt[:, 0:384], h_[:, kc, t4 * 128:(t4 + 1) * 128], w_in[:, kc, c0:c0 + 384], kc == 0, kc == 7, [bh, B_win], [bp])
                    if is_ret and part == 0:
                        tt(sk[:, 0:384], pt[:, 0:384], C("dectab"), ALU.mult, [bp, B_cst], [bsk])
                    elif part % 2 == 0:
                        vcopy(sk[:, part * 384:(part + 1) * 384], pt[:, 0:384], [bp], [bsk])
                    else:
                        acopy(sk[:, part * 384:(part + 1) * 384], pt[:, 0:384], [bp], [bsk])
                if is_ret:
                    dma("sp", kt_s[tok0:tok0 + 128, :], sk[:, 0:384], [bsk], [scr_w("kt")])
                    dma("sp", v_rows(tok0), sk[:, 384:1152], [bsk], [scr_w("v")])
                else:
                    dma("sp", v_rows(tok0), sk[:, 0:768], [bsk], [scr_w("v")])
        B_kTg = TB("kT_g")
        B_vg = TB("v_g")
        if not is_ret:
            for i2 in range(2):
                P.op("pool", lambda e, i2=i2: e.collective_compute("AllGather", ALU.bypass, replica_groups=RG, ins=[kT_s2[i2]], outs=[kT_g2[i2]]),
                     reads=list(scr_lists["kT"]), writes=[B_kTg], cc=True)
            for i2 in range(2):
                P.op("pool", lambda e, i2=i2: e.collective_compute("AllGather", ALU.bypass, replica_groups=RG, ins=[v_s2[i2]], outs=[v_g2[i2]]),
                     reads=list(scr_lists["v"]), writes=[B_vg], cc=True)
        P.barrier()

        if KSTOP <= 1:
            break
        a2 = Alloc(nc, AR2, SB_LIMIT, "mx%d" % layer)
        A_BANKS[0] = ["pA0", "pA1", "pA2"]
        ocT = a2.t("ocT", [128, 8, TOK], BF16)
        B_oc = [[TB("oc%d_%d" % (c, g)) for g in range(NG)] for c in range(8)]
        w_out = a2.t("w_out", [128, 8, D], BF16)
        B_wout = TB("w_out")
        dma("sp", w_out[:], wbf_out[layer].rearrange("(kc p) n -> p kc n", p=128), list(B_wbf[layer]["out"]), [B_wout])
        pm = [a2.t("pm%d" % i, [128, 512], BF16) for i in range(2)]
        B_pm = [TB("pm0"), TB("pm1")]
        omall = a2.t("omall", [128, 4, 256], BF16)
        B_omall = TB("omall")
        rl = a2.t("rl", [128, 8], F32)
        B_rl = TB("rl")
        pT_b = ps["pB"][:, :].bitcast(BF16)
        for g in range(NG):
            cols = slice(g * 512, (g + 1) * 512)
            for hd in range(4):
                c2, hh = hd // 2, hd % 2
                pacc = ps["pO%d" % (hd % 2)]
                bacc = B_ps["pO%d" % (hd % 2)]
                for mt in range(2):
                    sn = "pS%d" % rr("pS", 2)
                    mm(ps[sn][:, :], mkT[hh * 64:(hh + 1) * 64, c2, mt * 128:(mt + 1) * 128], qcnT[hh * 64:(hh + 1) * 64, c2, cols], True, True,
                       [B_mkT, B_qcn[g]], [B_ps[sn]])
                    i = rr("pm", 2)
                    act(pm[i][:], ps[sn][:, :], AF.Exp, [B_ps[sn]], [B_pm[i]], scale=0.125)
                    for sub in range(4):
                        mm(pacc[:, sub * 65:(sub + 1) * 65], pm[i][:, sub * 128:(sub + 1) * 128], mv[:, mt, hd, :], mt == 0 and sub == 0, mt == 1, [B_pm[i], B_mv], [bacc], skip=True)
                acc3 = pacc[:, 0:260].rearrange("p (s e) -> p s e", s=4)
                vrecip(rl[:, 0:4], acc3[:, :, 64], [bacc], [B_rl])
                for sub in range(4):
                    tsc(omall[:, sub, hd * 64:(hd + 1) * 64], pacc[:, sub * 65:sub * 65 + 64], rl[:, sub:sub + 1], ALU.mult, [bacc, B_rl], [B_omall])
            for c2 in range(2):
                for sub in range(4):
                    tpose(pT_b[:, (c2 * 4 + sub) * 128:(c2 * 4 + sub + 1) * 128], omall[:, sub, c2 * 128:(c2 + 1) * 128], id_b, [B_omall, B_tb], [B_ps["pB"]])
            for c2 in range(2):
                vcopy(ocT[:, 6 + c2, cols], pT_b[:, c2 * 512:(c2 + 1) * 512], [B_ps["pB"]], [B_oc[6 + c2][g]])

        if KSTOP <= 2:
            break
        if is_ret:
            qTp = a2.t("qTp", [128, TOK], BF16)
            qdTp = a2.t("qdTp", [128, TOK], BF16)
            kTp = a2.t("kTp", [128, TOK], BF16)
            ktp = a2.t("ktp", [128, 16, 128], BF16)
            vtp = a2.t("vtp", [128, 16, 256], BF16)
            gTp = a2.t("gTp", [128, 2, TOK], BF16)
            Sst = a2.t("Sst", [128, 256], F32)
            S0 = a2.t("S0", [128, 3, 256], F32)
            Sball = a2.t("Sball", [128, 16, 256], BF16)
            Aa2 = [[a2.t("A%d_%d" % (hh, i), [128, 128], BF16) for i in range(2)] for hh in range(2)]
            sqo2 = [a2.t("sqo%d" % i, [128, 512], BF16) for i in range(2)]
            rso2 = [a2.t("rso%d" % i, [128, 512], F32) for i in range(2)]
            t12 = [a2.t("t1_%d" % i, [128, 512], F32) for i in range(2)]
            B_sqo = [TB("sqo0"), TB("sqo1")]
            B_rso = [TB("rso0"), TB("rso1")]
            B_t1 = [TB("t1_0"), TB("t1_1")]
            Bn = {n: TB(n) for n in ["qTp", "qdTp", "kTp", "ktp", "vtp", "gTp", "S", "S0", "Sb", "A0", "A1", "sqo", "rso", "t1", "st_l", "st_g"]}
            kt_v = kt_s.rearrange("(t j) c -> j t c", j=128)
            v_v2 = [v.rearrange("(t j) c -> j t c", j=128) for v in v_s2]

            def load_vtp(pr):
                for i2 in range(2):
                    dma("sp", vtp[:, i2 * 8:(i2 + 1) * 8, :], v_v2[i2][:, :, pr * 256:(pr + 1) * 256], [B_scr["v"]], [Bn["vtp"]])

            def state_step(pr, b, first):
                sb_ = "pS1" if b % 2 == 0 else "pA2"
                mm(ps[sb_][:, 0:256], ktp[:, b, :], vtp[:, b, :], True, True, [Bn["ktp"], Bn["vtp"]], [B_ps[sb_]])
                if first:
                    vcopy(Sst[:], ps[sb_][:, 0:256], [B_ps[sb_]], [Bn["S"]])
                else:
                    stt(Sst[:], Sst[:], C("g128_%d" % pr), ps[sb_][:, 0:256], ALU.mult, ALU.add, [Bn["S"], B_ps[sb_], B_cst], [Bn["S"]])

            for pr in range(3):
                dma("sp", ktp[:], kt_v[:, :, pr * 128:(pr + 1) * 128], [B_scr["kt"]], [Bn["ktp"]])
                load_vtp(pr)
                for b in range(16):
                    state_step(pr, b, b == 0)
                dma("sp", st_l[pr * 128:(pr + 1) * 128, :], Sst[:], [Bn["S"]], [Bn["st_l"]])
            P.op("pool", lambda e: e.collective_compute("AllGather", ALU.bypass, replica_groups=RG, ins=[st_l], outs=[st_g]),
                 reads=[Bn["st_l"]], writes=[Bn["st_g"]], cc=True)
            dma("sp", S0[:], st_g[0:384, :].rearrange("(p r) c -> r p c", r=128), [Bn["st_g"]], [Bn["S0"]])
            for pr in range(3):
                dma("sp", qTp[:], qT_s[pr * 128:(pr + 1) * 128, :], [B_scr["qT"]], [Bn["qTp"]])
                dma("sp", qdTp[:], qdT_s[pr * 128:(pr + 1) * 128, :], [B_scr["qdT"]], [Bn["qdTp"]])
                dma("sp", kTp[:], kT_rows(pr * 128), [B_scr["kT"]], [Bn["kTp"]])
                dma("sp", ktp[:], kt_v[:, :, pr * 128:(pr + 1) * 128], [B_scr["kt"]], [Bn["ktp"]])
                load_vtp(pr)
                dma("sp", gTp[:], gT_s[pr * 256:(pr + 1) * 256, :].rearrange("(h p) n -> p h n", p=128), [B_scr["gT"]], [Bn["gTp"]])
                tsc(Sst[:], S0[:, pr, :], C("flag"), ALU.mult, [Bn["S0"], B_cst], [Bn["S"]])
                B_Sb = [TB("Sb%d" % b) for b in range(16)]
                B_A = [[TB("A%d_%d" % (hh, i)) for i in range(2)] for hh in range(2)]
                acopy(Sball[:, 0, :], Sst[:], [Bn["S"]], [B_Sb[0]])

                def rfront(b, pr=pr):
                    cb = slice(b * 128, (b + 1) * 128)
                    for hh in range(2):
                        h = 2 * pr + hh
                        rws = slice(hh * 64, (hh + 1) * 64)
                        psc = ps["pS0"][:, hh * 128:(hh + 1) * 128]
                        mm(psc, kTp[rws, cb], qTp[rws, cb], True, True, [Bn["kTp"], Bn["qTp"]], [B_ps["pS0"]])
                        tt(Aa2[hh][b % 2][:], psc, C("Dm")[:, h * 128:(h + 1) * 128], ALU.mult, [B_ps["pS0"], B_cst], [B_A[hh][b % 2]])

                rfront(0)
                for b in range(16):
                    cb = slice(b * 128, (b + 1) * 128)
                    if b + 1 < 16:
                        rfront(b + 1)
                    g = b // 4
                    obank = ["pO0", "pO1"] if g % 2 == 0 else ["pA0", "pA1"]
                    for hh in range(2):
                        rws = slice(hh * 64, (hh + 1) * 64)
                        po = ps[obank[hh]][:, (b % 4) * 128:(b % 4 + 1) * 128]
                        mm(po, vtp[:, b, hh * 128:(hh + 1) * 128], Aa2[hh][b % 2][:], True, False, [Bn["vtp"], B_A[hh][b % 2]], [B_ps[obank[hh]]])
                        mm(po, Sball[rws, b, hh * 128:(hh + 1) * 128], qdTp[rws, cb], False, True, [B_Sb[b], Bn["qdTp"]], [B_ps[obank[hh]]])
                    state_step(pr, b, False)
                    if b + 1 < 16:
                        acopy(Sball[:, b + 1, :], Sst[:], [Bn["S"]], [B_Sb[b + 1]])
                    if b % 4 == 3:
                        cols = slice(g * 512, (g + 1) * 512)
                        for hh in range(2):
                            h = 2 * pr + hh
                            pO, bO = ps[obank[hh]], B_ps[obank[hh]]
                            sbk = "pB"
                            act(sqo2[hh][:], pO[:, :], AF.Square, [bO], [B_sqo[hh]])
                            mm(ps[sbk][:, :], ones_b, sqo2[hh][:], True, True, [B_sqo[hh], B_tb], [B_ps[sbk]])
                            rstd_from(ps[sbk][:, :], rso2[hh][:], 128, [B_ps[sbk]], [B_rso[hh]])
                            stt(t12[hh][:], pO[:, :], C("gnw%d_%d" % (jj, h)), rso2[hh][:], ALU.mult, ALU.mult, [bO, B_rso[hh], B_cst], [B_t1[hh]])
                            tt(ocT[:, h, cols], t12[hh][:], gTp[:, hh, cols], ALU.mult, [B_t1[hh], Bn["gTp"]], [B_oc[h][g]])
        else:
            li = lambda_init(layer)
            qh = a2.t("qh", [128, TOK], BF16)
            ko2 = [a2.t("ko%d" % i, [128, TOK], BF16) for i in range(2)]
            kp2 = [a2.t("kp%d" % i, [128, TOK], BF16) for i in range(2)]
            vo2 = [a2.t("vo%d" % i, [128, 16, 129], BF16) for i in range(2)]
            vp2 = [a2.t("vp%d" % i, [128, 16, 129], BF16) for i in range(2)]
            Th = a2.t("Th", [128, 512], F32)
            Tg = a2.t("Tg", [128, 128], F32)
            NTMP = 5
            SBOUND = 16.0
            FAR = [(104.0 + 2.0 * SBOUND) / sl for sl in SLOPES]
            SCB = ["pS0", "pS1", "pA0", "pA1"]
            tmp = [a2.t("tmp%d" % i, [128, 512], F32) for i in range(NTMP)]
            Pt = [a2.t("Pt%d" % i, [128, 512], BF16) for i in range(NTMP)]
            o0 = a2.t("o0", [128, 4, 128], F32)
            od = a2.t("od", [128, 128], F32)
            junk = a2.t("junk", [128, 128], F32)
            otok = a2.t("otok", [128, 4, 128], BF16)
            sm = a2.t("sm", [128, 16], F32)
            lt = a2.t("lt", [128, 64], F32)
            Bn = {n: TB(n) for n in ["qh", "ko", "kp", "vo", "vp", "Th", "Tg", "o0", "od", "junk", "otok", "sm", "lam", "lt"]}
            B_tmp = [TB("tmp%d" % i) for i in range(NTMP)]
            B_Pt = [TB("Pt%d" % i) for i in range(NTMP)]
            v_v2 = [v.rearrange("(t j) c -> j t c", j=128) for v in v_s2]
            vg_v2 = [v[0:1024, :].rearrange("(t j) c -> j t c", j=128) for v in v_g2]
            BnKV = [{n: TB(n + str(i)) for n in ["ko", "kp", "vo", "vp"]} for i in range(2)]
            for i in range(2):
                P.op("pool", lambda e, t_=vo2[i]: e.memset(t_[:], 1.0), reads=[], writes=[BnKV[i]["vo"]])
                P.op("pool", lambda e, t_=vp2[i]: e.memset(t_[:], 1.0), reads=[], writes=[BnKV[i]["vp"]])
            lv = C("lamv")
            for k in range(2):
                tt(lt[:], lv[:, (jj * 4 + 2 * k) * 64:(jj * 4 + 2 * k + 1) * 64], lv[:, (jj * 4 + 2 * k + 1) * 64:(jj * 4 + 2 * k + 2) * 64], ALU.mult, [B_cst], [Bn["lt"]])
                P.op("dve", lambda e, k=k: e.reduce_sum(out=sm[:, k:k + 1], in_=lt[:], axis=mybir.AxisListType.X), reads=[Bn["lt"]], writes=[Bn["sm"]])
            act(sm[:, 2:4], sm[:, 0:2], AF.Exp, [Bn["sm"]], [Bn["sm"]])
            tt(sm[:, 4:5], sm[:, 3:4], sm[:, 2:3], ALU.subtract, [Bn["sm"]], [Bn["sm"]])
            tsc(sm[:, 4:5], sm[:, 4:5], -li, ALU.add, [Bn["sm"]], [Bn["sm"]])
            neglam = sm[:, 4:5]
            accs = [(ps["pO0"], B_ps["pO0"], 0), (ps["pO0"], B_ps["pO0"], 256), (ps["pO1"], B_ps["pO1"], 0), (ps["pO1"], B_ps["pO1"], 256)]
            for h in range(6):
                rws_h = slice(h * 128, (h + 1) * 128)
                ko, kp, vo, vp = ko2[h % 2], kp2[h % 2], vo2[h % 2], vp2[h % 2]
                Bn.update(BnKV[h % 2])
                dma("sp", qh[:], qT_s[rws_h, :], [B_scr["qT"]], [Bn["qh"]])
                dma("sp", ko[:], kT_rows(h * 128), [B_scr["kT"]], [Bn["ko"]])
                dma("sp", kp[:], kT_g2[h // 3][(h % 3) * 128:(h % 3 + 1) * 128, :], [B_kTg], [Bn["kp"]])
                for i2 in range(2):
                    dma("sp", vo[:, i2 * 8:(i2 + 1) * 8, 0:128], v_v2[i2][:, :, rws_h], [B_scr["v"]], [Bn["vo"]])
                    dma("sp", vp[:, i2 * 8:(i2 + 1) * 8, 0:128], vg_v2[i2][:, :, rws_h], [B_vg], [Bn["vp"]])
                dma("sp", Th[:], T_d[h], [], [Bn["Th"]])
                dma("sp", Tg[:], Tg_d[h], [], [Bn["Tg"]])
                units = []
                for g in range(NG):
                    for c in range(2):
                        blocks = [("p", kb) for kb in range(16)] + [("o", kb) for kb in range(4 * g + 4)]
                        def _dist(src, kb, g=g):
                            return (2048 if src == "p" else 0) + 512 * g - (128 * kb + 127)
                        blocks = [(src, kb) for (src, kb) in blocks if _dist(src, kb) < FAR[h]]
                        for bi, (src, kb) in enumerate(blocks):
                            units.append(dict(g=g, c=c, src=src, kb=kb, bi=bi, nb=len(blocks)))

                def front(u, h=h):
                    g, c, src, kb = u["g"], u["c"], u["src"], u["kb"]
                    rws = slice(c * 64, (c + 1) * 64)
                    kk, Bk = (kp, Bn["kp"]) if src == "p" else (ko, Bn["ko"])
                    r = kb - 4 * g if src == "o" else -1
                    c_lo = 128 * r if r > 0 else 0
                    sn = SCB[rr("pS", len(SCB))]
                    psc, bsc = ps[sn], B_ps[sn]
                    mm(psc[:, c_lo:512], kk[rws, kb * 128:(kb + 1) * 128], qh[rws, g * 512 + c_lo:(g + 1) * 512], True, True, [Bk, Bn["qh"]], [bsc])
                    i = rr("tmp", NTMP)
                    tm, btm, pt_, bpt = tmp[i], B_tmp[i], Pt[i], B_Pt[i]
                    u["pt"] = (pt_, bpt)
                    u["r"] = r
                    if r < 0:
                        m = (16 + 4 * g - kb) if src == "p" else (4 * g - kb)
                        bcol = C("bprev")[:, h * 32 + m:h * 32 + m + 1] if src == "p" else C("bown")[:, h * 16 + m + 3:h * 16 + m + 4]
                        stt(tm[:], psc[:, :], 0.125, Th[:], ALU.mult, ALU.add, [bsc, Bn["Th"]], [btm])
                        act(pt_[:], tm[:], AF.Exp, [btm, B_cst], [bpt], bias=bcol)
                    else:
                        d0 = 128 * r
                        stt(tm[:, d0:d0 + 128], psc[:, d0:d0 + 128], 0.125, Tg[:], ALU.mult, ALU.add, [bsc, Bn["Tg"]], [btm])
                        if r < 3:
                            stt(tm[:, d0 + 128:512], psc[:, d0 + 128:512], 0.125, Th[:, d0 + 128:512], ALU.mult, ALU.add, [bsc, Bn["Th"]], [btm])
                        act(pt_[:, d0:d0 + 128], tm[:, d0:d0 + 128], AF.Exp, [btm], [bpt])
                        if r < 3:
                            bcol = C("bown")[:, h * 16 + (-r) + 3:h * 16 + (-r) + 4]
                            act(pt_[:, d0 + 128:512], tm[:, d0 + 128:512], AF.Exp, [btm, B_cst], [bpt], bias=bcol)

                def back(u, h=h):
                    g, c, src, kb, bi, r = u["g"], u["c"], u["src"], u["kb"], u["bi"], u["r"]
                    pt_, bpt = u["pt"]
                    vv, Bv = (vp, Bn["vp"]) if src == "p" else (vo, Bn["vo"])
                    for sub in range(4):
                        if r >= 0 and sub < r:
                            continue
                        pa_, ba_, off = accs[sub]
                        last = (src == "o" and kb == 4 * g + sub)
                        mm(pa_[:, off:off + 129], pt_[:, sub * 128:(sub + 1) * 128], vv[:, kb, :], bi == 0 and off == 0, last, [bpt, Bv], [ba_], skip=True)
                    if bi == u["nb"] - 1:
                        finalize(g, c, h)

                def finalize(g, c, h):
                    cols = slice(g * 512, (g + 1) * 512)
                    for sub in range(4):
                        pa_, ba_, off = accs[sub]
                        vrecip(sm[:, 8 + sub:9 + sub], pa_[:, off + 128:off + 129], [ba_], [Bn["sm"]])
                        if c == 0:
                            tsc(o0[:, sub, :], pa_[:, off:off + 128], sm[:, 8 + sub:9 + sub], ALU.mult, [ba_, Bn["sm"]], [Bn["o0"]])
                        else:
                            tt(sm[:, 12 + sub:13 + sub], sm[:, 8 + sub:9 + sub], neglam, ALU.mult, [Bn["sm"]], [Bn["sm"]])
                            stt(od[:], pa_[:, off:off + 128], sm[:, 12 + sub:13 + sub], o0[:, sub, :], ALU.mult, ALU.add, [ba_, Bn["sm"], Bn["o0"]], [Bn["od"]])
                            act(junk[:], od[:], AF.Square, [Bn["od"]], [Bn["junk"], Bn["sm"]], accum_out=sm[:, 5:6])
                            f = 1.0 - li
                            rstd_from(sm[:, 5:6], sm[:, 6:7], 128, [Bn["sm"]], [Bn["sm"]], extra=f)
                            stt(otok[:, sub, :], od[:], sm[:, 6:7], C("subw")[:, jj * 128:(jj + 1) * 128], ALU.mult, ALU.mult, [Bn["od"], Bn["sm"], B_cst], [Bn["otok"]])
                    if c == 1:
                        for sub in range(4):
                            tpose(pT_b[:, sub * 128:(sub + 1) * 128], otok[:, sub, :], id_b, [Bn["otok"], B_tb], [B_ps["pB"]])
                        acopy(ocT[:, h, cols], pT_b[:, 0:512], [B_ps["pB"]], [B_oc[h][g]])

                LOOK = 3
                for idx in range(len(units) + LOOK):
                    if idx < len(units):
                        front(units[idx])
                    if idx - LOOK >= 0:
                        back(units[idx - LOOK])

        if KSTOP <= 3:
            break
        for g in range(NG):
            cols = slice(g * 512, (g + 1) * 512)
            for oc in range(8):
                pt, bp = nextA()
                for kc in range(8):
                    mm(pt[:, :], w_out[:, kc, oc * 128:(oc + 1) * 128], ocT[:, kc, cols], kc == 0, kc == 7, [B_wout, B_oc[kc][g]], [bp])
                tt(xT[:, oc, cols], xT[:, oc, cols], pt[:, :], ALU.add, [bp, B_xT[oc][g]], [B_xT[oc][g]])
        P.barrier()

        if KSTOP <= 4:
            break
        a3 = Alloc(nc, ARENA, SB_LIMIT, "ff%d" % layer)
        A_BANKS[0] = ["pA0", "pA1", "pA2", "pS0", "pS1", "pO0", "pO1"]
        hT2 = [a3.t("hT%d" % i, [128, 8, 512], BF16) for i in range(2)]
        B_hT2 = [TB("hT0"), TB("hT1")]
        sq = [a3.t("sq%d" % i, [128, 512], BF16) for i in range(2)]
        B_sq = [TB("sq0"), TB("sq1")]
        rs = a3.t("rs", [128, 512], F32)
        B_rs = TB("rs")
        actT = a3.t("actT", [128, NFC, 1024], BF16)
        B_act = [TB("act0"), TB("act1")]
        NWG = 4
        wgu = [a3.t("wgu%d" % i, [128, 8, 2, 128], BF16) for i in range(NWG)]
        B_wgu = [TB("wgu%d" % i) for i in range(NWG)]
        wd = [a3.t("wd%d" % i, [128, NFC, 128], BF16) for i in range(2)]
        B_wd = [TB("wd0"), TB("wd1")]
        sg = [a3.t("sg%d" % i, [128, 512], BF16) for i in range(2)]
        B_sg = [TB("sg0"), TB("sg1")]
        wgu_v = w_gu_d[layer].rearrange("(kc p) n -> p kc n", p=128)
        wd_v = w_dn_d[layer].rearrange("(fc p) n -> p fc n", p=128)
        for hf in range(2):
            for gi in range(2):
                norm_group(2 * hf + gi, "fnw%d" % layer, hT2[gi], B_hT2[gi], sq, B_sq, rs, B_rs)
            for fc in range(NFC):
                i = rr("wgu", NWG)
                dma("pool", wgu[i][:, :, 0, :], wgu_v[:, :, fc * 128:(fc + 1) * 128], [], [B_wgu[i]])
                dma("pool", wgu[i][:, :, 1, :], wgu_v[:, :, FFN + fc * 128:FFN + (fc + 1) * 128], [], [B_wgu[i]])
                for gi in range(2):
                    pg, bg = fm_proj(lambda kc, i=i: wgu[i][:, kc, 0, :], 128, hT2[gi], B_hT2[gi], B_wgu[i])
                    pu, bu = fm_proj(lambda kc, i=i: wgu[i][:, kc, 1, :], 128, hT2[gi], B_hT2[gi], B_wgu[i])
                    k = rr("sg", 2)
                    act(sg[k][:], pg[:, :], AF.Silu, [bg], [B_sg[k]])
                    tt(actT[:, fc, gi * 512:(gi + 1) * 512], pu[:, :], sg[k][:], ALU.mult, [bu, B_sg[k]], [B_act[gi]])
            for oc in range(8):
                i = rr("wd", 2)
                dma("pool", wd[i][:], wd_v[:, :, oc * 128:(oc + 1) * 128], [], [B_wd[i]])
                for gi in range(2):
                    g = 2 * hf + gi
                    cols = slice(g * 512, (g + 1) * 512)
                    pt, bp = nextA()
                    for fc in range(NFC):
                        mm(pt[:, :], wd[i][:, fc, :], actT[:, fc, gi * 512:(gi + 1) * 512], fc == 0, fc == NFC - 1, [B_wd[i], B_act[gi]], [bp])
                    tt(xT[:, oc, cols], xT[:, oc, cols], pt[:, :], ALU.add, [bp, B_xT[oc][g]], [B_xT[oc][g]])
        P.barrier()

    a9 = Alloc(nc, ARENA, SB_LIMIT, "st")
    A_BANKS[0] = ["pA0", "pA1", "pA2"]
    xo = [a9.t("xo%d" % i, [128, D], F32) for i in range(2)]
    B_xo = [TB("xo0"), TB("xo1")]
    B_out = TB("out")
    for t in range(16):
        g = t // 4
        xo_, bxo = xo[t % 2], B_xo[t % 2]
        for hf in range(2):
            pt, bp = nextA()
            for k4 in range(4):
                kc = hf * 4 + k4
                tpose(pt[:, k4 * 128:(k4 + 1) * 128], xT[:, kc, t * 128:(t + 1) * 128], id_f, [B_xT[kc][g], B_cst], [bp])
            if hf == 0:
                vcopy(xo_[:, 0:512], pt[:, :], [bp], [bxo])
            else:
                acopy(xo_[:, 512:1024], pt[:, :], [bp], [bxo])
        dma("sp", out_d[t * 128:(t + 1) * 128, :], xo_[:], [bxo], [B_out])
    P.wait_all("sp", [B_out])

    with nc.Block() as block:
        P.emit(nc, block, es)
    es.close()
    return nc


_CACHE = {}


def kernel(**inputs):
    NL = int(os.environ.get("KNL", DEPTH))
    inputs = {k: np.asarray(v) for k, v in inputs.items()}
    if NL not in _CACHE:
        _CACHE[NL] = build_program(NL)
    nc = _CACHE[NL]
    tb, T, Tg = host_tables()
    x = inputs["x"]
    in_maps = []
    shared = {k: np.ascontiguousarray(inputs[k][:NL], dtype=np.float32) for k in ["w_in", "w_out", "w_mem_kv", "w_gate_up", "w_down"]}
    for c in range(8):
        b, half = c // 2, c % 2
        m = dict(shared)
        m["x"] = np.ascontiguousarray(x[b, half * TOK:(half + 1) * TOK, :], dtype=np.float32)
        m["mem"] = np.ascontiguousarray(inputs["mem"][b], dtype=np.float32)
        m["cst"] = host_consts(inputs, half)
        m["tb"] = tb
        m["Ttab"] = T
        m["Tgtab"] = Tg
        in_maps.append(m)
    res = run_bass_kernel_spmd(nc, in_maps, core_ids=list(range(8)))
    out = np.zeros((4, S, D), np.float32)
    for c in range(8):
        b, half = c // 2, c % 2
        out[b, half * TOK:(half + 1) * TOK, :] = res.results[c]["out"]
    return out
```

```python
import math
import os
import contextlib
import numpy as np
import ml_dtypes
import concourse.bass as bass
import concourse.mybir as mybir
from concourse.bass_utils import run_bass_kernel_spmd

F32 = mybir.dt.float32
BF16 = mybir.dt.bfloat16
AF = mybir.ActivationFunctionType
ALU = mybir.AluOpType

D = 1024
S = 4096
TOK = 2048
NG = 4
DEPTH = 4
FFN = 2816
NFC = 22
EPS = 1e-6
NEG = -30000.0


class TB:
    __slots__ = ("name", "w", "r", "x")

    def __init__(self, name="", x=False):
        self.name = name
        self.w = None
        self.r = {}
        self.x = x


CE = ("pe", "act", "dve", "pool")
STREAMS = ("pe", "act", "dve", "pool", "sp")
NDSEM = 8
STRICT_SAME = os.environ.get('KSTRICT', '1') == '1'


class Prog:
    def __init__(self):
        self.ops = {s: [] for s in STREAMS}
        self.seen = {s: {} for s in STREAMS}
        self.marked = {e: set() for e in CE}
        self.dma_cnt = {}
        self.dma_rr = {"sp": 0, "pool": 0}
        self.ncc = 0
        self.last = {}

    def _need(self, stream, tok, waits):
        if tok is None:
            return
        key, val = tok
        if key == stream and (stream == "pe" or not STRICT_SAME):
            return
        if self.seen[stream].get(key, 0) >= val:
            return
        if waits.get(key, 0) < val:
            waits[key] = val

    def _commit(self, stream, waits):
        for key, val in waits.items():
            self.seen[stream][key] = val
            if key in CE:
                self.marked[key].add(val)

    def op(self, stream, fn, reads=(), writes=(), dma=False, cc=False):
        waits = {}
        for b in reads:
            self._need(stream, b.w, waits)
            if b.x:
                for t in b.r.values():
                    self._need(stream, t, waits)
        for b in writes:
            self._need(stream, b.w, waits)
            for t in b.r.values():
                self._need(stream, t, waits)
        if cc:
            key = ("cc", self.ncc)
            self.ncc += 1
            self.dma_cnt[key] = 1
            tok = (key, 1)
            kind = 2
        elif dma:
            k = self.dma_rr[stream] % NDSEM
            self.dma_rr[stream] += 1
            key = ("dma", stream, k)
            n = self.dma_cnt.get(key, 0)
            if n > 0:
                self._need(stream, (key, 16 * n), waits)
            self.dma_cnt[key] = n + 1
            tok = (key, 16 * (n + 1))
            kind = 1
        else:
            tok = (stream, len(self.ops[stream]) + 1)
            kind = 0
        self._commit(stream, waits)
        self.ops[stream].append((waits, fn, tok, kind))
        self.last[tok[0]] = tok
        for b in reads:
            b.r[tok[0]] = tok
        for b in writes:
            b.w = tok
            b.r = {}
        return tok

    def wait_all(self, stream, bufs):
        waits = {}
        for b in bufs:
            self._need(stream, b.w, waits)
            for t in b.r.values():
                self._need(stream, t, waits)
        self._commit(stream, waits)
        self.ops[stream].append((waits, None, None, 0))

    def barrier(self):
        toks = [t for t in self.last.values() if not (isinstance(t[0], tuple) and t[0][0] == "cc")]
        for s in STREAMS:
            waits = {}
            for t in toks:
                self._need(s, t, waits)
            self._commit(s, waits)
            self.ops[s].append((waits, None, None, 0))

    def emit(self, nc, block, stack):
        sems = {}
        for e in CE:
            sems[e] = stack.enter_context(nc.semaphore("s_" + e))
        for key in self.dma_cnt:
            sems[key] = stack.enter_context(nc.semaphore("d_" + "_".join(str(k) for k in key)))
        rank = {}
        for e in CE:
            m = sorted(self.marked[e])
            rank[e] = {v: i + 1 for i, v in enumerate(m)}
        marked = self.marked

        def runner(stream):
            ops = self.ops[stream]

            def body(eng):
                for i, (waits, fn, tok, kind) in enumerate(ops, 1):
                    for key, val in waits.items():
                        v = rank[key][val] if key in CE else val
                        eng.wait_ge(sems[key], v)
                    if fn is None:
                        continue
                    ins = fn(eng)
                    if kind == 1:
                        ins.then_inc(sems[tok[0]], 16)
                    elif kind == 2:
                        ins.then_inc(sems[tok[0]], 1)
                    elif stream in CE and i in marked[stream]:
                        ins.then_inc(sems[stream], 1)
            return body

        block.tensor(runner("pe"))
        block.scalar(runner("act"))
        block.vector(runner("dve"))
        block.gpsimd(runner("pool"))
        block.sync(runner("sp"))


RET_H = 6
GAM = [1.0 - 2.0 ** (-5.0 - h) for h in range(RET_H)]
SLOPES = [2.0 ** (-8.0 * (h + 1) / 6.0) for h in range(6)]


def lambda_init(layer):
    return 0.8 - 0.6 * math.exp(-0.3 * layer)


class CMap:
    def __init__(self):
        self.n = 0
        self.m = {}

    def add(self, name, w):
        self.m[name] = (self.n, w)
        self.n += w

    def sl(self, name):
        a, w = self.m[name]
        return slice(a, a + w)


def build_cmap():
    c = CMap()
    for l in range(DEPTH):
        c.add("anw%d" % l, 8)
        c.add("fnw%d" % l, 8)
        c.add("mqw%d" % l, 1)
        c.add("mkw%d" % l, 1)
    c.add("mnw", 8)
    for j in range(2):
        c.add("dqw%d" % j, 1)
        c.add("dkw%d" % j, 1)
        for h in range(6):
            c.add("gnw%d_%d" % (j, h), 1)
    for p in range(3):
        c.add("g128_%d" % p, 1)
    c.add("flag", 1)
    c.add("bown", 6 * 16)
    c.add("bprev", 6 * 32)
    c.add("identf", 128)
    c.add("Dm", 6 * 128)
    c.add("dectab", 384)
    c.add("qdtab", 3 * 512)
    c.add("subw", 2 * 128)
    c.add("lamv", 8 * 64)
    return c


CM = build_cmap()


def host_consts(inputs, half):
    c = np.zeros((128, CM.n), np.float32)
    p = np.arange(128)

    def colvec(v):
        return np.asarray(v, np.float32).reshape(8, 128).T

    for l in range(DEPTH):
        c[:, CM.sl("anw%d" % l)] = colvec(inputs["attn_norm_w"][l])
        c[:, CM.sl("fnw%d" % l)] = colvec(inputs["ffn_norm_w"][l])
        c[:, CM.sl("mqw%d" % l)] = inputs["mem_q_norm_w"][l][p % 64][:, None]
        c[:, CM.sl("mkw%d" % l)] = inputs["mem_k_norm_w"][l][p % 64][:, None]
    c[:, CM.sl("mnw")] = colvec(inputs["mem_norm_w"])
    for j in range(2):
        c[:, CM.sl("dqw%d" % j)] = inputs["diff_q_norm_w"][j][p % 64][:, None]
        c[:, CM.sl("dkw%d" % j)] = inputs["diff_k_norm_w"][j][p % 64][:, None]
        for h in range(6):
            c[:, CM.sl("gnw%d_%d" % (j, h))] = inputs["ret_gn_w"][j][h][:, None]
    g = np.array(GAM, np.float64)
    for pr in range(3):
        c[:, CM.sl("g128_%d" % pr)] = (g[2 * pr + p // 64] ** 128).astype(np.float32)[:, None]
    c[:, CM.sl("flag")] = float(half)
    bo = np.zeros((6, 16), np.float32)
    bp = np.zeros((6, 32), np.float32)
    for h in range(6):
        for m in range(-3, 13):
            bo[h, m + 3] = -SLOPES[h] * 128.0 * m
        for m in range(32):
            bp[h, m] = (-SLOPES[h] * 128.0 * m) if half == 1 else NEG
    c[:, CM.sl("bown")] = bo.reshape(1, -1)
    c[:, CM.sl("bprev")] = bp.reshape(1, -1)
    c[:, CM.sl("identf")] = np.eye(128, dtype=np.float32)
    j_ = np.arange(128)[:, None]
    i_ = np.arange(128)[None, :]
    Dm = np.zeros((128, 6, 128), np.float64)
    for h in range(6):
        Dm[:, h, :] = np.where((j_ // 64) <= (i_ // 64), g[h] ** np.abs(i_ - j_), 0.0) / 8.0
    c[:, CM.sl("Dm")] = Dm.reshape(128, -1).astype(np.float32)
    dec = np.zeros((128, 6, 64), np.float64)
    for h in range(6):
        dec[:, h, :] = (g[h] ** (127 - np.arange(128)))[:, None] / 8.0
    c[:, CM.sl("dectab")] = dec.reshape(128, -1).astype(np.float32)
    qd = np.zeros((128, 3, 512), np.float64)
    for pr in range(3):
        gg = g[2 * pr + p // 64]
        qd[:, pr, :] = gg[:, None] ** ((np.arange(512) % 128) + 1)[None, :]
    c[:, CM.sl("qdtab")] = qd.reshape(128, -1).astype(np.float32)
    sw = np.zeros((128, 2, 128), np.float32)
    for j in range(2):
        sw[:, j, :] = inputs["diff_subln_w"][j][None, :]
    c[:, CM.sl("subw")] = sw.reshape(128, -1)
    lv = np.zeros((128, 8, 64), np.float32)
    for j in range(2):
        for k, nm in enumerate(["diff_lambda_q1", "diff_lambda_k1", "diff_lambda_q2", "diff_lambda_k2"]):
            lv[:, j * 4 + k, :] = inputs[nm][j][None, :]
    c[:, CM.sl("lamv")] = lv.reshape(128, -1)
    return c


def host_tables():
    tb = np.zeros((128, 3, 128), np.float32)
    tb[:, 0, :] = 1.0
    tb[:, 1, :] = ((np.arange(128)[:, None] // 64) == (np.arange(128)[None, :] // 64)).astype(np.float32)
    tb[:, 2, :] = np.eye(128, dtype=np.float32)
    tb = tb.astype(ml_dtypes.bfloat16)
    j_ = np.arange(128)[:, None].astype(np.float64)
    i5 = np.arange(512)[None, :].astype(np.float64)
    i1 = np.arange(128)[None, :].astype(np.float64)
    T = np.zeros((6, 128, 512), np.float32)
    Tg = np.zeros((6, 128, 128), np.float32)
    for h in range(6):
        T[h] = (-SLOPES[h] * (i5 - j_)).astype(np.float32)
        allowed = (j_ // 64) <= (i1 // 64)
        Tg[h] = np.where(allowed, -SLOPES[h] * np.abs(i1 - j_), NEG).astype(np.float32)
    return tb, T, Tg


class Alloc:
    def __init__(self, nc, base, limit, tag):
        self.nc = nc
        self.off = base
        self.limit = limit
        self.tag = tag
        self.i = 0

    def t(self, name, shape, dt):
        esz = 2 if dt == BF16 else 4
        n = 1
        for s in shape[1:]:
            n *= s
        nb = (n * esz + 63) // 64 * 64
        assert self.off + nb <= self.limit, (name, self.off, nb, self.limit)
        h = self.nc.alloc_sbuf_tensor_at("%s_%s_%d" % (self.tag, name, self.i), list(shape), dt, offset=self.off)
        self.i += 1
        self.off += nb
        return h


SB_BASE = 16384 + 512
SB_LIMIT = 224 * 1024 - 512


def build_program(NL):
    nc = bass.Bass("TRN2", target_bir_lowering=False)
    P = Prog()
    dram = {}

    def din(name, shape, dt=F32):
        dram[name] = nc.dram_tensor(name, list(shape), dt, kind="ExternalInput")
        return dram[name].ap()

    x_d = din("x", [TOK, D])
    mem_d = din("mem", [256, D])
    w_in_d = din("w_in", [NL, D, 2560])
    w_out_d = din("w_out", [NL, D, D])
    w_mkv_d = din("w_mem_kv", [NL, D, 512])
    w_gu_d = din("w_gate_up", [NL, D, 2 * FFN])
    w_dn_d = din("w_down", [NL, FFN, D])
    cst_d = din("cst", [128, CM.n])
    tb_d = din("tb", [128, 3, 128], BF16)
    T_d = din("Ttab", [6, 128, 512])
    Tg_d = din("Tgtab", [6, 128, 128])
    out_d = nc.dram_tensor("out", [TOK, D], F32, kind="ExternalOutput").ap()

    qT_s = nc.dram_tensor("qT_s", [768, TOK], BF16, kind="ExternalOutput").ap()
    qdT_s = nc.dram_tensor("qdT_s", [384, TOK], BF16, kind="ExternalOutput").ap()
    kT_s2 = [nc.dram_tensor("kT_s%d" % i, [384, TOK], BF16).ap() for i in range(2)]
    kT_g2 = [nc.dram_tensor("kT_g%d" % i, [768, TOK], BF16).ap() for i in range(2)]
    v_s2 = [nc.dram_tensor("v_s%d" % i, [1024, 768], BF16).ap() for i in range(2)]
    v_g2 = [nc.dram_tensor("v_g%d" % i, [2048, 768], BF16).ap() for i in range(2)]

    def kT_rows(r0):
        return kT_s2[r0 // 384][r0 % 384:r0 % 384 + 128, :]

    def v_rows(tok0):
        return v_s2[tok0 // 1024][tok0 % 1024:tok0 % 1024 + 128, :]
    gT_s = nc.dram_tensor("gT_s", [768, TOK], BF16, kind="ExternalOutput").ap()
    kt_s = nc.dram_tensor("kt_s", [TOK, 384], BF16, kind="ExternalOutput").ap()
    st_l = nc.dram_tensor("st_l", [384, 256], F32).ap()
    st_g = nc.dram_tensor("st_g", [768, 256], F32).ap()
    NCORES = int(os.environ.get('KNC', '8'))
    RG = [[2 * i, 2 * i + 1] for i in range(NCORES // 2)]

    pa = Alloc(nc, SB_BASE, SB_LIMIT, "p")
    xT = pa.t("xT", [128, 8, TOK], F32)
    cst = pa.t("cst", [128, CM.n], F32)
    tbs = pa.t("tb", [128, 3, 128], BF16)
    memnT = pa.t("memnT", [128, 8, 256], BF16)
    ARENA = pa.off
    ones_b = tbs[:, 0, :]
    bd64_b = tbs[:, 1, :]
    id_b = tbs[:, 2, :]
    id_f = cst[:, CM.sl("identf")]
    B_xT = [[TB("xT%d_%d" % (kc, g)) for g in range(NG)] for kc in range(8)]
    B_cst = TB("cst")
    B_tb = TB("tb")
    B_memnT = TB("memnT")

    def C(name):
        return cst[:, CM.sl(name)]

    es = contextlib.ExitStack()
    psn = ["pA0", "pA1", "pA2", "pB", "pS0", "pS1", "pO0", "pO1"]
    ps = {n: es.enter_context(nc.psum_tensor(n, [128, 512], F32)) for n in psn}
    B_ps = {n: TB(n, x=True) for n in psn}
    rrA = [0]

    A_BANKS = [["pA0", "pA1", "pA2"]]

    def nextA():
        lst = A_BANKS[0]
        n = lst[rrA[0] % len(lst)]
        rrA[0] += 1
        return ps[n], B_ps[n]

    rot = {}

    def rr(key, n):
        v = rot.get(key, 0)
        rot[key] = v + 1
        return v % n

    def dma(stream, out, in_, reads, writes):
        return P.op(stream, lambda e: e.dma_start(out=out, in_=in_), reads=reads, writes=writes, dma=True)

    def mm(out, lhsT, rhs, start, stop, reads, writes, skip=False):
        if skip:
            return P.op("pe", lambda e: e.matmul(out, lhsT=lhsT, rhs=rhs, start=start, stop=stop, skip_group_check=True), reads=reads, writes=writes)
        return P.op("pe", lambda e: e.matmul(out, lhsT=lhsT, rhs=rhs, start=start, stop=stop), reads=reads, writes=writes)

    def act(out, in_, func, reads, writes, bias=0.0, scale=1.0, accum_out=None):
        if accum_out is None:
            return P.op("act", lambda e: e.activation(out=out, in_=in_, func=func, bias=bias, scale=scale), reads=reads, writes=writes)
        return P.op("act", lambda e: e.activation(out=out, in_=in_, func=func, bias=bias, scale=scale, accum_out=accum_out), reads=reads, writes=writes)

    def dve(fn, reads, writes):
        return P.op("dve", fn, reads=reads, writes=writes)

    def rstd_from(psum_ap, out_ap, n, reads, writes, eps=EPS, extra=1.0):
        act(out_ap, psum_ap, AF.Sqrt, reads, writes, bias=eps / (extra * extra), scale=1.0 / (n * extra * extra))
        dve(lambda e: e.reciprocal(out=out_ap, in_=out_ap), writes, writes)

    wbf_in = [nc.dram_tensor("wbf_in%d" % l, [D, 2560], BF16).ap() for l in range(NL)]
    wbf_mkv = [nc.dram_tensor("wbf_mkv%d" % l, [D, 512], BF16).ap() for l in range(NL)]
    wbf_out = [nc.dram_tensor("wbf_out%d" % l, [D, D], BF16).ap() for l in range(NL)]
    B_wbf = [{"in": [TB("wbi%d_%d" % (l, k)) for k in range(8)], "mkv": [TB("wbm%d_%d" % (l, k)) for k in range(2)], "out": [TB("wbo%d_%d" % (l, k)) for k in range(4)]} for l in range(NL)]
    def emit_precast(l):
        for kc in range(8):
            rsl = slice(kc * 128, (kc + 1) * 128)
            P.op("pool", lambda e, l=l, rsl=rsl: e.dma_start(out=wbf_in[l][rsl, :], in_=w_in_d[l, rsl, :]), reads=[], writes=[B_wbf[l]["in"][rsl.start // 128]], dma=True)
        for kc in range(0, 8, 4):
            rsl = slice(kc * 128, (kc + 4) * 128)
            P.op("pool", lambda e, l=l, rsl=rsl: e.dma_start(out=wbf_mkv[l][rsl, :], in_=w_mkv_d[l, rsl, :]), reads=[], writes=[B_wbf[l]["mkv"][rsl.start // 512]], dma=True)
        for kc in range(0, 8, 2):
            rsl = slice(kc * 128, (kc + 2) * 128)
            P.op("pool", lambda e, l=l, rsl=rsl: e.dma_start(out=wbf_out[l][rsl, :], in_=w_out_d[l, rsl, :]), reads=[], writes=[B_wbf[l]["out"][rsl.start // 256]], dma=True)

    dma("sp", cst[:], cst_d, [], [B_cst])
    dma("sp", tbs[:], tb_d, [], [B_tb])

    a0 = Alloc(nc, ARENA, SB_LIMIT, "ld")
    xin = [a0.t("xin%d" % i, [128, D], F32) for i in range(2)]
    B_xin = [TB("xin0"), TB("xin1")]
    memin = a0.t("memin", [128, 2, D], F32)
    B_memin = TB("memin")
    memT = a0.t("memT", [128, 8, 256], F32)
    B_memT = TB("memT")
    sqm = a0.t("sqm", [128, 256], BF16)
    B_sqm = TB("sqm")
    rsm = a0.t("rsm", [128, 256], F32)
    B_rsm = TB("rsm")
    for t in range(16):
        xi, bx = xin[t % 2], B_xin[t % 2]
        dma("sp", xi[:], x_d[t * 128:(t + 1) * 128, :], [], [bx])
        for hf in range(2):
            pt, bp = nextA()
            for k4 in range(4):
                kc = hf * 4 + k4
                P.op("pe", lambda e, pt=pt, xi=xi, k4=k4, kc=kc: e.transpose(pt[:, k4 * 128:(k4 + 1) * 128], xi[:, kc * 128:(kc + 1) * 128], id_f),
                     reads=[bx, B_cst], writes=[bp])
            g = t // 4
            wr = [B_xT[hf * 4 + k4][g] for k4 in range(4)]
            outap = xT[:, hf * 4:hf * 4 + 4, t * 128:(t + 1) * 128]
            inap = pt[:, :].rearrange("p (k n) -> p k n", k=4)
            if hf == 0:
                dve(lambda e, o=outap, i=inap: e.tensor_copy(out=o, in_=i), [bp], wr)
            else:
                P.op("act", lambda e, o=outap, i=inap: e.copy(out=o, in_=i), reads=[bp], writes=wr)
    dma("sp", memin[:], mem_d.rearrange("(t p) d -> p t d", p=128), [], [B_memin])
    for mt in range(2):
        for hf in range(2):
            pt, bp = nextA()
            for k4 in range(4):
                kc = hf * 4 + k4
                P.op("pe", lambda e, pt=pt, k4=k4, kc=kc, mt=mt: e.transpose(pt[:, k4 * 128:(k4 + 1) * 128], memin[:, mt, kc * 128:(kc + 1) * 128], id_f),
                     reads=[B_memin, B_cst], writes=[bp])
            dve(lambda e, pt=pt, hf=hf, mt=mt: e.tensor_copy(out=memT[:, hf * 4:hf * 4 + 4, mt * 128:(mt + 1) * 128], in_=pt[:, :].rearrange("p (k n) -> p k n", k=4)),
                [bp], [B_memT])
    for kc in range(8):
        act(sqm[:], memT[:, kc, :], AF.Square, [B_memT], [B_sqm])
        mm(ps["pB"][:, 0:256], ones_b, sqm[:], kc == 0, kc == 7, [B_sqm, B_tb], [B_ps["pB"]])
    rstd_from(ps["pB"][:, 0:256], rsm[:], D, [B_ps["pB"]], [B_rsm])
    for kc in range(8):
        dve(lambda e, kc=kc: e.scalar_tensor_tensor(out=memnT[:, kc, :], in0=memT[:, kc, :], scalar=C("mnw")[:, kc:kc + 1], in1=rsm[:], op0=ALU.mult, op1=ALU.mult),
            [B_memT, B_rsm, B_cst], [B_memnT])
    P.barrier()

    def norm_group(g, wname, hT, B_h, sq, B_sq, rs, B_rs):
        cols = slice(g * 512, (g + 1) * 512)
        for kc in range(8):
            s, bs = sq[kc % 2], B_sq[kc % 2]
            act(s[:], xT[:, kc, cols], AF.Square, [B_xT[kc][g]], [bs])
            mm(ps["pB"][:, :], ones_b, s[:], kc == 0, kc == 7, [bs, B_tb], [B_ps["pB"]])
        rstd_from(ps["pB"][:, :], rs[:], D, [B_ps["pB"]], [B_rs])
        for kc in range(8):
            dve(lambda e, kc=kc: e.scalar_tensor_tensor(out=hT[:, kc, :], in0=xT[:, kc, cols], scalar=C(wname)[:, kc:kc + 1], in1=rs[:], op0=ALU.mult, op1=ALU.mult),
                [B_xT[kc][g], B_rs, B_cst], [B_h])

    def fm_proj(w_ap_fn, M, hT, B_h, B_w, ncols=512, hcols=None):
        pt, bp = nextA()
        for kc in range(8):
            rhs = hT[:, kc, :] if hcols is None else hT[:, kc, hcols]
            mm(pt[0:M, 0:ncols], w_ap_fn(kc), rhs, kc == 0, kc == 7, [B_h, B_w], [bp])
        return pt, bp

    def qknorm_evac(pt, bp, N, wcol, out_ap, B_out, sql, B_sql, rsl, B_rsl):
        i_ = rr("qkn", len(sql))
        sq, B_sq, rs, B_rs = sql[i_], B_sql[i_], rsl[i_], B_rsl[i_]
        sb_ = ["pS0", "pS1"][rr("qknb", 2)]
        act(sq[:, 0:N], pt[:, 0:N], AF.Square, [bp], [B_sq])
        mm(ps[sb_][:, 0:N], bd64_b, sq[:, 0:N], True, True, [B_sq, B_tb], [B_ps[sb_]])
        rstd_from(ps[sb_][:, 0:N], rs[:, 0:N], 64, [B_ps[sb_]], [B_rs])
        dve(lambda e: e.scalar_tensor_tensor(out=out_ap, in0=pt[:, 0:N], scalar=wcol, in1=rs[:, 0:N], op0=ALU.mult, op1=ALU.mult),
            [bp, B_rs, B_cst], [B_out])

    def stt(out, in0, scalar, in1, op0, op1, reads, writes):
        return P.op("dve", lambda e: e.scalar_tensor_tensor(out=out, in0=in0, scalar=scalar, in1=in1, op0=op0, op1=op1), reads=reads, writes=writes)

    def tt(out, in0, in1, op, reads, writes):
        return P.op("dve", lambda e: e.tensor_tensor(out=out, in0=in0, in1=in1, op=op), reads=reads, writes=writes)

    def tsc(out, in0, scalar1, op0, reads, writes):
        return P.op("dve", lambda e: e.tensor_scalar(out=out, in0=in0, scalar1=scalar1, scalar2=None, op0=op0), reads=reads, writes=writes)

    def vcopy(out, in_, reads, writes):
        return P.op("dve", lambda e: e.tensor_copy(out=out, in_=in_), reads=reads, writes=writes)

    def vrecip(out, in_, reads, writes):
        return P.op("dve", lambda e: e.reciprocal(out=out, in_=in_), reads=reads, writes=writes)

    def acopy(out, in_, reads, writes):
        return P.op("act", lambda e: e.copy(out=out, in_=in_), reads=reads, writes=writes)

    def tpose(out, in_, ident, reads, writes):
        return P.op("pe", lambda e: e.transpose(out, in_, ident), reads=reads, writes=writes)

    def wslice(w, c0, M=128):
        return lambda kc: w[:, kc, c0:c0 + M]

    KSTOP = int(os.environ.get('KSTOP', '99'))
    for layer in range(NL):
        if KSTOP <= 0:
            break
        is_ret = (layer % 2 == 0)
        jj = layer // 2
        a1 = Alloc(nc, ARENA, SB_LIMIT, "ip%d" % layer)
        qcnT = a1.t("qcnT", [128, 2, TOK], BF16)
        B_qcn = [TB("qcn%d" % g) for g in range(NG)]
        mkT = a1.t("mkT", [128, 2, 256], BF16)
        B_mkT = TB("mkT")
        mv = a1.t("mv", [128, 2, 4, 65], BF16)
        B_mv = TB("mv")
        AR2 = a1.off
        w_in = a1.t("w_in", [128, 8, 2560], BF16)
        B_win = TB("w_in")
        w_mkv = a1.t("w_mkv", [128, 8, 512], BF16)
        B_wmkv = TB("w_mkv")
        hT = [a1.t("hT%d" % i, [128, 8, 512], BF16) for i in range(2)]
        B_hT = [TB("hT0"), TB("hT1")]
        sq = [a1.t("sq%d" % i, [128, 512], BF16) for i in range(2)]
        B_sq = [TB("sq0"), TB("sq1")]
        rs = a1.t("rs", [128, 512], F32)
        B_rs = TB("rs")
        sq2 = [a1.t("sq2_%d" % i, [128, 512], BF16) for i in range(3)]
        B_sq2 = [TB("sq2_%d" % i) for i in range(3)]
        rs2 = [a1.t("rs2_%d" % i, [128, 512], F32) for i in range(3)]
        B_rs2 = [TB("rs2_%d" % i) for i in range(3)]
        A_BANKS[0] = ["pA0", "pA1", "pA2", "pO0", "pO1"]
        NST = 4
        stg = [a1.t("stg%d" % i, [128, 512], BF16) for i in range(NST)]
        B_stg = [TB("stg%d" % i) for i in range(NST)]
        stk = [a1.t("stk%d" % i, [128, 1152], BF16) for i in range(2)]
        B_stk = [TB("stk%d" % i) for i in range(2)]

        for kc in range(8):
            for q2 in range(2):
                if layer == 0:
                    dma("pool", w_in[:, kc, q2 * 1280:(q2 + 1) * 1280], w_in_d[layer, kc * 128:(kc + 1) * 128, q2 * 1280:(q2 + 1) * 1280], [], [B_win])
                else:
                    dma("sp", w_in[:, kc, q2 * 1280:(q2 + 1) * 1280], wbf_in[layer][kc * 128:(kc + 1) * 128, q2 * 1280:(q2 + 1) * 1280], [B_wbf[layer]["in"][kc]], [B_win])
            if layer == 0:
                dma("pool", w_mkv[:, kc, :], w_mkv_d[layer, kc * 128:(kc + 1) * 128, :], [], [B_wmkv])
            else:
                dma("sp", w_mkv[:, kc, :], wbf_mkv[layer][kc * 128:(kc + 1) * 128, :], [B_wbf[layer]["mkv"][kc // 4]], [B_wmkv])

        P.op("pool", lambda e, mv=mv: e.memset(mv[:], 1.0), reads=[], writes=[B_mv])
        for c2 in range(2):
            pt, bp = fm_proj(wslice(w_mkv, c2 * 128), 128, memnT, B_memnT, B_wmkv, ncols=256)
            qknorm_evac(pt, bp, 256, C("mkw%d" % layer), mkT[:, c2, :], B_mkT, sq2, B_sq2, rs2, B_rs2)
        for mt in range(2):
            pt, bp = nextA()
            for kc in range(8):
                mm(pt[:, 0:256], memnT[:, kc, mt * 128:(mt + 1) * 128], w_mkv[:, kc, 256:512], kc == 0, kc == 7, [B_memnT, B_wmkv], [bp])
            acopy(mv[:, mt, :, 0:64], pt[:, 0:256].rearrange("p (h d) -> p h d", h=4), [bp], [B_mv])

        B_scr = {n: TB(n) for n in ["qT", "qdT", "kT", "gT", "kt", "v"]}
        scr_lists = {n: [] for n in B_scr}

        def scr_w(name):
            t_ = TB(name)
            scr_lists[name].append(t_)
            return t_

        def stage_out(pt_ap, bp, dst_ap, kind, scr, mul_ap=None):
            if os.environ.get('KNOSTAGE') == '1':
                return
            if os.environ.get('KONLY') and os.environ.get('KONLY') != kind:
                return
            i = rr("stg", NST)
            st, bst = stg[i], B_stg[i]
            if kind == "copy":
                acopy(st[:], pt_ap, [bp], [bst])
            elif kind == "silu":
                act(st[:], pt_ap, AF.Silu, [bp], [bst])
            elif kind == "mul":
                tt(st[:], pt_ap, mul_ap, ALU.mult, [bp, B_cst], [bst])
            if os.environ.get('KNODMA') != '1':
                dma("sp", dst_ap, st[:], [bst], [scr_w(scr)])

        KSUB = int(os.environ.get('KSUB', '99'))
        for g in range(NG):
            if KSUB <= 1:
                break
            cols = slice(g * 512, (g + 1) * 512)
            h_, bh = hT[g % 2], B_hT[g % 2]
            norm_group(g, "anw%d" % layer, h_, bh, sq, B_sq, rs, B_rs)
            if KSUB <= 2:
                continue
            if is_ret:
                for pr in range(3):
                    pt, bp = fm_proj(wslice(w_in, pr * 128), 128, h_, bh, B_win)
                    stage_out(pt[:, :], bp, qT_s[pr * 128:(pr + 1) * 128, cols], "copy", "qT")
                    stage_out(pt[:, :], bp, qdT_s[pr * 128:(pr + 1) * 128, cols], "mul", "qdT", mul_ap=C("qdtab")[:, pr * 512:(pr + 1) * 512])
                for pr in range(3):
                    pt, bp = fm_proj(wslice(w_in, 384 + pr * 128), 128, h_, bh, B_win)
                    stage_out(pt[:, :], bp, kT_rows(pr * 128)[:, cols], "copy", "kT")
                for hh in range(6):
                    pt, bp = fm_proj(wslice(w_in, 1536 + hh * 128), 128, h_, bh, B_win)
                    stage_out(pt[:, :], bp, gT_s[hh * 128:(hh + 1) * 128, cols], "silu", "gT")
            else:
                for hh in range(6):
                    pt, bp = fm_proj(wslice(w_in, hh * 128), 128, h_, bh, B_win)
                    i = rr("stg", NST)
                    qknorm_evac(pt, bp, 512, C("dqw%d" % jj), stg[i][:], B_stg[i], sq2, B_sq2, rs2, B_rs2)
                    dma("sp", qT_s[hh * 128:(hh + 1) * 128, cols], stg[i][:], [B_stg[i]], [scr_w("qT")])
                for hh in range(6):
                    pt, bp = fm_proj(wslice(w_in, 768 + hh * 128), 128, h_, bh, B_win)
                    i = rr("stg", NST)
                    qknorm_evac(pt, bp, 512, C("dkw%d" % jj), stg[i][:], B_stg[i], sq2, B_sq2, rs2, B_rs2)
                    dma("sp", kT_rows(hh * 128)[:, cols], stg[i][:], [B_stg[i]], [scr_w("kT")])
            for c2 in range(2):
                pt, bp = fm_proj(wslice(w_in, 2304 + c2 * 128), 128, h_, bh, B_win)
                qknorm_evac(pt, bp, 512, C("mqw%d" % layer), qcnT[:, c2, cols], B_qcn[g], sq2, B_sq2, rs2, B_rs2)
            for t4 in range(4):
                if KSUB <= 3:
                    break
                tok0 = g * 512 + t4 * 128
                i = rr("stk", 2)
                sk, bsk = stk[i], B_stk[i]
                nparts = 3 if is_ret else 2
                cbase = 384 if is_ret else 1536
                for part in range(nparts):
                    pt, bp = nextA()
                    c0 = cbase + part * 384
                    for kc in range(8):
                        mm(pt[:, 0:384], h_[:, kc, t4 * 128:(t4 + 1) * 128], w_in[:, kc, c0:c0 + 384], kc == 0, kc == 7, [bh, B_win], [bp])
                    if is_ret and part == 0:
                        tt(sk[:, 0:384], pt[:, 0:384], C("dectab"), ALU.mult, [bp, B_cst], [bsk])
                    elif part % 2 == 0:
                        vcopy(sk[:, part * 384:(part + 1) * 384], pt[:, 0:384], [bp], [bsk])
                    else:
                        acopy(sk[:, part * 384:(part + 1) * 384], pt[:, 0:384], [bp], [bsk])
                if is_ret:
                    dma("sp", kt_s[tok0:tok0 + 128, :], sk[:, 0:384], [bsk], [scr_w("kt")])
                    dma("sp", v_rows(tok0), sk[:, 384:1152], [bsk], [scr_w("v")])
                else:
                    dma("sp", v_rows(tok0), sk[:, 0:768], [bsk], [scr_w("v")])
        B_kTg = TB("kT_g")
        B_vg = TB("v_g")
        if not is_ret:
            for i2 in range(2):
                P.op("pool", lambda e, i2=i2: e.collective_compute("AllGather", ALU.bypass, replica_groups=RG, ins=[kT_s2[i2]], outs=[kT_g2[i2]]),
                     reads=list(scr_lists["kT"]), writes=[B_kTg], cc=True)
            for i2 in range(2):
                P.op("pool", lambda e, i2=i2: e.collective_compute("AllGather", ALU.bypass, replica_groups=RG, ins=[v_s2[i2]], outs=[v_g2[i2]]),
                     reads=list(scr_lists["v"]), writes=[B_vg], cc=True)
        P.barrier()

        if KSTOP <= 1:
            break
        a2 = Alloc(nc, AR2, SB_LIMIT, "mx%d" % layer)
        A_BANKS[0] = ["pA0", "pA1", "pA2"]
        ocT = a2.t("ocT", [128, 8, TOK], BF16)
        B_oc = [[TB("oc%d_%d" % (c, g)) for g in range(NG)] for c in range(8)]
        w_out = a2.t("w_out", [128, 8, D], BF16)
        B_wout = TB("w_out")
        if layer == 0:
            dma("pool", w_out[:], w_out_d[layer].rearrange("(kc p) n -> p kc n", p=128), [], [B_wout])
        else:
            dma("sp", w_out[:], wbf_out[layer].rearrange("(kc p) n -> p kc n", p=128), list(B_wbf[layer]["out"]), [B_wout])
        if layer + 1 < NL:
            emit_precast(layer + 1)
        pm = [a2.t("pm%d" % i, [128, 512], BF16) for i in range(2)]
        B_pm = [TB("pm0"), TB("pm1")]
        omall = a2.t("omall", [128, 4, 256], BF16)
        B_omall = TB("omall")
        rl = a2.t("rl", [128, 8], F32)
        B_rl = TB("rl")
        pT_b = ps["pB"][:, :].bitcast(BF16)
        for g in range(NG):
            cols = slice(g * 512, (g + 1) * 512)
            for hd in range(4):
                c2, hh = hd // 2, hd % 2
                pacc = ps["pO%d" % (hd % 2)]
                bacc = B_ps["pO%d" % (hd % 2)]
                for mt in range(2):
                    sn = "pS%d" % rr("pS", 2)
                    mm(ps[sn][:, :], mkT[hh * 64:(hh + 1) * 64, c2, mt * 128:(mt + 1) * 128], qcnT[hh * 64:(hh + 1) * 64, c2, cols], True, True,
                       [B_mkT, B_qcn[g]], [B_ps[sn]])
                    i = rr("pm", 2)
                    act(pm[i][:], ps[sn][:, :], AF.Exp, [B_ps[sn]], [B_pm[i]], scale=0.125)
                    for sub in range(4):
                        mm(pacc[:, sub * 65:(sub + 1) * 65], pm[i][:, sub * 128:(sub + 1) * 128], mv[:, mt, hd, :], mt == 0 and sub == 0, mt == 1, [B_pm[i], B_mv], [bacc], skip=True)
                acc3 = pacc[:, 0:260].rearrange("p (s e) -> p s e", s=4)
                vrecip(rl[:, 0:4], acc3[:, :, 64], [bacc], [B_rl])
                for sub in range(4):
                    tsc(omall[:, sub, hd * 64:(hd + 1) * 64], pacc[:, sub * 65:sub * 65 + 64], rl[:, sub:sub + 1], ALU.mult, [bacc, B_rl], [B_omall])
            for c2 in range(2):
                for sub in range(4):
                    tpose(pT_b[:, (c2 * 4 + sub) * 128:(c2 * 4 + sub + 1) * 128], omall[:, sub, c2 * 128:(c2 + 1) * 128], id_b, [B_omall, B_tb], [B_ps["pB"]])
            for c2 in range(2):
                vcopy(ocT[:, 6 + c2, cols], pT_b[:, c2 * 512:(c2 + 1) * 512], [B_ps["pB"]], [B_oc[6 + c2][g]])

        if KSTOP <= 2:
            break
        if is_ret:
            qTp = a2.t("qTp", [128, TOK], BF16)
            qdTp = a2.t("qdTp", [128, TOK], BF16)
            kTp = a2.t("kTp", [128, TOK], BF16)
            ktp = a2.t("ktp", [128, 16, 128], BF16)
            vtp = a2.t("vtp", [128, 16, 256], BF16)
            gTp = a2.t("gTp", [128, 2, TOK], BF16)
            Sst = a2.t("Sst", [128, 256], F32)
            S0 = a2.t("S0", [128, 3, 256], F32)
            Sball = a2.t("Sball", [128, 16, 256], BF16)
            Aa2 = [[a2.t("A%d_%d" % (hh, i), [128, 128], BF16) for i in range(2)] for hh in range(2)]
            sqo2 = [a2.t("sqo%d" % i, [128, 512], BF16) for i in range(2)]
            rso2 = [a2.t("rso%d" % i, [128, 512], F32) for i in range(2)]
            t12 = [a2.t("t1_%d" % i, [128, 512], F32) for i in range(2)]
            B_sqo = [TB("sqo0"), TB("sqo1")]
            B_rso = [TB("rso0"), TB("rso1")]
            B_t1 = [TB("t1_0"), TB("t1_1")]
            Bn = {n: TB(n) for n in ["qTp", "qdTp", "kTp", "ktp", "vtp", "gTp", "S", "S0", "Sb", "A0", "A1", "sqo", "rso", "t1", "st_l", "st_g"]}
            kt_v = kt_s.rearrange("(t j) c -> j t c", j=128)
            v_v2 = [v.rearrange("(t j) c -> j t c", j=128) for v in v_s2]

            def load_vtp(pr):
                for i2 in range(2):
                    dma("sp", vtp[:, i2 * 8:(i2 + 1) * 8, :], v_v2[i2][:, :, pr * 256:(pr + 1) * 256], [B_scr["v"]], [Bn["vtp"]])

            def state_step(pr, b, first):
                sb_ = "pS1" if b % 2 == 0 else "pA2"
                mm(ps[sb_][:, 0:256], ktp[:, b, :], vtp[:, b, :], True, True, [Bn["ktp"], Bn["vtp"]], [B_ps[sb_]])
                if first:
                    vcopy(Sst[:], ps[sb_][:, 0:256], [B_ps[sb_]], [Bn["S"]])
                else:
                    stt(Sst[:], Sst[:], C("g128_%d" % pr), ps[sb_][:, 0:256], ALU.mult, ALU.add, [Bn["S"], B_ps[sb_], B_cst], [Bn["S"]])

            for pr in range(3):
                dma("sp", ktp[:], kt_v[:, :, pr * 128:(pr + 1) * 128], [B_scr["kt"]], [Bn["ktp"]])
                load_vtp(pr)
                for b in range(16):
                    state_step(pr, b, b == 0)
                dma("sp", st_l[pr * 128:(pr + 1) * 128, :], Sst[:], [Bn["S"]], [Bn["st_l"]])
            P.op("pool", lambda e: e.collective_compute("AllGather", ALU.bypass, replica_groups=RG, ins=[st_l], outs=[st_g]),
                 reads=[Bn["st_l"]], writes=[Bn["st_g"]], cc=True)
            dma("sp", S0[:], st_g[0:384, :].rearrange("(p r) c -> r p c", r=128), [Bn["st_g"]], [Bn["S0"]])
            for pr in range(3):
                dma("sp", qTp[:], qT_s[pr * 128:(pr + 1) * 128, :], [B_scr["qT"]], [Bn["qTp"]])
                dma("sp", qdTp[:], qdT_s[pr * 128:(pr + 1) * 128, :], [B_scr["qdT"]], [Bn["qdTp"]])
                dma("sp", kTp[:], kT_rows(pr * 128), [B_scr["kT"]], [Bn["kTp"]])
                dma("sp", ktp[:], kt_v[:, :, pr * 128:(pr + 1) * 128], [B_scr["kt"]], [Bn["ktp"]])
                load_vtp(pr)
                dma("sp", gTp[:], gT_s[pr * 256:(pr + 1) * 256, :].rearrange("(h p) n -> p h n", p=128), [B_scr["gT"]], [Bn["gTp"]])
                tsc(Sst[:], S0[:, pr, :], C("flag"), ALU.mult, [Bn["S0"], B_cst], [Bn["S"]])
                B_Sb = [TB("Sb%d" % b) for b in range(16)]
                B_A = [[TB("A%d_%d" % (hh, i)) for i in range(2)] for hh in range(2)]
                acopy(Sball[:, 0, :], Sst[:], [Bn["S"]], [B_Sb[0]])

                def rfront(b, pr=pr):
                    cb = slice(b * 128, (b + 1) * 128)
                    for hh in range(2):
                        h = 2 * pr + hh
                        rws = slice(hh * 64, (hh + 1) * 64)
                        psc = ps["pS0"][:, hh * 128:(hh + 1) * 128]
                        mm(psc, kTp[rws, cb], qTp[rws, cb], True, True, [Bn["kTp"], Bn["qTp"]], [B_ps["pS0"]])
                        tt(Aa2[hh][b % 2][:], psc, C("Dm")[:, h * 128:(h + 1) * 128], ALU.mult, [B_ps["pS0"], B_cst], [B_A[hh][b % 2]])

                rfront(0)
                for b in range(16):
                    cb = slice(b * 128, (b + 1) * 128)
                    if b + 1 < 16:
                        rfront(b + 1)
                    g = b // 4
                    obank = ["pO0", "pO1"] if g % 2 == 0 else ["pA0", "pA1"]
                    for hh in range(2):
                        rws = slice(hh * 64, (hh + 1) * 64)
                        po = ps[obank[hh]][:, (b % 4) * 128:(b % 4 + 1) * 128]
                        mm(po, vtp[:, b, hh * 128:(hh + 1) * 128], Aa2[hh][b % 2][:], True, False, [Bn["vtp"], B_A[hh][b % 2]], [B_ps[obank[hh]]])
                        mm(po, Sball[rws, b, hh * 128:(hh + 1) * 128], qdTp[rws, cb], False, True, [B_Sb[b], Bn["qdTp"]], [B_ps[obank[hh]]])
                    state_step(pr, b, False)
                    if b + 1 < 16:
                        acopy(Sball[:, b + 1, :], Sst[:], [Bn["S"]], [B_Sb[b + 1]])
                    if b % 4 == 3:
                        cols = slice(g * 512, (g + 1) * 512)
                        for hh in range(2):
                            h = 2 * pr + hh
                            pO, bO = ps[obank[hh]], B_ps[obank[hh]]
                            sbk = "pB"
                            act(sqo2[hh][:], pO[:, :], AF.Square, [bO], [B_sqo[hh]])
                            mm(ps[sbk][:, :], ones_b, sqo2[hh][:], True, True, [B_sqo[hh], B_tb], [B_ps[sbk]])
                            rstd_from(ps[sbk][:, :], rso2[hh][:], 128, [B_ps[sbk]], [B_rso[hh]])
                            stt(t12[hh][:], pO[:, :], C("gnw%d_%d" % (jj, h)), rso2[hh][:], ALU.mult, ALU.mult, [bO, B_rso[hh], B_cst], [B_t1[hh]])
                            tt(ocT[:, h, cols], t12[hh][:], gTp[:, hh, cols], ALU.mult, [B_t1[hh], Bn["gTp"]], [B_oc[h][g]])
        else:
            li = lambda_init(layer)
            qh = a2.t("qh", [128, TOK], BF16)
            ko2 = [a2.t("ko%d" % i, [128, TOK], BF16) for i in range(2)]
            kp2 = [a2.t("kp%d" % i, [128, TOK], BF16) for i in range(2)]
            vo2 = [a2.t("vo%d" % i, [128, 16, 129], BF16) for i in range(2)]
            vp2 = [a2.t("vp%d" % i, [128, 16, 129], BF16) for i in range(2)]
            Th = a2.t("Th", [128, 512], F32)
            Tg = a2.t("Tg", [128, 128], F32)
            NTMP = 5
            SBOUND = 16.0
            FAR = [(104.0 + 2.0 * SBOUND) / sl for sl in SLOPES]
            SCB = ["pS0", "pS1", "pA0", "pA1"]
            tmp = [a2.t("tmp%d" % i, [128, 512], F32) for i in range(NTMP)]
            Pt = [a2.t("Pt%d" % i, [128, 512], BF16) for i in range(NTMP)]
            o0 = a2.t("o0", [128, 4, 128], F32)
            od = a2.t("od", [128, 128], F32)
            junk = a2.t("junk", [128, 128], F32)
            otok = a2.t("otok", [128, 4, 128], BF16)
            sm = a2.t("sm", [128, 16], F32)
            lt = a2.t("lt", [128, 64], F32)
            Bn = {n: TB(n) for n in ["qh", "ko", "kp", "vo", "vp", "Th", "Tg", "o0", "od", "junk", "otok", "sm", "lam", "lt"]}
            B_tmp = [TB("tmp%d" % i) for i in range(NTMP)]
            B_Pt = [TB("Pt%d" % i) for i in range(NTMP)]
            v_v2 = [v.rearrange("(t j) c -> j t c", j=128) for v in v_s2]
            vg_v2 = [v[0:1024, :].rearrange("(t j) c -> j t c", j=128) for v in v_g2]
            BnKV = [{n: TB(n + str(i)) for n in ["ko", "kp", "vo", "vp"]} for i in range(2)]
            for i in range(2):
                P.op("pool", lambda e, t_=vo2[i]: e.memset(t_[:], 1.0), reads=[], writes=[BnKV[i]["vo"]])
                P.op("pool", lambda e, t_=vp2[i]: e.memset(t_[:], 1.0), reads=[], writes=[BnKV[i]["vp"]])
            lv = C("lamv")
            for k in range(2):
                tt(lt[:], lv[:, (jj * 4 + 2 * k) * 64:(jj * 4 + 2 * k + 1) * 64], lv[:, (jj * 4 + 2 * k + 1) * 64:(jj * 4 + 2 * k + 2) * 64], ALU.mult, [B_cst], [Bn["lt"]])
                P.op("dve", lambda e, k=k: e.reduce_sum(out=sm[:, k:k + 1], in_=lt[:], axis=mybir.AxisListType.X), reads=[Bn["lt"]], writes=[Bn["sm"]])
            act(sm[:, 2:4], sm[:, 0:2], AF.Exp, [Bn["sm"]], [Bn["sm"]])
            tt(sm[:, 4:5], sm[:, 3:4], sm[:, 2:3], ALU.subtract, [Bn["sm"]], [Bn["sm"]])
            tsc(sm[:, 4:5], sm[:, 4:5], -li, ALU.add, [Bn["sm"]], [Bn["sm"]])
            neglam = sm[:, 4:5]
            accs = [(ps["pO0"], B_ps["pO0"], 0), (ps["pO0"], B_ps["pO0"], 256), (ps["pO1"], B_ps["pO1"], 0), (ps["pO1"], B_ps["pO1"], 256)]
            for h in range(6):
                rws_h = slice(h * 128, (h + 1) * 128)
                ko, kp, vo, vp = ko2[h % 2], kp2[h % 2], vo2[h % 2], vp2[h % 2]
                Bn.update(BnKV[h % 2])
                dma("sp", qh[:], qT_s[rws_h, :], [B_scr["qT"]], [Bn["qh"]])
                dma("sp", ko[:], kT_rows(h * 128), [B_scr["kT"]], [Bn["ko"]])
                dma("sp", kp[:], kT_g2[h // 3][(h % 3) * 128:(h % 3 + 1) * 128, :], [B_kTg], [Bn["kp"]])
                for i2 in range(2):
                    dma("sp", vo[:, i2 * 8:(i2 + 1) * 8, 0:128], v_v2[i2][:, :, rws_h], [B_scr["v"]], [Bn["vo"]])
                    dma("sp", vp[:, i2 * 8:(i2 + 1) * 8, 0:128], vg_v2[i2][:, :, rws_h], [B_vg], [Bn["vp"]])
                dma("sp", Th[:], T_d[h], [], [Bn["Th"]])
                dma("sp", Tg[:], Tg_d[h], [], [Bn["Tg"]])
                units = []
                for g in range(NG):
                    for c in range(2):
                        blocks = [("p", kb) for kb in range(16)] + [("o", kb) for kb in range(4 * g + 4)]
                        def _dist(src, kb, g=g):
                            return (2048 if src == "p" else 0) + 512 * g - (128 * kb + 127)
                        blocks = [(src, kb) for (src, kb) in blocks if _dist(src, kb) < FAR[h]]
                        for bi, (src, kb) in enumerate(blocks):
                            units.append(dict(g=g, c=c, src=src, kb=kb, bi=bi, nb=len(blocks)))

                def front(u, h=h):
                    g, c, src, kb = u["g"], u["c"], u["src"], u["kb"]
                    rws = slice(c * 64, (c + 1) * 64)
                    kk, Bk = (kp, Bn["kp"]) if src == "p" else (ko, Bn["ko"])
                    r = kb - 4 * g if src == "o" else -1
                    c_lo = 128 * r if r > 0 else 0
                    sn = SCB[rr("pS", len(SCB))]
                    psc, bsc = ps[sn], B_ps[sn]
                    mm(psc[:, c_lo:512], kk[rws, kb * 128:(kb + 1) * 128], qh[rws, g * 512 + c_lo:(g + 1) * 512], True, True, [Bk, Bn["qh"]], [bsc])
                    i = rr("tmp", NTMP)
                    tm, btm, pt_, bpt = tmp[i], B_tmp[i], Pt[i], B_Pt[i]
                    u["pt"] = (pt_, bpt)
                    u["r"] = r
                    if r < 0:
                        m = (16 + 4 * g - kb) if src == "p" else (4 * g - kb)
                        bcol = C("bprev")[:, h * 32 + m:h * 32 + m + 1] if src == "p" else C("bown")[:, h * 16 + m + 3:h * 16 + m + 4]
                        stt(tm[:], psc[:, :], 0.125, Th[:], ALU.mult, ALU.add, [bsc, Bn["Th"]], [btm])
                        act(pt_[:], tm[:], AF.Exp, [btm, B_cst], [bpt], bias=bcol)
                    else:
                        d0 = 128 * r
                        stt(tm[:, d0:d0 + 128], psc[:, d0:d0 + 128], 0.125, Tg[:], ALU.mult, ALU.add, [bsc, Bn["Tg"]], [btm])
                        if r < 3:
                            stt(tm[:, d0 + 128:512], psc[:, d0 + 128:512], 0.125, Th[:, d0 + 128:512], ALU.mult, ALU.add, [bsc, Bn["Th"]], [btm])
                        act(pt_[:, d0:d0 + 128], tm[:, d0:d0 + 128], AF.Exp, [btm], [bpt])
                        if r < 3:
                            bcol = C("bown")[:, h * 16 + (-r) + 3:h * 16 + (-r) + 4]
                            act(pt_[:, d0 + 128:512], tm[:, d0 + 128:512], AF.Exp, [btm, B_cst], [bpt], bias=bcol)

                def back(u, h=h):
                    g, c, src, kb, bi, r = u["g"], u["c"], u["src"], u["kb"], u["bi"], u["r"]
                    pt_, bpt = u["pt"]
                    vv, Bv = (vp, Bn["vp"]) if src == "p" else (vo, Bn["vo"])
                    for sub in range(4):
                        if r >= 0 and sub < r:
                            continue
                        pa_, ba_, off = accs[sub]
                        last = (src == "o" and kb == 4 * g + sub)
                        mm(pa_[:, off:off + 129], pt_[:, sub * 128:(sub + 1) * 128], vv[:, kb, :], bi == 0 and off == 0, last, [bpt, Bv], [ba_], skip=True)
                    if bi == u["nb"] - 1:
                        finalize(g, c, h)

                def finalize(g, c, h):
                    cols = slice(g * 512, (g + 1) * 512)
                    for sub in range(4):
                        pa_, ba_, off = accs[sub]
                        vrecip(sm[:, 8 + sub:9 + sub], pa_[:, off + 128:off + 129], [ba_], [Bn["sm"]])
                        if c == 0:
                            tsc(o0[:, sub, :], pa_[:, off:off + 128], sm[:, 8 + sub:9 + sub], ALU.mult, [ba_, Bn["sm"]], [Bn["o0"]])
                        else:
                            tt(sm[:, 12 + sub:13 + sub], sm[:, 8 + sub:9 + sub], neglam, ALU.mult, [Bn["sm"]], [Bn["sm"]])
                            stt(od[:], pa_[:, off:off + 128], sm[:, 12 + sub:13 + sub], o0[:, sub, :], ALU.mult, ALU.add, [ba_, Bn["sm"], Bn["o0"]], [Bn["od"]])
                            act(junk[:], od[:], AF.Square, [Bn["od"]], [Bn["junk"], Bn["sm"]], accum_out=sm[:, 5:6])
                            f = 1.0 - li
                            rstd_from(sm[:, 5:6], sm[:, 6:7], 128, [Bn["sm"]], [Bn["sm"]], extra=f)
                            stt(otok[:, sub, :], od[:], sm[:, 6:7], C("subw")[:, jj * 128:(jj + 1) * 128], ALU.mult, ALU.mult, [Bn["od"], Bn["sm"], B_cst], [Bn["otok"]])
                    if c == 1:
                        for sub in range(4):
                            tpose(pT_b[:, sub * 128:(sub + 1) * 128], otok[:, sub, :], id_b, [Bn["otok"], B_tb], [B_ps["pB"]])
                        acopy(ocT[:, h, cols], pT_b[:, 0:512], [B_ps["pB"]], [B_oc[h][g]])

                LOOK = 3
                for idx in range(len(units) + LOOK):
                    if idx < len(units):
                        front(units[idx])
                    if idx - LOOK >= 0:
                        back(units[idx - LOOK])

        if KSTOP <= 3:
            break
        for g in range(NG):
            cols = slice(g * 512, (g + 1) * 512)
            for oc in range(8):
                pt, bp = nextA()
                for kc in range(8):
                    mm(pt[:, :], w_out[:, kc, oc * 128:(oc + 1) * 128], ocT[:, kc, cols], kc == 0, kc == 7, [B_wout, B_oc[kc][g]], [bp])
                tt(xT[:, oc, cols], xT[:, oc, cols], pt[:, :], ALU.add, [bp, B_xT[oc][g]], [B_xT[oc][g]])
        P.barrier()

        if KSTOP <= 4:
            break
        a3 = Alloc(nc, ARENA, SB_LIMIT, "ff%d" % layer)
        A_BANKS[0] = ["pA0", "pA1", "pA2", "pS0", "pS1", "pO0", "pO1"]
        hT2 = [a3.t("hT%d" % i, [128, 8, 512], BF16) for i in range(2)]
        B_hT2 = [TB("hT0"), TB("hT1")]
        sq = [a3.t("sq%d" % i, [128, 512], BF16) for i in range(2)]
        B_sq = [TB("sq0"), TB("sq1")]
        rs = a3.t("rs", [128, 512], F32)
        B_rs = TB("rs")
        actT = a3.t("actT", [128, NFC, 1024], BF16)
        B_act = [TB("act0"), TB("act1")]
        NWG = 4
        wgu = [a3.t("wgu%d" % i, [128, 8, 2, 128], BF16) for i in range(NWG)]
        B_wgu = [TB("wgu%d" % i) for i in range(NWG)]
        wd = [a3.t("wd%d" % i, [128, NFC, 128], BF16) for i in range(2)]
        B_wd = [TB("wd0"), TB("wd1")]
        sg = [a3.t("sg%d" % i, [128, 512], BF16) for i in range(2)]
        B_sg = [TB("sg0"), TB("sg1")]
        wgu_v = w_gu_d[layer].rearrange("(kc p) n -> p kc n", p=128)
        wd_v = w_dn_d[layer].rearrange("(fc p) n -> p fc n", p=128)
        for hf in range(2):
            for gi in range(2):
                norm_group(2 * hf + gi, "fnw%d" % layer, hT2[gi], B_hT2[gi], sq, B_sq, rs, B_rs)
            for fc in range(NFC):
                i = rr("wgu", NWG)
                dma("pool", wgu[i][:, :, 0, :], wgu_v[:, :, fc * 128:(fc + 1) * 128], [], [B_wgu[i]])
                dma("pool", wgu[i][:, :, 1, :], wgu_v[:, :, FFN + fc * 128:FFN + (fc + 1) * 128], [], [B_wgu[i]])
                for gi in range(2):
                    pg, bg = fm_proj(lambda kc, i=i: wgu[i][:, kc, 0, :], 128, hT2[gi], B_hT2[gi], B_wgu[i])
                    pu, bu = fm_proj(lambda kc, i=i: wgu[i][:, kc, 1, :], 128, hT2[gi], B_hT2[gi], B_wgu[i])
                    k = rr("sg", 2)
                    act(sg[k][:], pg[:, :], AF.Silu, [bg], [B_sg[k]])
                    tt(actT[:, fc, gi * 512:(gi + 1) * 512], pu[:, :], sg[k][:], ALU.mult, [bu, B_sg[k]], [B_act[gi]])
            for oc in range(8):
                i = rr("wd", 2)
                dma("pool", wd[i][:], wd_v[:, :, oc * 128:(oc + 1) * 128], [], [B_wd[i]])
                for gi in range(2):
                    g = 2 * hf + gi
                    cols = slice(g * 512, (g + 1) * 512)
                    pt, bp = nextA()
                    for fc in range(NFC):
                        mm(pt[:, :], wd[i][:, fc, :], actT[:, fc, gi * 512:(gi + 1) * 512], fc == 0, fc == NFC - 1, [B_wd[i], B_act[gi]], [bp])
                    tt(xT[:, oc, cols], xT[:, oc, cols], pt[:, :], ALU.add, [bp, B_xT[oc][g]], [B_xT[oc][g]])
        P.barrier()

    a9 = Alloc(nc, ARENA, SB_LIMIT, "st")
    A_BANKS[0] = ["pA0", "pA1", "pA2"]
    xo = [a9.t("xo%d" % i, [128, D], F32) for i in range(2)]
    B_xo = [TB("xo0"), TB("xo1")]
    B_out = TB("out")
    for t in range(16):
        g = t // 4
        xo_, bxo = xo[t % 2], B_xo[t % 2]
        for hf in range(2):
            pt, bp = nextA()
            for k4 in range(4):
                kc = hf * 4 + k4
                tpose(pt[:, k4 * 128:(k4 + 1) * 128], xT[:, kc, t * 128:(t + 1) * 128], id_f, [B_xT[kc][g], B_cst], [bp])
            if hf == 0:
                vcopy(xo_[:, 0:512], pt[:, :], [bp], [bxo])
            else:
                acopy(xo_[:, 512:1024], pt[:, :], [bp], [bxo])
        dma("sp", out_d[t * 128:(t + 1) * 128, :], xo_[:], [bxo], [B_out])
    P.wait_all("sp", [B_out])

    with nc.Block() as block:
        P.emit(nc, block, es)
    es.close()
    return nc


_CACHE = {}


def kernel(**inputs):
    NL = int(os.environ.get("KNL", DEPTH))
    inputs = {k: np.asarray(v) for k, v in inputs.items()}
    if NL not in _CACHE:
        _CACHE[NL] = build_program(NL)
    nc = _CACHE[NL]
    tb, T, Tg = host_tables()
    x = inputs["x"]
    in_maps = []
    shared = {k: np.ascontiguousarray(inputs[k][:NL], dtype=np.float32) for k in ["w_in", "w_out", "w_mem_kv", "w_gate_up", "w_down"]}
    for c in range(8):
        b, half = c // 2, c % 2
        m = dict(shared)
        m["x"] = np.ascontiguousarray(x[b, half * TOK:(half + 1) * TOK, :], dtype=np.float32)
        m["mem"] = np.ascontiguousarray(inputs["mem"][b], dtype=np.float32)
        m["cst"] = host_consts(inputs, half)
        m["tb"] = tb
        m["Ttab"] = T
        m["Tgtab"] = Tg
        in_maps.append(m)
    res = run_bass_kernel_spmd(nc, in_maps, core_ids=list(range(8)))
    out = np.zeros((4, S, D), np.float32)
    for c in range(8):
        b, half = c // 2, c % 2
        out[b, half * TOK:(half + 1) * TOK, :] = res.results[c]["out"]
    return out
```

```python
import math
import os
import contextlib
import numpy as np
import ml_dtypes
import concourse.bass as bass
import concourse.mybir as mybir
from concourse.bass_utils import run_bass_kernel_spmd

F32 = mybir.dt.float32
BF16 = mybir.dt.bfloat16
AF = mybir.ActivationFunctionType
ALU = mybir.AluOpType

D = 1024
S = 4096
TOK = 2048
NG = 4
DEPTH = 4
FFN = 2816
NFC = 22
EPS = 1e-6
NEG = -30000.0


class TB:
    __slots__ = ("name", "w", "r", "x")

    def __init__(self, name="", x=False):
        self.name = name
        self.w = None
        self.r = {}
        self.x = x


CE = ("pe", "act", "dve", "pool")
STREAMS = ("pe", "act", "dve", "pool", "sp")
NDSEM = 8
STRICT_SAME = os.environ.get('KSTRICT', '1') == '1'


class Prog:
    def __init__(self):
        self.ops = {s: [] for s in STREAMS}
        self.seen = {s: {} for s in STREAMS}
        self.marked = {e: set() for e in CE}
        self.dma_cnt = {}
        self.dma_rr = {"sp": 0, "pool": 0}
        self.ncc = 0
        self.last = {}

    def _need(self, stream, tok, waits):
        if tok is None:
            return
        key, val = tok
        if key == stream and (stream == "pe" or not STRICT_SAME):
            return
        if self.seen[stream].get(key, 0) >= val:
            return
        if waits.get(key, 0) < val:
            waits[key] = val

    def _commit(self, stream, waits):
        for key, val in waits.items():
            self.seen[stream][key] = val
            if key in CE:
                self.marked[key].add(val)

    def op(self, stream, fn, reads=(), writes=(), dma=False, cc=False):
        waits = {}
        for b in reads:
            self._need(stream, b.w, waits)
            if b.x:
                for t in b.r.values():
                    self._need(stream, t, waits)
        for b in writes:
            self._need(stream, b.w, waits)
            for t in b.r.values():
                self._need(stream, t, waits)
        if cc:
            key = ("cc", self.ncc)
            self.ncc += 1
            self.dma_cnt[key] = 1
            tok = (key, 1)
            kind = 2
        elif dma:
            k = self.dma_rr[stream] % NDSEM
            self.dma_rr[stream] += 1
            key = ("dma", stream, k)
            n = self.dma_cnt.get(key, 0)
            if n > 0:
                self._need(stream, (key, 16 * n), waits)
            self.dma_cnt[key] = n + 1
            tok = (key, 16 * (n + 1))
            kind = 1
        else:
            tok = (stream, len(self.ops[stream]) + 1)
            kind = 0
        self._commit(stream, waits)
        self.ops[stream].append((waits, fn, tok, kind))
        self.last[tok[0]] = tok
        for b in reads:
            b.r[tok[0]] = tok
        for b in writes:
            b.w = tok
            b.r = {}
        return tok

    def wait_all(self, stream, bufs):
        waits = {}
        for b in bufs:
            self._need(stream, b.w, waits)
            for t in b.r.values():
                self._need(stream, t, waits)
        self._commit(stream, waits)
        self.ops[stream].append((waits, None, None, 0))

    def barrier(self):
        toks = [t for t in self.last.values() if not (isinstance(t[0], tuple) and t[0][0] == "cc")]
        for s in STREAMS:
            waits = {}
            for t in toks:
                self._need(s, t, waits)
            self._commit(s, waits)
            self.ops[s].append((waits, None, None, 0))

    def emit(self, nc, block, stack):
        sems = {}
        for e in CE:
            sems[e] = stack.enter_context(nc.semaphore("s_" + e))
        for key in self.dma_cnt:
            sems[key] = stack.enter_context(nc.semaphore("d_" + "_".join(str(k) for k in key)))
        rank = {}
        for e in CE:
            m = sorted(self.marked[e])
            rank[e] = {v: i + 1 for i, v in enumerate(m)}
        marked = self.marked

        def runner(stream):
            ops = self.ops[stream]

            def body(eng):
                for i, (waits, fn, tok, kind) in enumerate(ops, 1):
                    for key, val in waits.items():
                        v = rank[key][val] if key in CE else val
                        eng.wait_ge(sems[key], v)
                    if fn is None:
                        continue
                    ins = fn(eng)
                    if kind == 1:
                        ins.then_inc(sems[tok[0]], 16)
                    elif kind == 2:
                        ins.then_inc(sems[tok[0]], 1)
                    elif stream in CE and i in marked[stream]:
                        ins.then_inc(sems[stream], 1)
            return body

        block.tensor(runner("pe"))
        block.scalar(runner("act"))
        block.vector(runner("dve"))
        block.gpsimd(runner("pool"))
        block.sync(runner("sp"))


RET_H = 6
GAM = [1.0 - 2.0 ** (-5.0 - h) for h in range(RET_H)]
SLOPES = [2.0 ** (-8.0 * (h + 1) / 6.0) for h in range(6)]


def lambda_init(layer):
    return 0.8 - 0.6 * math.exp(-0.3 * layer)


class CMap:
    def __init__(self):
        self.n = 0
        self.m = {}

    def add(self, name, w):
        self.m[name] = (self.n, w)
        self.n += w

    def sl(self, name):
        a, w = self.m[name]
        return slice(a, a + w)


def build_cmap():
    c = CMap()
    for l in range(DEPTH):
        c.add("anw%d" % l, 8)
        c.add("fnw%d" % l, 8)
        c.add("mqw%d" % l, 1)
        c.add("mkw%d" % l, 1)
    c.add("mnw", 8)
    for j in range(2):
        c.add("dqw%d" % j, 1)
        c.add("dkw%d" % j, 1)
        for h in range(6):
            c.add("gnw%d_%d" % (j, h), 1)
    for p in range(3):
        c.add("g128_%d" % p, 1)
    c.add("flag", 1)
    c.add("bown", 6 * 16)
    c.add("bprev", 6 * 32)
    c.add("identf", 128)
    c.add("Dm", 6 * 128)
    c.add("dectab", 384)
    c.add("qdtab", 3 * 512)
    c.add("subw", 2 * 128)
    c.add("lamv", 8 * 64)
    return c


CM = build_cmap()


def host_consts(inputs, half):
    c = np.zeros((128, CM.n), np.float32)
    p = np.arange(128)

    def colvec(v):
        return np.asarray(v, np.float32).reshape(8, 128).T

    for l in range(DEPTH):
        c[:, CM.sl("anw%d" % l)] = colvec(inputs["attn_norm_w"][l])
        c[:, CM.sl("fnw%d" % l)] = colvec(inputs["ffn_norm_w"][l])
        c[:, CM.sl("mqw%d" % l)] = inputs["mem_q_norm_w"][l][p % 64][:, None]
        c[:, CM.sl("mkw%d" % l)] = inputs["mem_k_norm_w"][l][p % 64][:, None]
    c[:, CM.sl("mnw")] = colvec(inputs["mem_norm_w"])
    for j in range(2):
        c[:, CM.sl("dqw%d" % j)] = inputs["diff_q_norm_w"][j][p % 64][:, None]
        c[:, CM.sl("dkw%d" % j)] = inputs["diff_k_norm_w"][j][p % 64][:, None]
        for h in range(6):
            c[:, CM.sl("gnw%d_%d" % (j, h))] = inputs["ret_gn_w"][j][h][:, None]
    g = np.array(GAM, np.float64)
    for pr in range(3):
        c[:, CM.sl("g128_%d" % pr)] = (g[2 * pr + p // 64] ** 128).astype(np.float32)[:, None]
    c[:, CM.sl("flag")] = float(half)
    bo = np.zeros((6, 16), np.float32)
    bp = np.zeros((6, 32), np.float32)
    for h in range(6):
        for m in range(-3, 13):
            bo[h, m + 3] = -SLOPES[h] * 128.0 * m
        for m in range(32):
            bp[h, m] = (-SLOPES[h] * 128.0 * m) if half == 1 else NEG
    c[:, CM.sl("bown")] = bo.reshape(1, -1)
    c[:, CM.sl("bprev")] = bp.reshape(1, -1)
    c[:, CM.sl("identf")] = np.eye(128, dtype=np.float32)
    j_ = np.arange(128)[:, None]
    i_ = np.arange(128)[None, :]
    Dm = np.zeros((128, 6, 128), np.float64)
    for h in range(6):
        Dm[:, h, :] = np.where((j_ // 64) <= (i_ // 64), g[h] ** np.abs(i_ - j_), 0.0) / 8.0
    c[:, CM.sl("Dm")] = Dm.reshape(128, -1).astype(np.float32)
    dec = np.zeros((128, 6, 64), np.float64)
    for h in range(6):
        dec[:, h, :] = (g[h] ** (127 - np.arange(128)))[:, None] / 8.0
    c[:, CM.sl("dectab")] = dec.reshape(128, -1).astype(np.float32)
    qd = np.zeros((128, 3, 512), np.float64)
    for pr in range(3):
        gg = g[2 * pr + p // 64]
        qd[:, pr, :] = gg[:, None] ** ((np.arange(512) % 128) + 1)[None, :]
    c[:, CM.sl("qdtab")] = qd.reshape(128, -1).astype(np.float32)
    sw = np.zeros((128, 2, 128), np.float32)
    for j in range(2):
        sw[:, j, :] = inputs["diff_subln_w"][j][None, :]
    c[:, CM.sl("subw")] = sw.reshape(128, -1)
    lv = np.zeros((128, 8, 64), np.float32)
    for j in range(2):
        for k, nm in enumerate(["diff_lambda_q1", "diff_lambda_k1", "diff_lambda_q2", "diff_lambda_k2"]):
            lv[:, j * 4 + k, :] = inputs[nm][j][None, :]
    c[:, CM.sl("lamv")] = lv.reshape(128, -1)
    return c


def host_tables():
    tb = np.zeros((128, 3, 128), np.float32)
    tb[:, 0, :] = 1.0
    tb[:, 1, :] = ((np.arange(128)[:, None] // 64) == (np.arange(128)[None, :] // 64)).astype(np.float32)
    tb[:, 2, :] = np.eye(128, dtype=np.float32)
    tb = tb.astype(ml_dtypes.bfloat16)
    j_ = np.arange(128)[:, None].astype(np.float64)
    i5 = np.arange(512)[None, :].astype(np.float64)
    i1 = np.arange(128)[None, :].astype(np.float64)
    T = np.zeros((6, 128, 512), np.float32)
    Tg = np.zeros((6, 128, 128), np.float32)
    for h in range(6):
        T[h] = (-SLOPES[h] * (i5 - j_)).astype(np.float32)
        allowed = (j_ // 64) <= (i1 // 64)
        Tg[h] = np.where(allowed, -SLOPES[h] * np.abs(i1 - j_), NEG).astype(np.float32)
    return tb, T, Tg


class Alloc:
    def __init__(self, nc, base, limit, tag):
        self.nc = nc
        self.off = base
        self.limit = limit
        self.tag = tag
        self.i = 0

    def t(self, name, shape, dt):
        esz = 2 if dt == BF16 else 4
        n = 1
        for s in shape[1:]:
            n *= s
        nb = (n * esz + 63) // 64 * 64
        assert self.off + nb <= self.limit, (name, self.off, nb, self.limit)
        h = self.nc.alloc_sbuf_tensor_at("%s_%s_%d" % (self.tag, name, self.i), list(shape), dt, offset=self.off)
        self.i += 1
        self.off += nb
        return h


SB_BASE = 16384 + 512
SB_LIMIT = 224 * 1024 - 512


def build_program(NL):
    nc = bass.Bass("TRN2", target_bir_lowering=False)
    P = Prog()
    dram = {}

    def din(name, shape, dt=F32):
        dram[name] = nc.dram_tensor(name, list(shape), dt, kind="ExternalInput")
        return dram[name].ap()

    x_d = din("x", [TOK, D])
    mem_d = din("mem", [256, D])
    w_in_d = din("w_in", [NL, D, 2560])
    w_out_d = din("w_out", [NL, D, D])
    w_mkv_d = din("w_mem_kv", [NL, D, 512])
    w_gu_d = din("w_gate_up", [NL, D, 2 * FFN])
    w_dn_d = din("w_down", [NL, FFN, D])
    cst_d = din("cst", [128, CM.n])
    tb_d = din("tb", [128, 3, 128], BF16)
    T_d = din("Ttab", [6, 128, 512])
    Tg_d = din("Tgtab", [6, 128, 128])
    out_d = nc.dram_tensor("out", [TOK, D], F32, kind="ExternalOutput").ap()

    qT_s = nc.dram_tensor("qT_s", [768, TOK], BF16, kind="ExternalOutput").ap()
    qdT_s = nc.dram_tensor("qdT_s", [384, TOK], BF16, kind="ExternalOutput").ap()
    kT_s2 = [nc.dram_tensor("kT_s%d" % i, [384, TOK], BF16).ap() for i in range(2)]
    kT_g2 = [nc.dram_tensor("kT_g%d" % i, [768, TOK], BF16).ap() for i in range(2)]
    v_s2 = [nc.dram_tensor("v_s%d" % i, [1024, 768], BF16).ap() for i in range(2)]
    v_g2 = [nc.dram_tensor("v_g%d" % i, [2048, 768], BF16).ap() for i in range(2)]

    def kT_rows(r0):
        return kT_s2[r0 // 384][r0 % 384:r0 % 384 + 128, :]

    def v_rows(tok0):
        return v_s2[tok0 // 1024][tok0 % 1024:tok0 % 1024 + 128, :]
    gT_s = nc.dram_tensor("gT_s", [768, TOK], BF16, kind="ExternalOutput").ap()
    kt_s = nc.dram_tensor("kt_s", [TOK, 384], BF16, kind="ExternalOutput").ap()
    st_l = nc.dram_tensor("st_l", [384, 256], F32).ap()
    st_g = nc.dram_tensor("st_g", [768, 256], F32).ap()
    NCORES = int(os.environ.get('KNC', '8'))
    RG = [[2 * i, 2 * i + 1] for i in range(NCORES // 2)]

    pa = Alloc(nc, SB_BASE, SB_LIMIT, "p")
    xT = pa.t("xT", [128, 8, TOK], F32)
    cst = pa.t("cst", [128, CM.n], F32)
    tbs = pa.t("tb", [128, 3, 128], BF16)
    memnT = pa.t("memnT", [128, 8, 256], BF16)
    ARENA = pa.off
    ones_b = tbs[:, 0, :]
    bd64_b = tbs[:, 1, :]
    id_b = tbs[:, 2, :]
    id_f = cst[:, CM.sl("identf")]
    B_xT = [[TB("xT%d_%d" % (kc, g)) for g in range(NG)] for kc in range(8)]
    B_cst = TB("cst")
    B_tb = TB("tb")
    B_memnT = TB("memnT")

    def C(name):
        return cst[:, CM.sl(name)]

    es = contextlib.ExitStack()
    psn = ["pA0", "pA1", "pA2", "pB", "pS0", "pS1", "pO0", "pO1"]
    ps = {n: es.enter_context(nc.psum_tensor(n, [128, 512], F32)) for n in psn}
    B_ps = {n: TB(n, x=True) for n in psn}
    rrA = [0]

    A_BANKS = [["pA0", "pA1", "pA2"]]

    def nextA():
        lst = A_BANKS[0]
        n = lst[rrA[0] % len(lst)]
        rrA[0] += 1
        return ps[n], B_ps[n]

    rot = {}

    def rr(key, n):
        v = rot.get(key, 0)
        rot[key] = v + 1
        return v % n

    def dma(stream, out, in_, reads, writes):
        return P.op(stream, lambda e: e.dma_start(out=out, in_=in_), reads=reads, writes=writes, dma=True)

    def mm(out, lhsT, rhs, start, stop, reads, writes, skip=False):
        if skip:
            return P.op("pe", lambda e: e.matmul(out, lhsT=lhsT, rhs=rhs, start=start, stop=stop, skip_group_check=True), reads=reads, writes=writes)
        return P.op("pe", lambda e: e.matmul(out, lhsT=lhsT, rhs=rhs, start=start, stop=stop), reads=reads, writes=writes)

    def act(out, in_, func, reads, writes, bias=0.0, scale=1.0, accum_out=None):
        if accum_out is None:
            return P.op("act", lambda e: e.activation(out=out, in_=in_, func=func, bias=bias, scale=scale), reads=reads, writes=writes)
        return P.op("act", lambda e: e.activation(out=out, in_=in_, func=func, bias=bias, scale=scale, accum_out=accum_out), reads=reads, writes=writes)

    def dve(fn, reads, writes):
        return P.op("dve", fn, reads=reads, writes=writes)

    def rstd_from(psum_ap, out_ap, n, reads, writes, eps=EPS, extra=1.0):
        act(out_ap, psum_ap, AF.Sqrt, reads, writes, bias=eps / (extra * extra), scale=1.0 / (n * extra * extra))
        dve(lambda e: e.reciprocal(out=out_ap, in_=out_ap), writes, writes)

    wbf_in = [nc.dram_tensor("wbf_in%d" % l, [D, 2560], BF16).ap() for l in range(NL)]
    wbf_mkv = [nc.dram_tensor("wbf_mkv%d" % l, [D, 512], BF16).ap() for l in range(NL)]
    wbf_out = [nc.dram_tensor("wbf_out%d" % l, [D, D], BF16).ap() for l in range(NL)]
    B_wbf = [{"in": [TB("wbi%d_%d" % (l, k)) for k in range(8)], "mkv": [TB("wbm%d_%d" % (l, k)) for k in range(2)], "out": [TB("wbo%d_%d" % (l, k)) for k in range(4)]} for l in range(NL)]
    def emit_precast(l):
        for kc in range(8):
            rsl = slice(kc * 128, (kc + 1) * 128)
            P.op("pool", lambda e, l=l, rsl=rsl: e.dma_start(out=wbf_in[l][rsl, :], in_=w_in_d[l, rsl, :]), reads=[], writes=[B_wbf[l]["in"][rsl.start // 128]], dma=True)
        for kc in range(0, 8, 4):
            rsl = slice(kc * 128, (kc + 4) * 128)
            P.op("pool", lambda e, l=l, rsl=rsl: e.dma_start(out=wbf_mkv[l][rsl, :], in_=w_mkv_d[l, rsl, :]), reads=[], writes=[B_wbf[l]["mkv"][rsl.start // 512]], dma=True)
        for kc in range(0, 8, 2):
            rsl = slice(kc * 128, (kc + 2) * 128)
            P.op("pool", lambda e, l=l, rsl=rsl: e.dma_start(out=wbf_out[l][rsl, :], in_=w_out_d[l, rsl, :]), reads=[], writes=[B_wbf[l]["out"][rsl.start // 256]], dma=True)

    dma("sp", cst[:], cst_d, [], [B_cst])
    dma("sp", tbs[:], tb_d, [], [B_tb])

    a0 = Alloc(nc, ARENA, SB_LIMIT, "ld")
    xin = [a0.t("xin%d" % i, [128, D], F32) for i in range(2)]
    B_xin = [TB("xin0"), TB("xin1")]
    memin = a0.t("memin", [128, 2, D], F32)
    B_memin = TB("memin")
    memT = a0.t("memT", [128, 8, 256], F32)
    B_memT = TB("memT")
    sqm = a0.t("sqm", [128, 256], BF16)
    B_sqm = TB("sqm")
    rsm = a0.t("rsm", [128, 256], F32)
    B_rsm = TB("rsm")
    for t in range(16):
        xi, bx = xin[t % 2], B_xin[t % 2]
        dma("sp", xi[:], x_d[t * 128:(t + 1) * 128, :], [], [bx])
        for hf in range(2):
            pt, bp = nextA()
            for k4 in range(4):
                kc = hf * 4 + k4
                P.op("pe", lambda e, pt=pt, xi=xi, k4=k4, kc=kc: e.transpose(pt[:, k4 * 128:(k4 + 1) * 128], xi[:, kc * 128:(kc + 1) * 128], id_f),
                     reads=[bx, B_cst], writes=[bp])
            g = t // 4
            wr = [B_xT[hf * 4 + k4][g] for k4 in range(4)]
            outap = xT[:, hf * 4:hf * 4 + 4, t * 128:(t + 1) * 128]
            inap = pt[:, :].rearrange("p (k n) -> p k n", k=4)
            if hf == 0:
                dve(lambda e, o=outap, i=inap: e.tensor_copy(out=o, in_=i), [bp], wr)
            else:
                P.op("act", lambda e, o=outap, i=inap: e.copy(out=o, in_=i), reads=[bp], writes=wr)
    dma("sp", memin[:], mem_d.rearrange("(t p) d -> p t d", p=128), [], [B_memin])
    for mt in range(2):
        for hf in range(2):
            pt, bp = nextA()
            for k4 in range(4):
                kc = hf * 4 + k4
                P.op("pe", lambda e, pt=pt, k4=k4, kc=kc, mt=mt: e.transpose(pt[:, k4 * 128:(k4 + 1) * 128], memin[:, mt, kc * 128:(kc + 1) * 128], id_f),
                     reads=[B_memin, B_cst], writes=[bp])
            dve(lambda e, pt=pt, hf=hf, mt=mt: e.tensor_copy(out=memT[:, hf * 4:hf * 4 + 4, mt * 128:(mt + 1) * 128], in_=pt[:, :].rearrange("p (k n) -> p k n", k=4)),
                [bp], [B_memT])
    for kc in range(8):
        act(sqm[:], memT[:, kc, :], AF.Square, [B_memT], [B_sqm])
        mm(ps["pB"][:, 0:256], ones_b, sqm[:], kc == 0, kc == 7, [B_sqm, B_tb], [B_ps["pB"]])
    rstd_from(ps["pB"][:, 0:256], rsm[:], D, [B_ps["pB"]], [B_rsm])
    for kc in range(8):
        dve(lambda e, kc=kc: e.scalar_tensor_tensor(out=memnT[:, kc, :], in0=memT[:, kc, :], scalar=C("mnw")[:, kc:kc + 1], in1=rsm[:], op0=ALU.mult, op1=ALU.mult),
            [B_memT, B_rsm, B_cst], [B_memnT])
    P.barrier()

    def norm_group(g, wname, hT, B_h, sq, B_sq, rs, B_rs):
        cols = slice(g * 512, (g + 1) * 512)
        for kc in range(8):
            s, bs = sq[kc % 2], B_sq[kc % 2]
            act(s[:], xT[:, kc, cols], AF.Square, [B_xT[kc][g]], [bs])
            mm(ps["pB"][:, :], ones_b, s[:], kc == 0, kc == 7, [bs, B_tb], [B_ps["pB"]])
        rstd_from(ps["pB"][:, :], rs[:], D, [B_ps["pB"]], [B_rs])
        for kc in range(8):
            dve(lambda e, kc=kc: e.scalar_tensor_tensor(out=hT[:, kc, :], in0=xT[:, kc, cols], scalar=C(wname)[:, kc:kc + 1], in1=rs[:], op0=ALU.mult, op1=ALU.mult),
                [B_xT[kc][g], B_rs, B_cst], [B_h])

    def fm_proj(w_ap_fn, M, hT, B_h, B_w, ncols=512, hcols=None):
        pt, bp = nextA()
        for kc in range(8):
            rhs = hT[:, kc, :] if hcols is None else hT[:, kc, hcols]
            mm(pt[0:M, 0:ncols], w_ap_fn(kc), rhs, kc == 0, kc == 7, [B_h, B_w], [bp])
        return pt, bp

    def qknorm_evac(pt, bp, N, wcol, out_ap, B_out, sql, B_sql, rsl, B_rsl):
        i_ = rr("qkn", len(sql))
        sq, B_sq, rs, B_rs = sql[i_], B_sql[i_], rsl[i_], B_rsl[i_]
        sb_ = ["pS0", "pS1"][rr("qknb", 2)]
        act(sq[:, 0:N], pt[:, 0:N], AF.Square, [bp], [B_sq])
        mm(ps[sb_][:, 0:N], bd64_b, sq[:, 0:N], True, True, [B_sq, B_tb], [B_ps[sb_]])
        rstd_from(ps[sb_][:, 0:N], rs[:, 0:N], 64, [B_ps[sb_]], [B_rs])
        dve(lambda e: e.scalar_tensor_tensor(out=out_ap, in0=pt[:, 0:N], scalar=wcol, in1=rs[:, 0:N], op0=ALU.mult, op1=ALU.mult),
            [bp, B_rs, B_cst], [B_out])

    def stt(out, in0, scalar, in1, op0, op1, reads, writes):
        return P.op("dve", lambda e: e.scalar_tensor_tensor(out=out, in0=in0, scalar=scalar, in1=in1, op0=op0, op1=op1), reads=reads, writes=writes)

    def tt(out, in0, in1, op, reads, writes):
        return P.op("dve", lambda e: e.tensor_tensor(out=out, in0=in0, in1=in1, op=op), reads=reads, writes=writes)

    def tsc(out, in0, scalar1, op0, reads, writes):
        return P.op("dve", lambda e: e.tensor_scalar(out=out, in0=in0, scalar1=scalar1, scalar2=None, op0=op0), reads=reads, writes=writes)

    def vcopy(out, in_, reads, writes):
        return P.op("dve", lambda e: e.tensor_copy(out=out, in_=in_), reads=reads, writes=writes)

    def vrecip(out, in_, reads, writes):
        return P.op("dve", lambda e: e.reciprocal(out=out, in_=in_), reads=reads, writes=writes)

    def acopy(out, in_, reads, writes):
        return P.op("act", lambda e: e.copy(out=out, in_=in_), reads=reads, writes=writes)

    def tpose(out, in_, ident, reads, writes):
        return P.op("pe", lambda e: e.transpose(out, in_, ident), reads=reads, writes=writes)

    def wslice(w, c0, M=128):
        return lambda kc: w[:, kc, c0:c0 + M]

    KSTOP = int(os.environ.get('KSTOP', '99'))
    for layer in range(NL):
        if KSTOP <= 0:
            break
        is_ret = (layer % 2 == 0)
        jj = layer // 2
        a1 = Alloc(nc, ARENA, SB_LIMIT, "ip%d" % layer)
        qcnT = a1.t("qcnT", [128, 2, TOK], BF16)
        B_qcn = [TB("qcn%d" % g) for g in range(NG)]
        mkT = a1.t("mkT", [128, 2, 256], BF16)
        B_mkT = TB("mkT")
        mv = a1.t("mv", [128, 2, 4, 65], BF16)
        B_mv = TB("mv")
        AR2 = a1.off
        w_in = a1.t("w_in", [128, 8, 2560], BF16)
        B_win = TB("w_in")
        w_mkv = a1.t("w_mkv", [128, 8, 512], BF16)
        B_wmkv = TB("w_mkv")
        hT = [a1.t("hT%d" % i, [128, 8, 512], BF16) for i in range(NG)]
        B_hT = [TB("hT%d" % i) for i in range(NG)]
        sq = [a1.t("sq%d" % i, [128, 512], BF16) for i in range(2)]
        B_sq = [TB("sq0"), TB("sq1")]
        rs = a1.t("rs", [128, 512], F32)
        B_rs = TB("rs")
        sq2 = [a1.t("sq2_%d" % i, [128, 512], BF16) for i in range(3)]
        B_sq2 = [TB("sq2_%d" % i) for i in range(3)]
        rs2 = [a1.t("rs2_%d" % i, [128, 512], F32) for i in range(3)]
        B_rs2 = [TB("rs2_%d" % i) for i in range(3)]
        A_BANKS[0] = ["pA0", "pA1", "pA2", "pO0", "pO1"]
        NST = 4
        stg = [a1.t("stg%d" % i, [128, 512], BF16) for i in range(NST)]
        B_stg = [TB("stg%d" % i) for i in range(NST)]
        stk = [a1.t("stk%d" % i, [128, 1152], BF16) for i in range(2)]
        B_stk = [TB("stk%d" % i) for i in range(2)]

        for kc in range(8):
            for q2 in range(2):
                if layer == 0:
                    dma("pool", w_in[:, kc, q2 * 1280:(q2 + 1) * 1280], w_in_d[layer, kc * 128:(kc + 1) * 128, q2 * 1280:(q2 + 1) * 1280], [], [B_win])
                else:
                    dma("sp", w_in[:, kc, q2 * 1280:(q2 + 1) * 1280], wbf_in[layer][kc * 128:(kc + 1) * 128, q2 * 1280:(q2 + 1) * 1280], [B_wbf[layer]["in"][kc]], [B_win])
            if layer == 0:
                dma("pool", w_mkv[:, kc, :], w_mkv_d[layer, kc * 128:(kc + 1) * 128, :], [], [B_wmkv])
            else:
                dma("sp", w_mkv[:, kc, :], wbf_mkv[layer][kc * 128:(kc + 1) * 128, :], [B_wbf[layer]["mkv"][kc // 4]], [B_wmkv])

        P.op("pool", lambda e, mv=mv: e.memset(mv[:], 1.0), reads=[], writes=[B_mv])
        for c2 in range(2):
            pt, bp = fm_proj(wslice(w_mkv, c2 * 128), 128, memnT, B_memnT, B_wmkv, ncols=256)
            qknorm_evac(pt, bp, 256, C("mkw%d" % layer), mkT[:, c2, :], B_mkT, sq2, B_sq2, rs2, B_rs2)
        for mt in range(2):
            pt, bp = nextA()
            for kc in range(8):
                mm(pt[:, 0:256], memnT[:, kc, mt * 128:(mt + 1) * 128], w_mkv[:, kc, 256:512], kc == 0, kc == 7, [B_memnT, B_wmkv], [bp])
            acopy(mv[:, mt, :, 0:64], pt[:, 0:256].rearrange("p (h d) -> p h d", h=4), [bp], [B_mv])

        B_scr = {n: TB(n) for n in ["qT", "qdT", "kT", "gT", "kt", "v"]}
        scr_lists = {n: [] for n in B_scr}

        def scr_w(name):
            t_ = TB(name)
            scr_lists[name].append(t_)
            return t_

        def stage_out(pt_ap, bp, dst_ap, kind, scr, mul_ap=None):
            if os.environ.get('KNOSTAGE') == '1':
                return
            if os.environ.get('KONLY') and os.environ.get('KONLY') != kind:
                return
            i = rr("stg", NST)
            st, bst = stg[i], B_stg[i]
            if kind == "copy":
                acopy(st[:], pt_ap, [bp], [bst])
            elif kind == "silu":
                act(st[:], pt_ap, AF.Silu, [bp], [bst])
            elif kind == "mul":
                tt(st[:], pt_ap, mul_ap, ALU.mult, [bp, B_cst], [bst])
            if os.environ.get('KNODMA') != '1':
                dma("sp", dst_ap, st[:], [bst], [scr_w(scr)])

        def do_q_diff(g, h_, bh):
            cols = slice(g * 512, (g + 1) * 512)
            for hh in range(6):
                pt, bp = fm_proj(wslice(w_in, hh * 128), 128, h_, bh, B_win)
                i = rr("stg", NST)
                qknorm_evac(pt, bp, 512, C("dqw%d" % jj), stg[i][:], B_stg[i], sq2, B_sq2, rs2, B_rs2)
                dma("sp", qT_s[hh * 128:(hh + 1) * 128, cols], stg[i][:], [B_stg[i]], [scr_w("qT")])

        def do_k_diff(g, h_, bh):
            cols = slice(g * 512, (g + 1) * 512)
            for hh in range(6):
                pt, bp = fm_proj(wslice(w_in, 768 + hh * 128), 128, h_, bh, B_win)
                i = rr("stg", NST)
                qknorm_evac(pt, bp, 512, C("dkw%d" % jj), stg[i][:], B_stg[i], sq2, B_sq2, rs2, B_rs2)
                dma("sp", kT_rows(hh * 128)[:, cols], stg[i][:], [B_stg[i]], [scr_w("kT")])

        def do_ret_fm(g, h_, bh):
            cols = slice(g * 512, (g + 1) * 512)
            for pr in range(3):
                pt, bp = fm_proj(wslice(w_in, pr * 128), 128, h_, bh, B_win)
                stage_out(pt[:, :], bp, qT_s[pr * 128:(pr + 1) * 128, cols], "copy", "qT")
                stage_out(pt[:, :], bp, qdT_s[pr * 128:(pr + 1) * 128, cols], "mul", "qdT", mul_ap=C("qdtab")[:, pr * 512:(pr + 1) * 512])
            for pr in range(3):
                pt, bp = fm_proj(wslice(w_in, 384 + pr * 128), 128, h_, bh, B_win)
                stage_out(pt[:, :], bp, kT_rows(pr * 128)[:, cols], "copy", "kT")
            for hh in range(6):
                pt, bp = fm_proj(wslice(w_in, 1536 + hh * 128), 128, h_, bh, B_win)
                stage_out(pt[:, :], bp, gT_s[hh * 128:(hh + 1) * 128, cols], "silu", "gT")

        def do_qc(g, h_, bh):
            cols = slice(g * 512, (g + 1) * 512)
            for c2 in range(2):
                pt, bp = fm_proj(wslice(w_in, 2304 + c2 * 128), 128, h_, bh, B_win)
                qknorm_evac(pt, bp, 512, C("mqw%d" % layer), qcnT[:, c2, cols], B_qcn[g], sq2, B_sq2, rs2, B_rs2)

        def do_tok(g, h_, bh):
            for t4 in range(4):
                tok0 = g * 512 + t4 * 128
                i = rr("stk", 2)
                sk, bsk = stk[i], B_stk[i]
                nparts = 3 if is_ret else 2
                cbase = 384 if is_ret else 1536
                for part in range(nparts):
                    pt, bp = nextA()
                    c0 = cbase + part * 384
                    for kc in range(8):
                        mm(pt[:, 0:384], h_[:, kc, t4 * 128:(t4 + 1) * 128], w_in[:, kc, c0:c0 + 384], kc == 0, kc == 7, [bh, B_win], [bp])
                    if is_ret and part == 0:
                        tt(sk[:, 0:384], pt[:, 0:384], C("dectab"), ALU.mult, [bp, B_cst], [bsk])
                    elif part % 2 == 0:
                        vcopy(sk[:, part * 384:(part + 1) * 384], pt[:, 0:384], [bp], [bsk])
                    else:
                        acopy(sk[:, part * 384:(part + 1) * 384], pt[:, 0:384], [bp], [bsk])
                if is_ret:
                    dma("sp", kt_s[tok0:tok0 + 128, :], sk[:, 0:384], [bsk], [scr_w("kt")])
                    dma("sp", v_rows(tok0), sk[:, 384:1152], [bsk], [scr_w("v")])
                else:
                    dma("sp", v_rows(tok0), sk[:, 0:768], [bsk], [scr_w("v")])

        B_kTg = TB("kT_g")
        B_vg = TB("v_g")
        if is_ret:
            for g in range(NG):
                h_, bh = hT[g], B_hT[g]
                norm_group(g, "anw%d" % layer, h_, bh, sq, B_sq, rs, B_rs)
                do_ret_fm(g, h_, bh)
                do_qc(g, h_, bh)
                do_tok(g, h_, bh)
        else:
            for g in range(NG):
                h_, bh = hT[g], B_hT[g]
                norm_group(g, "anw%d" % layer, h_, bh, sq, B_sq, rs, B_rs)
                do_k_diff(g, h_, bh)
                do_tok(g, h_, bh)
            for i2 in range(2):
                P.op("pool", lambda e, i2=i2: e.collective_compute("AllGather", ALU.bypass, replica_groups=RG, ins=[kT_s2[i2]], outs=[kT_g2[i2]]),
                     reads=list(scr_lists["kT"]), writes=[B_kTg], cc=True)
            for i2 in range(2):
                P.op("pool", lambda e, i2=i2: e.collective_compute("AllGather", ALU.bypass, replica_groups=RG, ins=[v_s2[i2]], outs=[v_g2[i2]]),
                     reads=list(scr_lists["v"]), writes=[B_vg], cc=True)
            for g in range(NG):
                h_, bh = hT[g], B_hT[g]
                do_q_diff(g, h_, bh)
                do_qc(g, h_, bh)
        P.barrier()

        if KSTOP <= 1:
            break
        a2 = Alloc(nc, AR2, SB_LIMIT, "mx%d" % layer)
        A_BANKS[0] = ["pA0", "pA1", "pA2"]
        ocT = a2.t("ocT", [128, 8, TOK], BF16)
        B_oc = [[TB("oc%d_%d" % (c, g)) for g in range(NG)] for c in range(8)]
        w_out = a2.t("w_out", [128, 8, D], BF16)
        B_wout = TB("w_out")
        if layer == 0:
            dma("pool", w_out[:], w_out_d[layer].rearrange("(kc p) n -> p kc n", p=128), [], [B_wout])
        else:
            dma("sp", w_out[:], wbf_out[layer].rearrange("(kc p) n -> p kc n", p=128), list(B_wbf[layer]["out"]), [B_wout])
        if layer + 1 < NL:
            emit_precast(layer + 1)
        pm = [a2.t("pm%d" % i, [128, 512], BF16) for i in range(2)]
        B_pm = [TB("pm0"), TB("pm1")]
        omall = a2.t("omall", [128, 4, 256], BF16)
        B_omall = TB("omall")
        rl = a2.t("rl", [128, 8], F32)
        B_rl = TB("rl")
        pT_b = ps["pB"][:, :].bitcast(BF16)
        def emit_memattn():
            for g in range(NG):
                cols = slice(g * 512, (g + 1) * 512)
                for hd in range(4):
                    c2, hh = hd // 2, hd % 2
                    pacc = ps["pO%d" % (hd % 2)]
                    bacc = B_ps["pO%d" % (hd % 2)]
                    for mt in range(2):
                        sn = "pS%d" % rr("pS", 2)
                        mm(ps[sn][:, :], mkT[hh * 64:(hh + 1) * 64, c2, mt * 128:(mt + 1) * 128], qcnT[hh * 64:(hh + 1) * 64, c2, cols], True, True,
                           [B_mkT, B_qcn[g]], [B_ps[sn]])
                        i = rr("pm", 2)
                        act(pm[i][:], ps[sn][:, :], AF.Exp, [B_ps[sn]], [B_pm[i]], scale=0.125)
                        for sub in range(4):
                            mm(pacc[:, sub * 65:(sub + 1) * 65], pm[i][:, sub * 128:(sub + 1) * 128], mv[:, mt, hd, :], mt == 0 and sub == 0, mt == 1, [B_pm[i], B_mv], [bacc], skip=True)
                    acc3 = pacc[:, 0:260].rearrange("p (s e) -> p s e", s=4)
                    vrecip(rl[:, 0:4], acc3[:, :, 64], [bacc], [B_rl])
                    for sub in range(4):
                        tsc(omall[:, sub, hd * 64:(hd + 1) * 64], pacc[:, sub * 65:sub * 65 + 64], rl[:, sub:sub + 1], ALU.mult, [bacc, B_rl], [B_omall])
                for c2 in range(2):
                    for sub in range(4):
                        tpose(pT_b[:, (c2 * 4 + sub) * 128:(c2 * 4 + sub + 1) * 128], omall[:, sub, c2 * 128:(c2 + 1) * 128], id_b, [B_omall, B_tb], [B_ps["pB"]])
                for c2 in range(2):
                    vcopy(ocT[:, 6 + c2, cols], pT_b[:, c2 * 512:(c2 + 1) * 512], [B_ps["pB"]], [B_oc[6 + c2][g]])


        if not is_ret:
            emit_memattn()

        if KSTOP <= 2:
            break
        if is_ret:
            qTp = a2.t("qTp", [128, TOK], BF16)
            qdTp = a2.t("qdTp", [128, TOK], BF16)
            kTp = a2.t("kTp", [128, TOK], BF16)
            ktp = a2.t("ktp", [128, 16, 128], BF16)
            vtp = a2.t("vtp", [128, 16, 256], BF16)
            gTp = a2.t("gTp", [128, 2, TOK], BF16)
            Sst = a2.t("Sst", [128, 256], F32)
            S0 = a2.t("S0", [128, 3, 256], F32)
            Sball = a2.t("Sball", [128, 16, 256], BF16)
            Aa2 = [[a2.t("A%d_%d" % (hh, i), [128, 128], BF16) for i in range(2)] for hh in range(2)]
            sqo2 = [a2.t("sqo%d" % i, [128, 512], BF16) for i in range(2)]
            rso2 = [a2.t("rso%d" % i, [128, 512], F32) for i in range(2)]
            t12 = [a2.t("t1_%d" % i, [128, 512], F32) for i in range(2)]
            B_sqo = [TB("sqo0"), TB("sqo1")]
            B_rso = [TB("rso0"), TB("rso1")]
            B_t1 = [TB("t1_0"), TB("t1_1")]
            Bn = {n: TB(n) for n in ["qTp", "qdTp", "kTp", "ktp", "vtp", "gTp", "S", "S0", "Sb", "A0", "A1", "sqo", "rso", "t1", "st_l", "st_g"]}
            kt_v = kt_s.rearrange("(t j) c -> j t c", j=128)
            v_v2 = [v.rearrange("(t j) c -> j t c", j=128) for v in v_s2]

            def load_vtp(pr):
                for i2 in range(2):
                    dma("sp", vtp[:, i2 * 8:(i2 + 1) * 8, :], v_v2[i2][:, :, pr * 256:(pr + 1) * 256], [B_scr["v"]], [Bn["vtp"]])

            def state_step(pr, b, first):
                sb_ = "pS1" if b % 2 == 0 else "pA2"
                mm(ps[sb_][:, 0:256], ktp[:, b, :], vtp[:, b, :], True, True, [Bn["ktp"], Bn["vtp"]], [B_ps[sb_]])
                if first:
                    vcopy(Sst[:], ps[sb_][:, 0:256], [B_ps[sb_]], [Bn["S"]])
                else:
                    stt(Sst[:], Sst[:], C("g128_%d" % pr), ps[sb_][:, 0:256], ALU.mult, ALU.add, [Bn["S"], B_ps[sb_], B_cst], [Bn["S"]])

            for pr in range(3):
                dma("sp", ktp[:], kt_v[:, :, pr * 128:(pr + 1) * 128], [B_scr["kt"]], [Bn["ktp"]])
                load_vtp(pr)
                for b in range(16):
                    state_step(pr, b, b == 0)
                dma("sp", st_l[pr * 128:(pr + 1) * 128, :], Sst[:], [Bn["S"]], [Bn["st_l"]])
            P.op("pool", lambda e: e.collective_compute("AllGather", ALU.bypass, replica_groups=RG, ins=[st_l], outs=[st_g]),
                 reads=[Bn["st_l"]], writes=[Bn["st_g"]], cc=True)
            dma("pool", S0[:], st_g[0:384, :].rearrange("(p r) c -> r p c", r=128), [Bn["st_g"]], [Bn["S0"]])
            emit_memattn()
            for pr in range(3):
                dma("sp", qTp[:], qT_s[pr * 128:(pr + 1) * 128, :], [B_scr["qT"]], [Bn["qTp"]])
                dma("sp", qdTp[:], qdT_s[pr * 128:(pr + 1) * 128, :], [B_scr["qdT"]], [Bn["qdTp"]])
                dma("sp", kTp[:], kT_rows(pr * 128), [B_scr["kT"]], [Bn["kTp"]])
                dma("sp", ktp[:], kt_v[:, :, pr * 128:(pr + 1) * 128], [B_scr["kt"]], [Bn["ktp"]])
                load_vtp(pr)
                dma("sp", gTp[:], gT_s[pr * 256:(pr + 1) * 256, :].rearrange("(h p) n -> p h n", p=128), [B_scr["gT"]], [Bn["gTp"]])
                tsc(Sst[:], S0[:, pr, :], C("flag"), ALU.mult, [Bn["S0"], B_cst], [Bn["S"]])
                B_Sb = [TB("Sb%d" % b) for b in range(16)]
                B_A = [[TB("A%d_%d" % (hh, i)) for i in range(2)] for hh in range(2)]
                acopy(Sball[:, 0, :], Sst[:], [Bn["S"]], [B_Sb[0]])

                def rfront(b, pr=pr):
                    cb = slice(b * 128, (b + 1) * 128)
                    for hh in range(2):
                        h = 2 * pr + hh
                        rws = slice(hh * 64, (hh + 1) * 64)
                        psc = ps["pS0"][:, hh * 128:(hh + 1) * 128]
                        mm(psc, kTp[rws, cb], qTp[rws, cb], True, True, [Bn["kTp"], Bn["qTp"]], [B_ps["pS0"]])
                        tt(Aa2[hh][b % 2][:], psc, C("Dm")[:, h * 128:(h + 1) * 128], ALU.mult, [B_ps["pS0"], B_cst], [B_A[hh][b % 2]])

                rfront(0)
                for b in range(16):
                    cb = slice(b * 128, (b + 1) * 128)
                    if b + 1 < 16:
                        rfront(b + 1)
                    g = b // 4
                    obank = ["pO0", "pO1"] if g % 2 == 0 else ["pA0", "pA1"]
                    for hh in range(2):
                        rws = slice(hh * 64, (hh + 1) * 64)
                        po = ps[obank[hh]][:, (b % 4) * 128:(b % 4 + 1) * 128]
                        mm(po, vtp[:, b, hh * 128:(hh + 1) * 128], Aa2[hh][b % 2][:], True, False, [Bn["vtp"], B_A[hh][b % 2]], [B_ps[obank[hh]]])
                        mm(po, Sball[rws, b, hh * 128:(hh + 1) * 128], qdTp[rws, cb], False, True, [B_Sb[b], Bn["qdTp"]], [B_ps[obank[hh]]])
                    state_step(pr, b, False)
                    if b + 1 < 16:
                        acopy(Sball[:, b + 1, :], Sst[:], [Bn["S"]], [B_Sb[b + 1]])
                    if b % 4 == 3:
                        cols = slice(g * 512, (g + 1) * 512)
                        for hh in range(2):
                            h = 2 * pr + hh
                            pO, bO = ps[obank[hh]], B_ps[obank[hh]]
                            sbk = "pB"
                            act(sqo2[hh][:], pO[:, :], AF.Square, [bO], [B_sqo[hh]])
                            mm(ps[sbk][:, :], ones_b, sqo2[hh][:], True, True, [B_sqo[hh], B_tb], [B_ps[sbk]])
                            rstd_from(ps[sbk][:, :], rso2[hh][:], 128, [B_ps[sbk]], [B_rso[hh]])
                            stt(t12[hh][:], pO[:, :], C("gnw%d_%d" % (jj, h)), rso2[hh][:], ALU.mult, ALU.mult, [bO, B_rso[hh], B_cst], [B_t1[hh]])
                            tt(ocT[:, h, cols], t12[hh][:], gTp[:, hh, cols], ALU.mult, [B_t1[hh], Bn["gTp"]], [B_oc[h][g]])
        else:
            li = lambda_init(layer)
            qh = a2.t("qh", [128, TOK], BF16)
            ko2 = [a2.t("ko%d" % i, [128, TOK], BF16) for i in range(2)]
            kp2 = [a2.t("kp%d" % i, [128, TOK], BF16) for i in range(2)]
            vo2 = [a2.t("vo%d" % i, [128, 16, 129], BF16) for i in range(2)]
            vp2 = [a2.t("vp%d" % i, [128, 16, 129], BF16) for i in range(2)]
            Th = a2.t("Th", [128, 512], F32)
            Tg = a2.t("Tg", [128, 128], F32)
            NTMP = 5
            SBOUND = 16.0
            FAR = [(104.0 + 2.0 * SBOUND) / sl for sl in SLOPES]
            SCB = ["pS0", "pS1", "pA0", "pA1"]
            tmp = [a2.t("tmp%d" % i, [128, 512], F32) for i in range(NTMP)]
            Pt = [a2.t("Pt%d" % i, [128, 512], BF16) for i in range(NTMP)]
            o0 = a2.t("o0", [128, 4, 128], F32)
            od = a2.t("od", [128, 128], F32)
            junk = a2.t("junk", [128, 128], F32)
            otok = a2.t("otok", [128, 4, 128], BF16)
            sm = a2.t("sm", [128, 16], F32)
            lt = a2.t("lt", [128, 64], F32)
            Bn = {n: TB(n) for n in ["qh", "ko", "kp", "vo", "vp", "Th", "Tg", "o0", "od", "junk", "otok", "sm", "lam", "lt"]}
            B_tmp = [TB("tmp%d" % i) for i in range(NTMP)]
            B_Pt = [TB("Pt%d" % i) for i in range(NTMP)]
            v_v2 = [v.rearrange("(t j) c -> j t c", j=128) for v in v_s2]
            vg_v2 = [v[0:1024, :].rearrange("(t j) c -> j t c", j=128) for v in v_g2]
            BnKV = [{n: TB(n + str(i)) for n in ["ko", "kp", "vo", "vp"]} for i in range(2)]
            for i in range(2):
                P.op("pool", lambda e, t_=vo2[i]: e.memset(t_[:], 1.0), reads=[], writes=[BnKV[i]["vo"]])
                P.op("pool", lambda e, t_=vp2[i]: e.memset(t_[:], 1.0), reads=[], writes=[BnKV[i]["vp"]])
            lv = C("lamv")
            for k in range(2):
                tt(lt[:], lv[:, (jj * 4 + 2 * k) * 64:(jj * 4 + 2 * k + 1) * 64], lv[:, (jj * 4 + 2 * k + 1) * 64:(jj * 4 + 2 * k + 2) * 64], ALU.mult, [B_cst], [Bn["lt"]])
                P.op("dve", lambda e, k=k: e.reduce_sum(out=sm[:, k:k + 1], in_=lt[:], axis=mybir.AxisListType.X), reads=[Bn["lt"]], writes=[Bn["sm"]])
            act(sm[:, 2:4], sm[:, 0:2], AF.Exp, [Bn["sm"]], [Bn["sm"]])
            tt(sm[:, 4:5], sm[:, 3:4], sm[:, 2:3], ALU.subtract, [Bn["sm"]], [Bn["sm"]])
            tsc(sm[:, 4:5], sm[:, 4:5], -li, ALU.add, [Bn["sm"]], [Bn["sm"]])
            neglam = sm[:, 4:5]
            accs = [(ps["pO0"], B_ps["pO0"], 0), (ps["pO0"], B_ps["pO0"], 256), (ps["pO1"], B_ps["pO1"], 0), (ps["pO1"], B_ps["pO1"], 256)]
            for h in range(6):
                rws_h = slice(h * 128, (h + 1) * 128)
                ko, kp, vo, vp = ko2[h % 2], kp2[h % 2], vo2[h % 2], vp2[h % 2]
                Bn.update(BnKV[h % 2])
                dma("sp", qh[:], qT_s[rws_h, :], [B_scr["qT"]], [Bn["qh"]])
                dma("sp", ko[:], kT_rows(h * 128), [B_scr["kT"]], [Bn["ko"]])
                dma("sp", kp[:], kT_g2[h // 3][(h % 3) * 128:(h % 3 + 1) * 128, :], [B_kTg], [Bn["kp"]])
                for i2 in range(2):
                    dma("sp", vo[:, i2 * 8:(i2 + 1) * 8, 0:128], v_v2[i2][:, :, rws_h], [B_scr["v"]], [Bn["vo"]])
                    dma("sp", vp[:, i2 * 8:(i2 + 1) * 8, 0:128], vg_v2[i2][:, :, rws_h], [B_vg], [Bn["vp"]])
                dma("sp", Th[:], T_d[h], [], [Bn["Th"]])
                dma("sp", Tg[:], Tg_d[h], [], [Bn["Tg"]])
                units = []
                for g in range(NG):
                    for c in range(2):
                        blocks = [("p", kb) for kb in range(16)] + [("o", kb) for kb in range(4 * g + 4)]
                        def _dist(src, kb, g=g):
                            return (2048 if src == "p" else 0) + 512 * g - (128 * kb + 127)
                        blocks = [(src, kb) for (src, kb) in blocks if _dist(src, kb) < FAR[h]]
                        for bi, (src, kb) in enumerate(blocks):
                            units.append(dict(g=g, c=c, src=src, kb=kb, bi=bi, nb=len(blocks)))

                def front(u, h=h):
                    g, c, src, kb = u["g"], u["c"], u["src"], u["kb"]
                    rws = slice(c * 64, (c + 1) * 64)
                    kk, Bk = (kp, Bn["kp"]) if src == "p" else (ko, Bn["ko"])
                    r = kb - 4 * g if src == "o" else -1
                    c_lo = 128 * r if r > 0 else 0
                    sn = SCB[rr("pS", len(SCB))]
                    psc, bsc = ps[sn], B_ps[sn]
                    mm(psc[:, c_lo:512], kk[rws, kb * 128:(kb + 1) * 128], qh[rws, g * 512 + c_lo:(g + 1) * 512], True, True, [Bk, Bn["qh"]], [bsc])
                    i = rr("tmp", NTMP)
                    tm, btm, pt_, bpt = tmp[i], B_tmp[i], Pt[i], B_Pt[i]
                    u["pt"] = (pt_, bpt)
                    u["r"] = r
                    if r < 0:
                        m = (16 + 4 * g - kb) if src == "p" else (4 * g - kb)
                        bcol = C("bprev")[:, h * 32 + m:h * 32 + m + 1] if src == "p" else C("bown")[:, h * 16 + m + 3:h * 16 + m + 4]
                        stt(tm[:], psc[:, :], 0.125, Th[:], ALU.mult, ALU.add, [bsc, Bn["Th"]], [btm])
                        act(pt_[:], tm[:], AF.Exp, [btm, B_cst], [bpt], bias=bcol)
                    else:
                        d0 = 128 * r
                        stt(tm[:, d0:d0 + 128], psc[:, d0:d0 + 128], 0.125, Tg[:], ALU.mult, ALU.add, [bsc, Bn["Tg"]], [btm])
                        if r < 3:
                            stt(tm[:, d0 + 128:512], psc[:, d0 + 128:512], 0.125, Th[:, d0 + 128:512], ALU.mult, ALU.add, [bsc, Bn["Th"]], [btm])
                        act(pt_[:, d0:d0 + 128], tm[:, d0:d0 + 128], AF.Exp, [btm], [bpt])
                        if r < 3:
                            bcol = C("bown")[:, h * 16 + (-r) + 3:h * 16 + (-r) + 4]
                            act(pt_[:, d0 + 128:512], tm[:, d0 + 128:512], AF.Exp, [btm, B_cst], [bpt], bias=bcol)

                def back(u, h=h):
                    g, c, src, kb, bi, r = u["g"], u["c"], u["src"], u["kb"], u["bi"], u["r"]
                    pt_, bpt = u["pt"]
                    vv, Bv = (vp, Bn["vp"]) if src == "p" else (vo, Bn["vo"])
                    for sub in range(4):
                        if r >= 0 and sub < r:
                            continue
                        pa_, ba_, off = accs[sub]
                        last = (src == "o" and kb == 4 * g + sub)
                        mm(pa_[:, off:off + 129], pt_[:, sub * 128:(sub + 1) * 128], vv[:, kb, :], bi == 0 and off == 0, last, [bpt, Bv], [ba_], skip=True)
                    if bi == u["nb"] - 1:
                        finalize(g, c, h)

                def finalize(g, c, h):
                    cols = slice(g * 512, (g + 1) * 512)
                    for sub in range(4):
                        pa_, ba_, off = accs[sub]
                        vrecip(sm[:, 8 + sub:9 + sub], pa_[:, off + 128:off + 129], [ba_], [Bn["sm"]])
                        if c == 0:
                            tsc(o0[:, sub, :], pa_[:, off:off + 128], sm[:, 8 + sub:9 + sub], ALU.mult, [ba_, Bn["sm"]], [Bn["o0"]])
                        else:
                            tt(sm[:, 12 + sub:13 + sub], sm[:, 8 + sub:9 + sub], neglam, ALU.mult, [Bn["sm"]], [Bn["sm"]])
                            stt(od[:], pa_[:, off:off + 128], sm[:, 12 + sub:13 + sub], o0[:, sub, :], ALU.mult, ALU.add, [ba_, Bn["sm"], Bn["o0"]], [Bn["od"]])
                            act(junk[:], od[:], AF.Square, [Bn["od"]], [Bn["junk"], Bn["sm"]], accum_out=sm[:, 5:6])
                            f = 1.0 - li
                            rstd_from(sm[:, 5:6], sm[:, 6:7], 128, [Bn["sm"]], [Bn["sm"]], extra=f)
                            stt(otok[:, sub, :], od[:], sm[:, 6:7], C("subw")[:, jj * 128:(jj + 1) * 128], ALU.mult, ALU.mult, [Bn["od"], Bn["sm"], B_cst], [Bn["otok"]])
                    if c == 1:
                        for sub in range(4):
                            tpose(pT_b[:, sub * 128:(sub + 1) * 128], otok[:, sub, :], id_b, [Bn["otok"], B_tb], [B_ps["pB"]])
                        acopy(ocT[:, h, cols], pT_b[:, 0:512], [B_ps["pB"]], [B_oc[h][g]])

                LOOK = 3
                for idx in range(len(units) + LOOK):
                    if idx < len(units):
                        front(units[idx])
                    if idx - LOOK >= 0:
                        back(units[idx - LOOK])

        if KSTOP <= 3:
            break
        for g in range(NG):
            cols = slice(g * 512, (g + 1) * 512)
            for oc in range(8):
                pt, bp = nextA()
                for kc in range(8):
                    mm(pt[:, :], w_out[:, kc, oc * 128:(oc + 1) * 128], ocT[:, kc, cols], kc == 0, kc == 7, [B_wout, B_oc[kc][g]], [bp])
                tt(xT[:, oc, cols], xT[:, oc, cols], pt[:, :], ALU.add, [bp, B_xT[oc][g]], [B_xT[oc][g]])
        P.barrier()

        if KSTOP <= 4:
            break
        a3 = Alloc(nc, ARENA, SB_LIMIT, "ff%d" % layer)
        A_BANKS[0] = ["pA0", "pA1", "pA2", "pS0", "pS1", "pO0", "pO1"]
        hT2 = [a3.t("hT%d" % i, [128, 8, 512], BF16) for i in range(2)]
        B_hT2 = [TB("hT0"), TB("hT1")]
        sq = [a3.t("sq%d" % i, [128, 512], BF16) for i in range(2)]
        B_sq = [TB("sq0"), TB("sq1")]
        rs = a3.t("rs", [128, 512], F32)
        B_rs = TB("rs")
        actT = a3.t("actT", [128, NFC, 1024], BF16)
        B_act = [TB("act0"), TB("act1")]
        NWG = 4
        wgu = [a3.t("wgu%d" % i, [128, 8, 2, 128], BF16) for i in range(NWG)]
        B_wgu = [TB("wgu%d" % i) for i in range(NWG)]
        wd = [a3.t("wd%d" % i, [128, NFC, 128], BF16) for i in range(2)]
        B_wd = [TB("wd0"), TB("wd1")]
        sg = [a3.t("sg%d" % i, [128, 512], BF16) for i in range(2)]
        B_sg = [TB("sg0"), TB("sg1")]
        wgu_v = w_gu_d[layer].rearrange("(kc p) n -> p kc n", p=128)
        wd_v = w_dn_d[layer].rearrange("(fc p) n -> p fc n", p=128)
        for hf in range(2):
            for gi in range(2):
                norm_group(2 * hf + gi, "fnw%d" % layer, hT2[gi], B_hT2[gi], sq, B_sq, rs, B_rs)
            for fc in range(NFC):
                i = rr("wgu", NWG)
                dma("pool", wgu[i][:, :, 0, :], wgu_v[:, :, fc * 128:(fc + 1) * 128], [], [B_wgu[i]])
                dma("pool", wgu[i][:, :, 1, :], wgu_v[:, :, FFN + fc * 128:FFN + (fc + 1) * 128], [], [B_wgu[i]])
                for gi in range(2):
                    pg, bg = fm_proj(lambda kc, i=i: wgu[i][:, kc, 0, :], 128, hT2[gi], B_hT2[gi], B_wgu[i])
                    pu, bu = fm_proj(lambda kc, i=i: wgu[i][:, kc, 1, :], 128, hT2[gi], B_hT2[gi], B_wgu[i])
                    k = rr("sg", 2)
                    act(sg[k][:], pg[:, :], AF.Silu, [bg], [B_sg[k]])
                    tt(actT[:, fc, gi * 512:(gi + 1) * 512], pu[:, :], sg[k][:], ALU.mult, [bu, B_sg[k]], [B_act[gi]])
            for oc in range(8):
                i = rr("wd", 2)
                dma("pool", wd[i][:], wd_v[:, :, oc * 128:(oc + 1) * 128], [], [B_wd[i]])
                for gi in range(2):
                    g = 2 * hf + gi
                    cols = slice(g * 512, (g + 1) * 512)
                    pt, bp = nextA()
                    for fc in range(NFC):
                        mm(pt[:, :], wd[i][:, fc, :], actT[:, fc, gi * 512:(gi + 1) * 512], fc == 0, fc == NFC - 1, [B_wd[i], B_act[gi]], [bp])
                    tt(xT[:, oc, cols], xT[:, oc, cols], pt[:, :], ALU.add, [bp, B_xT[oc][g]], [B_xT[oc][g]])
        P.barrier()

    a9 = Alloc(nc, ARENA, SB_LIMIT, "st")
    A_BANKS[0] = ["pA0", "pA1", "pA2"]
    xo = [a9.t("xo%d" % i, [128, D], F32) for i in range(2)]
    B_xo = [TB("xo0"), TB("xo1")]
    B_out = TB("out")
    for t in range(16):
        g = t // 4
        xo_, bxo = xo[t % 2], B_xo[t % 2]
        for hf in range(2):
            pt, bp = nextA()
            for k4 in range(4):
                kc = hf * 4 + k4
                tpose(pt[:, k4 * 128:(k4 + 1) * 128], xT[:, kc, t * 128:(t + 1) * 128], id_f, [B_xT[kc][g], B_cst], [bp])
            if hf == 0:
                vcopy(xo_[:, 0:512], pt[:, :], [bp], [bxo])
            else:
                acopy(xo_[:, 512:1024], pt[:, :], [bp], [bxo])
        dma("sp", out_d[t * 128:(t + 1) * 128, :], xo_[:], [bxo], [B_out])
    P.wait_all("sp", [B_out])

    with nc.Block() as block:
        P.emit(nc, block, es)
    es.close()
    return nc


_CACHE = {}


def kernel(**inputs):
    NL = int(os.environ.get("KNL", DEPTH))
    inputs = {k: np.asarray(v) for k, v in inputs.items()}
    if NL not in _CACHE:
        _CACHE[NL] = build_program(NL)
    nc = _CACHE[NL]
    tb, T, Tg = host_tables()
    x = inputs["x"]
    in_maps = []
    shared = {k: np.ascontiguousarray(inputs[k][:NL], dtype=np.float32) for k in ["w_in", "w_out", "w_mem_kv", "w_gate_up", "w_down"]}
    for c in range(8):
        b, half = c // 2, c % 2
        m = dict(shared)
        m["x"] = np.ascontiguousarray(x[b, half * TOK:(half + 1) * TOK, :], dtype=np.float32)
        m["mem"] = np.ascontiguousarray(inputs["mem"][b], dtype=np.float32)
        m["cst"] = host_consts(inputs, half)
        m["tb"] = tb
        m["Ttab"] = T
        m["Tgtab"] = Tg
        in_maps.append(m)
    res = run_bass_kernel_spmd(nc, in_maps, core_ids=list(range(8)))
    out = np.zeros((4, S, D), np.float32)
    for c in range(8):
        b, half = c // 2, c % 2
        out[b, half * TOK:(half + 1) * TOK, :] = res.results[c]["out"]
    return out
```

```python
import math
import os
import contextlib
import numpy as np
import ml_dtypes
import concourse.bass as bass
import concourse.mybir as mybir
from concourse.bass_utils import run_bass_kernel_spmd

F32 = mybir.dt.float32
BF16 = mybir.dt.bfloat16
AF = mybir.ActivationFunctionType
ALU = mybir.AluOpType

D = 1024
S = 4096
TOK = 2048
NG = 4
DEPTH = 4
FFN = 2816
NFC = 22
EPS = 1e-6
NEG = -30000.0


class TB:
    __slots__ = ("name", "w", "r", "x")

    def __init__(self, name="", x=False):
        self.name = name
        self.w = None
        self.r = {}
        self.x = x


CE = ("pe", "act", "dve", "pool")
STREAMS = ("pe", "act", "dve", "pool", "sp")
NDSEM = 8
STRICT_SAME = os.environ.get('KSTRICT', '1') == '1'


class Prog:
    def __init__(self):
        self.ops = {s: [] for s in STREAMS}
        self.seen = {s: {} for s in STREAMS}
        self.marked = {e: set() for e in CE}
        self.dma_cnt = {}
        self.dma_rr = {"sp": 0, "pool": 0}
        self.ncc = 0
        self.last = {}

    def _need(self, stream, tok, waits):
        if tok is None:
            return
        key, val = tok
        if key == stream and (stream == "pe" or not STRICT_SAME):
            return
        if self.seen[stream].get(key, 0) >= val:
            return
        if waits.get(key, 0) < val:
            waits[key] = val

    def _commit(self, stream, waits):
        for key, val in waits.items():
            self.seen[stream][key] = val
            if key in CE:
                self.marked[key].add(val)

    def op(self, stream, fn, reads=(), writes=(), dma=False, cc=False):
        waits = {}
        for b in reads:
            self._need(stream, b.w, waits)
            if b.x:
                for t in b.r.values():
                    self._need(stream, t, waits)
        for b in writes:
            self._need(stream, b.w, waits)
            for t in b.r.values():
                self._need(stream, t, waits)
        if cc:
            key = ("cc", self.ncc)
            self.ncc += 1
            self.dma_cnt[key] = 1
            tok = (key, 1)
            kind = 2
        elif dma:
            k = self.dma_rr[stream] % NDSEM
            self.dma_rr[stream] += 1
            key = ("dma", stream, k)
            n = self.dma_cnt.get(key, 0)
            if n > 0:
                self._need(stream, (key, 16 * n), waits)
            self.dma_cnt[key] = n + 1
            tok = (key, 16 * (n + 1))
            kind = 1
        else:
            tok = (stream, len(self.ops[stream]) + 1)
            kind = 0
        self._commit(stream, waits)
        self.ops[stream].append((waits, fn, tok, kind))
        self.last[tok[0]] = tok
        for b in reads:
            b.r[tok[0]] = tok
        for b in writes:
            b.w = tok
            b.r = {}
        return tok

    def wait_all(self, stream, bufs):
        waits = {}
        for b in bufs:
            self._need(stream, b.w, waits)
            for t in b.r.values():
                self._need(stream, t, waits)
        self._commit(stream, waits)
        self.ops[stream].append((waits, None, None, 0))

    def barrier(self):
        toks = [t for t in self.last.values() if not (isinstance(t[0], tuple) and t[0][0] == "cc")]
        for s in STREAMS:
            waits = {}
            for t in toks:
                self._need(s, t, waits)
            self._commit(s, waits)
            self.ops[s].append((waits, None, None, 0))

    def emit(self, nc, block, stack):
        sems = {}
        for e in CE:
            sems[e] = stack.enter_context(nc.semaphore("s_" + e))
        for key in self.dma_cnt:
            sems[key] = stack.enter_context(nc.semaphore("d_" + "_".join(str(k) for k in key)))
        rank = {}
        for e in CE:
            m = sorted(self.marked[e])
            rank[e] = {v: i + 1 for i, v in enumerate(m)}
        marked = self.marked

        def runner(stream):
            ops = self.ops[stream]

            def body(eng):
                for i, (waits, fn, tok, kind) in enumerate(ops, 1):
                    for key, val in waits.items():
                        v = rank[key][val] if key in CE else val
                        eng.wait_ge(sems[key], v)
                    if fn is None:
                        continue
                    ins = fn(eng)
                    if kind == 1:
                        ins.then_inc(sems[tok[0]], 16)
                    elif kind == 2:
                        ins.then_inc(sems[tok[0]], 1)
                    elif stream in CE and i in marked[stream]:
                        ins.then_inc(sems[stream], 1)
            return body

        block.tensor(runner("pe"))
        block.scalar(runner("act"))
        block.vector(runner("dve"))
        block.gpsimd(runner("pool"))
        block.sync(runner("sp"))


RET_H = 6
GAM = [1.0 - 2.0 ** (-5.0 - h) for h in range(RET_H)]
SLOPES = [2.0 ** (-8.0 * (h + 1) / 6.0) for h in range(6)]


def lambda_init(layer):
    return 0.8 - 0.6 * math.exp(-0.3 * layer)


class CMap:
    def __init__(self):
        self.n = 0
        self.m = {}

    def add(self, name, w):
        self.m[name] = (self.n, w)
        self.n += w

    def sl(self, name):
        a, w = self.m[name]
        return slice(a, a + w)


def build_cmap():
    c = CMap()
    for l in range(DEPTH):
        c.add("anw%d" % l, 8)
        c.add("fnw%d" % l, 8)
        c.add("mqw%d" % l, 1)
        c.add("mkw%d" % l, 1)
    c.add("mnw", 8)
    for j in range(2):
        c.add("dqw%d" % j, 1)
        c.add("dkw%d" % j, 1)
        for h in range(6):
            c.add("gnw%d_%d" % (j, h), 1)
    for p in range(3):
        c.add("g128_%d" % p, 1)
    c.add("flag", 1)
    c.add("bown", 6 * 16)
    c.add("bprev", 6 * 32)
    c.add("identf", 128)
    c.add("Dm", 6 * 128)
    c.add("dectab", 384)
    c.add("qdtab", 3 * 512)
    c.add("subw", 2 * 128)
    c.add("lamv", 8 * 64)
    return c


CM = build_cmap()


def host_consts(inputs, half):
    c = np.zeros((128, CM.n), np.float32)
    p = np.arange(128)

    def colvec(v):
        return np.asarray(v, np.float32).reshape(8, 128).T

    for l in range(DEPTH):
        c[:, CM.sl("anw%d" % l)] = colvec(inputs["attn_norm_w"][l])
        c[:, CM.sl("fnw%d" % l)] = colvec(inputs["ffn_norm_w"][l])
        c[:, CM.sl("mqw%d" % l)] = inputs["mem_q_norm_w"][l][p % 64][:, None]
        c[:, CM.sl("mkw%d" % l)] = inputs["mem_k_norm_w"][l][p % 64][:, None]
    c[:, CM.sl("mnw")] = colvec(inputs["mem_norm_w"])
    for j in range(2):
        c[:, CM.sl("dqw%d" % j)] = inputs["diff_q_norm_w"][j][p % 64][:, None]
        c[:, CM.sl("dkw%d" % j)] = inputs["diff_k_norm_w"][j][p % 64][:, None]
        for h in range(6):
            c[:, CM.sl("gnw%d_%d" % (j, h))] = inputs["ret_gn_w"][j][h][:, None]
    g = np.array(GAM, np.float64)
    for pr in range(3):
        c[:, CM.sl("g128_%d" % pr)] = (g[2 * pr + p // 64] ** 128).astype(np.float32)[:, None]
    c[:, CM.sl("flag")] = float(half)
    bo = np.zeros((6, 16), np.float32)
    bp = np.zeros((6, 32), np.float32)
    for h in range(6):
        for m in range(-3, 13):
            bo[h, m + 3] = -SLOPES[h] * 128.0 * m
        for m in range(32):
            bp[h, m] = (-SLOPES[h] * 128.0 * m) if half == 1 else NEG
    c[:, CM.sl("bown")] = bo.reshape(1, -1)
    c[:, CM.sl("bprev")] = bp.reshape(1, -1)
    c[:, CM.sl("identf")] = np.eye(128, dtype=np.float32)
    j_ = np.arange(128)[:, None]
    i_ = np.arange(128)[None, :]
    Dm = np.zeros((128, 6, 128), np.float64)
    for h in range(6):
        Dm[:, h, :] = np.where((j_ // 64) <= (i_ // 64), g[h] ** np.abs(i_ - j_), 0.0) / 8.0
    c[:, CM.sl("Dm")] = Dm.reshape(128, -1).astype(np.float32)
    dec = np.zeros((128, 6, 64), np.float64)
    for h in range(6):
        dec[:, h, :] = (g[h] ** (127 - np.arange(128)))[:, None] / 8.0
    c[:, CM.sl("dectab")] = dec.reshape(128, -1).astype(np.float32)
    qd = np.zeros((128, 3, 512), np.float64)
    for pr in range(3):
        gg = g[2 * pr + p // 64]
        qd[:, pr, :] = gg[:, None] ** ((np.arange(512) % 128) + 1)[None, :]
    c[:, CM.sl("qdtab")] = qd.reshape(128, -1).astype(np.float32)
    sw = np.zeros((128, 2, 128), np.float32)
    for j in range(2):
        sw[:, j, :] = inputs["diff_subln_w"][j][None, :]
    c[:, CM.sl("subw")] = sw.reshape(128, -1)
    lv = np.zeros((128, 8, 64), np.float32)
    for j in range(2):
        for k, nm in enumerate(["diff_lambda_q1", "diff_lambda_k1", "diff_lambda_q2", "diff_lambda_k2"]):
            lv[:, j * 4 + k, :] = inputs[nm][j][None, :]
    c[:, CM.sl("lamv")] = lv.reshape(128, -1)
    return c


def host_tables():
    tb = np.zeros((128, 3, 128), np.float32)
    tb[:, 0, :] = 1.0
    tb[:, 1, :] = ((np.arange(128)[:, None] // 64) == (np.arange(128)[None, :] // 64)).astype(np.float32)
    tb[:, 2, :] = np.eye(128, dtype=np.float32)
    tb = tb.astype(ml_dtypes.bfloat16)
    j_ = np.arange(128)[:, None].astype(np.float64)
    i5 = np.arange(512)[None, :].astype(np.float64)
    i1 = np.arange(128)[None, :].astype(np.float64)
    T = np.zeros((6, 128, 512), np.float32)
    Tg = np.zeros((6, 128, 128), np.float32)
    for h in range(6):
        T[h] = (-SLOPES[h] * (i5 - j_)).astype(np.float32)
        allowed = (j_ // 64) <= (i1 // 64)
        Tg[h] = np.where(allowed, -SLOPES[h] * np.abs(i1 - j_), NEG).astype(np.float32)
    return tb, T, Tg


class Alloc:
    def __init__(self, nc, base, limit, tag):
        self.nc = nc
        self.off = base
        self.limit = limit
        self.tag = tag
        self.i = 0

    def t(self, name, shape, dt):
        esz = 2 if dt == BF16 else 4
        n = 1
        for s in shape[1:]:
            n *= s
        nb = (n * esz + 63) // 64 * 64
        assert self.off + nb <= self.limit, (name, self.off, nb, self.limit)
        h = self.nc.alloc_sbuf_tensor_at("%s_%s_%d" % (self.tag, name, self.i), list(shape), dt, offset=self.off)
        self.i += 1
        self.off += nb
        return h


SB_BASE = 16384 + 512
SB_LIMIT = 224 * 1024 - 512


def build_program(NL):
    nc = bass.Bass("TRN2", target_bir_lowering=False)
    P = Prog()
    dram = {}

    def din(name, shape, dt=F32):
        dram[name] = nc.dram_tensor(name, list(shape), dt, kind="ExternalInput")
        return dram[name].ap()

    x_d = din("x", [TOK, D])
    mem_d = din("mem", [256, D])
    w_in_d = din("w_in", [NL, D, 2560])
    w_out_d = din("w_out", [NL, D, D])
    w_mkv_d = din("w_mem_kv", [NL, D, 512])
    w_gu_d = din("w_gate_up", [NL, D, 2 * FFN])
    w_dn_d = din("w_down", [NL, FFN, D])
    cst_d = din("cst", [128, CM.n])
    tb_d = din("tb", [128, 3, 128], BF16)
    T_d = din("Ttab", [6, 128, 512])
    Tg_d = din("Tgtab", [6, 128, 128])
    out_d = nc.dram_tensor("out", [TOK, D], F32, kind="ExternalOutput").ap()

    qT_s = nc.dram_tensor("qT_s", [768, TOK], BF16, kind="ExternalOutput").ap()
    qdT_s = nc.dram_tensor("qdT_s", [384, TOK], BF16, kind="ExternalOutput").ap()
    kT_s2 = [nc.dram_tensor("kT_s%d" % i, [384, TOK], BF16).ap() for i in range(2)]
    kT_g2 = [nc.dram_tensor("kT_g%d" % i, [768, TOK], BF16).ap() for i in range(2)]
    v_s2 = [nc.dram_tensor("v_s%d" % i, [1024, 768], BF16).ap() for i in range(2)]
    v_g2 = [nc.dram_tensor("v_g%d" % i, [2048, 768], BF16).ap() for i in range(2)]

    def kT_rows(r0):
        return kT_s2[r0 // 384][r0 % 384:r0 % 384 + 128, :]

    def v_rows(tok0):
        return v_s2[tok0 // 1024][tok0 % 1024:tok0 % 1024 + 128, :]
    gT_s = nc.dram_tensor("gT_s", [768, TOK], BF16, kind="ExternalOutput").ap()
    kt_s = nc.dram_tensor("kt_s", [TOK, 384], BF16, kind="ExternalOutput").ap()
    st_l = nc.dram_tensor("st_l", [384, 256], F32).ap()
    st_g = nc.dram_tensor("st_g", [768, 256], F32).ap()
    NCORES = int(os.environ.get('KNC', '8'))
    RG = [[2 * i, 2 * i + 1] for i in range(NCORES // 2)]

    pa = Alloc(nc, SB_BASE, SB_LIMIT, "p")
    xT = pa.t("xT", [128, 8, TOK], F32)
    cst = pa.t("cst", [128, CM.n], F32)
    tbs = pa.t("tb", [128, 3, 128], BF16)
    memnT = pa.t("memnT", [128, 8, 256], BF16)
    ARENA = pa.off
    ones_b = tbs[:, 0, :]
    bd64_b = tbs[:, 1, :]
    id_b = tbs[:, 2, :]
    id_f = cst[:, CM.sl("identf")]
    B_xT = [[TB("xT%d_%d" % (kc, g)) for g in range(NG)] for kc in range(8)]
    B_cst = TB("cst")
    B_tb = TB("tb")
    B_memnT = TB("memnT")

    def C(name):
        return cst[:, CM.sl(name)]

    es = contextlib.ExitStack()
    psn = ["pA0", "pA1", "pA2", "pB", "pS0", "pS1", "pO0", "pO1"]
    ps = {n: es.enter_context(nc.psum_tensor(n, [128, 512], F32)) for n in psn}
    B_ps = {n: TB(n, x=True) for n in psn}
    rrA = [0]

    A_BANKS = [["pA0", "pA1", "pA2"]]

    def nextA():
        lst = A_BANKS[0]
        n = lst[rrA[0] % len(lst)]
        rrA[0] += 1
        return ps[n], B_ps[n]

    rot = {}

    def rr(key, n):
        v = rot.get(key, 0)
        rot[key] = v + 1
        return v % n

    def dma(stream, out, in_, reads, writes):
        return P.op(stream, lambda e: e.dma_start(out=out, in_=in_), reads=reads, writes=writes, dma=True)

    def mm(out, lhsT, rhs, start, stop, reads, writes, skip=False):
        if skip:
            return P.op("pe", lambda e: e.matmul(out, lhsT=lhsT, rhs=rhs, start=start, stop=stop, skip_group_check=True), reads=reads, writes=writes)
        return P.op("pe", lambda e: e.matmul(out, lhsT=lhsT, rhs=rhs, start=start, stop=stop), reads=reads, writes=writes)

    def act(out, in_, func, reads, writes, bias=0.0, scale=1.0, accum_out=None):
        if accum_out is None:
            return P.op("act", lambda e: e.activation(out=out, in_=in_, func=func, bias=bias, scale=scale), reads=reads, writes=writes)
        return P.op("act", lambda e: e.activation(out=out, in_=in_, func=func, bias=bias, scale=scale, accum_out=accum_out), reads=reads, writes=writes)

    def dve(fn, reads, writes):
        return P.op("dve", fn, reads=reads, writes=writes)

    def rstd_from(psum_ap, out_ap, n, reads, writes, eps=EPS, extra=1.0):
        act(out_ap, psum_ap, AF.Sqrt, reads, writes, bias=eps / (extra * extra), scale=1.0 / (n * extra * extra))
        dve(lambda e: e.reciprocal(out=out_ap, in_=out_ap), writes, writes)

    wbf_in = [nc.dram_tensor("wbf_in%d" % l, [D, 2560], BF16).ap() for l in range(NL)]
    wbf_mkv = [nc.dram_tensor("wbf_mkv%d" % l, [D, 512], BF16).ap() for l in range(NL)]
    wbf_out = [nc.dram_tensor("wbf_out%d" % l, [D, D], BF16).ap() for l in range(NL)]
    B_wbf = [{"in": [TB("wbi%d_%d" % (l, k)) for k in range(8)], "mkv": [TB("wbm%d_%d" % (l, k)) for k in range(2)], "out": [TB("wbo%d_%d" % (l, k)) for k in range(4)]} for l in range(NL)]
    def emit_precast(l):
        for kc in range(8):
            rsl = slice(kc * 128, (kc + 1) * 128)
            P.op("pool", lambda e, l=l, rsl=rsl: e.dma_start(out=wbf_in[l][rsl, :], in_=w_in_d[l, rsl, :]), reads=[], writes=[B_wbf[l]["in"][rsl.start // 128]], dma=True)
        for kc in range(0, 8, 4):
            rsl = slice(kc * 128, (kc + 4) * 128)
            P.op("pool", lambda e, l=l, rsl=rsl: e.dma_start(out=wbf_mkv[l][rsl, :], in_=w_mkv_d[l, rsl, :]), reads=[], writes=[B_wbf[l]["mkv"][rsl.start // 512]], dma=True)
        for kc in range(0, 8, 2):
            rsl = slice(kc * 128, (kc + 2) * 128)
            P.op("pool", lambda e, l=l, rsl=rsl: e.dma_start(out=wbf_out[l][rsl, :], in_=w_out_d[l, rsl, :]), reads=[], writes=[B_wbf[l]["out"][rsl.start // 256]], dma=True)

    dma("sp", cst[:], cst_d, [], [B_cst])
    dma("sp", tbs[:], tb_d, [], [B_tb])

    a0 = Alloc(nc, ARENA, SB_LIMIT, "ld")
    xin = [a0.t("xin%d" % i, [128, D], F32) for i in range(2)]
    B_xin = [TB("xin0"), TB("xin1")]
    memin = a0.t("memin", [128, 2, D], F32)
    B_memin = TB("memin")
    memT = a0.t("memT", [128, 8, 256], F32)
    B_memT = TB("memT")
    sqm = a0.t("sqm", [128, 256], BF16)
    B_sqm = TB("sqm")
    rsm = a0.t("rsm", [128, 256], F32)
    B_rsm = TB("rsm")
    for t in range(16):
        xi, bx = xin[t % 2], B_xin[t % 2]
        dma("sp", xi[:], x_d[t * 128:(t + 1) * 128, :], [], [bx])
        for hf in range(2):
            pt, bp = nextA()
            for k4 in range(4):
                kc = hf * 4 + k4
                P.op("pe", lambda e, pt=pt, xi=xi, k4=k4, kc=kc: e.transpose(pt[:, k4 * 128:(k4 + 1) * 128], xi[:, kc * 128:(kc + 1) * 128], id_f),
                     reads=[bx, B_cst], writes=[bp])
            g = t // 4
            wr = [B_xT[hf * 4 + k4][g] for k4 in range(4)]
            outap = xT[:, hf * 4:hf * 4 + 4, t * 128:(t + 1) * 128]
            inap = pt[:, :].rearrange("p (k n) -> p k n", k=4)
            if hf == 0:
                dve(lambda e, o=outap, i=inap: e.tensor_copy(out=o, in_=i), [bp], wr)
            else:
                P.op("act", lambda e, o=outap, i=inap: e.copy(out=o, in_=i), reads=[bp], writes=wr)
    dma("sp", memin[:], mem_d.rearrange("(t p) d -> p t d", p=128), [], [B_memin])
    for mt in range(2):
        for hf in range(2):
            pt, bp = nextA()
            for k4 in range(4):
                kc = hf * 4 + k4
                P.op("pe", lambda e, pt=pt, k4=k4, kc=kc, mt=mt: e.transpose(pt[:, k4 * 128:(k4 + 1) * 128], memin[:, mt, kc * 128:(kc + 1) * 128], id_f),
                     reads=[B_memin, B_cst], writes=[bp])
            dve(lambda e, pt=pt, hf=hf, mt=mt: e.tensor_copy(out=memT[:, hf * 4:hf * 4 + 4, mt * 128:(mt + 1) * 128], in_=pt[:, :].rearrange("p (k n) -> p k n", k=4)),
                [bp], [B_memT])
    for kc in range(8):
        act(sqm[:], memT[:, kc, :], AF.Square, [B_memT], [B_sqm])
        mm(ps["pB"][:, 0:256], ones_b, sqm[:], kc == 0, kc == 7, [B_sqm, B_tb], [B_ps["pB"]])
    rstd_from(ps["pB"][:, 0:256], rsm[:], D, [B_ps["pB"]], [B_rsm])
    for kc in range(8):
        dve(lambda e, kc=kc: e.scalar_tensor_tensor(out=memnT[:, kc, :], in0=memT[:, kc, :], scalar=C("mnw")[:, kc:kc + 1], in1=rsm[:], op0=ALU.mult, op1=ALU.mult),
            [B_memT, B_rsm, B_cst], [B_memnT])
    P.barrier()

    def norm_group(g, wname, hT, B_h, sq, B_sq, rs, B_rs):
        cols = slice(g * 512, (g + 1) * 512)
        for kc in range(8):
            s, bs = sq[kc % 2], B_sq[kc % 2]
            act(s[:], xT[:, kc, cols], AF.Square, [B_xT[kc][g]], [bs])
            mm(ps["pB"][:, :], ones_b, s[:], kc == 0, kc == 7, [bs, B_tb], [B_ps["pB"]])
        rstd_from(ps["pB"][:, :], rs[:], D, [B_ps["pB"]], [B_rs])
        for kc in range(8):
            dve(lambda e, kc=kc: e.scalar_tensor_tensor(out=hT[:, kc, :], in0=xT[:, kc, cols], scalar=C(wname)[:, kc:kc + 1], in1=rs[:], op0=ALU.mult, op1=ALU.mult),
                [B_xT[kc][g], B_rs, B_cst], [B_h])

    def fm_proj(w_ap_fn, M, hT, B_h, B_w, ncols=512, hcols=None):
        pt, bp = nextA()
        for kc in range(8):
            rhs = hT[:, kc, :] if hcols is None else hT[:, kc, hcols]
            mm(pt[0:M, 0:ncols], w_ap_fn(kc), rhs, kc == 0, kc == 7, [B_h, B_w], [bp])
        return pt, bp

    def qknorm_evac(pt, bp, N, wcol, out_ap, B_out, sql, B_sql, rsl, B_rsl):
        i_ = rr("qkn", len(sql))
        sq, B_sq, rs, B_rs = sql[i_], B_sql[i_], rsl[i_], B_rsl[i_]
        sb_ = ["pS0", "pS1"][rr("qknb", 2)]
        act(sq[:, 0:N], pt[:, 0:N], AF.Square, [bp], [B_sq])
        mm(ps[sb_][:, 0:N], bd64_b, sq[:, 0:N], True, True, [B_sq, B_tb], [B_ps[sb_]])
        rstd_from(ps[sb_][:, 0:N], rs[:, 0:N], 64, [B_ps[sb_]], [B_rs])
        dve(lambda e: e.scalar_tensor_tensor(out=out_ap, in0=pt[:, 0:N], scalar=wcol, in1=rs[:, 0:N], op0=ALU.mult, op1=ALU.mult),
            [bp, B_rs, B_cst], [B_out])

    def stt(out, in0, scalar, in1, op0, op1, reads, writes):
        return P.op("dve", lambda e: e.scalar_tensor_tensor(out=out, in0=in0, scalar=scalar, in1=in1, op0=op0, op1=op1), reads=reads, writes=writes)

    def tt(out, in0, in1, op, reads, writes):
        return P.op("dve", lambda e: e.tensor_tensor(out=out, in0=in0, in1=in1, op=op), reads=reads, writes=writes)

    def tsc(out, in0, scalar1, op0, reads, writes):
        return P.op("dve", lambda e: e.tensor_scalar(out=out, in0=in0, scalar1=scalar1, scalar2=None, op0=op0), reads=reads, writes=writes)

    def vcopy(out, in_, reads, writes):
        return P.op("dve", lambda e: e.tensor_copy(out=out, in_=in_), reads=reads, writes=writes)

    def vrecip(out, in_, reads, writes):
        return P.op("dve", lambda e: e.reciprocal(out=out, in_=in_), reads=reads, writes=writes)

    def acopy(out, in_, reads, writes):
        return P.op("act", lambda e: e.copy(out=out, in_=in_), reads=reads, writes=writes)

    def tpose(out, in_, ident, reads, writes):
        return P.op("pe", lambda e: e.transpose(out, in_, ident), reads=reads, writes=writes)

    def wslice(w, c0, M=128):
        return lambda kc: w[:, kc, c0:c0 + M]

    KSTOP = int(os.environ.get('KSTOP', '99'))
    for layer in range(NL):
        if KSTOP <= 0:
            break
        is_ret = (layer % 2 == 0)
        jj = layer // 2
        a1 = Alloc(nc, ARENA, SB_LIMIT, "ip%d" % layer)
        qcnT = a1.t("qcnT", [128, 2, TOK], BF16)
        B_qcn = [TB("qcn%d" % g) for g in range(NG)]
        mkT = a1.t("mkT", [128, 2, 256], BF16)
        B_mkT = TB("mkT")
        mv = a1.t("mv", [128, 2, 4, 65], BF16)
        B_mv = TB("mv")
        AR2 = a1.off
        w_in = a1.t("w_in", [128, 8, 2560], BF16)
        B_win = TB("w_in")
        w_mkv = a1.t("w_mkv", [128, 8, 512], BF16)
        B_wmkv = TB("w_mkv")
        hT = [a1.t("hT%d" % i, [128, 8, 512], BF16) for i in range(NG)]
        B_hT = [TB("hT%d" % i) for i in range(NG)]
        sq = [a1.t("sq%d" % i, [128, 512], BF16) for i in range(2)]
        B_sq = [TB("sq0"), TB("sq1")]
        rs = a1.t("rs", [128, 512], F32)
        B_rs = TB("rs")
        sq2 = [a1.t("sq2_%d" % i, [128, 512], BF16) for i in range(3)]
        B_sq2 = [TB("sq2_%d" % i) for i in range(3)]
        rs2 = [a1.t("rs2_%d" % i, [128, 512], F32) for i in range(3)]
        B_rs2 = [TB("rs2_%d" % i) for i in range(3)]
        A_BANKS[0] = ["pA0", "pA1", "pA2", "pO0", "pO1"]
        NST = 4
        stg = [a1.t("stg%d" % i, [128, 512], BF16) for i in range(NST)]
        B_stg = [TB("stg%d" % i) for i in range(NST)]
        stk = [a1.t("stk%d" % i, [128, 1152], BF16) for i in range(2)]
        S1 = a1.t("S1", [128, 3, 256], F32)
        B_S1 = [TB("S1_%d" % i) for i in range(3)]
        B_stk = [TB("stk%d" % i) for i in range(2)]

        for kc in range(8):
            for q2 in range(2):
                if layer == 0:
                    dma("pool", w_in[:, kc, q2 * 1280:(q2 + 1) * 1280], w_in_d[layer, kc * 128:(kc + 1) * 128, q2 * 1280:(q2 + 1) * 1280], [], [B_win])
                else:
                    dma("sp", w_in[:, kc, q2 * 1280:(q2 + 1) * 1280], wbf_in[layer][kc * 128:(kc + 1) * 128, q2 * 1280:(q2 + 1) * 1280], [B_wbf[layer]["in"][kc]], [B_win])
            if layer == 0:
                dma("pool", w_mkv[:, kc, :], w_mkv_d[layer, kc * 128:(kc + 1) * 128, :], [], [B_wmkv])
            else:
                dma("sp", w_mkv[:, kc, :], wbf_mkv[layer][kc * 128:(kc + 1) * 128, :], [B_wbf[layer]["mkv"][kc // 4]], [B_wmkv])

        P.op("pool", lambda e, mv=mv: e.memset(mv[:], 1.0), reads=[], writes=[B_mv])
        for c2 in range(2):
            pt, bp = fm_proj(wslice(w_mkv, c2 * 128), 128, memnT, B_memnT, B_wmkv, ncols=256)
            qknorm_evac(pt, bp, 256, C("mkw%d" % layer), mkT[:, c2, :], B_mkT, sq2, B_sq2, rs2, B_rs2)
        for mt in range(2):
            pt, bp = nextA()
            for kc in range(8):
                mm(pt[:, 0:256], memnT[:, kc, mt * 128:(mt + 1) * 128], w_mkv[:, kc, 256:512], kc == 0, kc == 7, [B_memnT, B_wmkv], [bp])
            acopy(mv[:, mt, :, 0:64], pt[:, 0:256].rearrange("p (h d) -> p h d", h=4), [bp], [B_mv])

        B_scr = {n: TB(n) for n in ["qT", "qdT", "kT", "gT", "kt", "v"]}
        scr_lists = {n: [] for n in B_scr}

        def scr_w(name):
            t_ = TB(name)
            scr_lists[name].append(t_)
            return t_

        def stage_out(pt_ap, bp, dst_ap, kind, scr, mul_ap=None):
            if os.environ.get('KNOSTAGE') == '1':
                return
            if os.environ.get('KONLY') and os.environ.get('KONLY') != kind:
                return
            i = rr("stg", NST)
            st, bst = stg[i], B_stg[i]
            if kind == "copy":
                acopy(st[:], pt_ap, [bp], [bst])
            elif kind == "silu":
                act(st[:], pt_ap, AF.Silu, [bp], [bst])
            elif kind == "mul":
                tt(st[:], pt_ap, mul_ap, ALU.mult, [bp, B_cst], [bst])
            if os.environ.get('KNODMA') != '1':
                dma("sp", dst_ap, st[:], [bst], [scr_w(scr)])

        def do_q_diff(g, h_, bh):
            cols = slice(g * 512, (g + 1) * 512)
            for hh in range(6):
                pt, bp = fm_proj(wslice(w_in, hh * 128), 128, h_, bh, B_win)
                i = rr("stg", NST)
                qknorm_evac(pt, bp, 512, C("dqw%d" % jj), stg[i][:], B_stg[i], sq2, B_sq2, rs2, B_rs2)
                dma("sp", qT_s[hh * 128:(hh + 1) * 128, cols], stg[i][:], [B_stg[i]], [scr_w("qT")])

        def do_k_diff(g, h_, bh):
            cols = slice(g * 512, (g + 1) * 512)
            for hh in range(6):
                pt, bp = fm_proj(wslice(w_in, 768 + hh * 128), 128, h_, bh, B_win)
                i = rr("stg", NST)
                qknorm_evac(pt, bp, 512, C("dkw%d" % jj), stg[i][:], B_stg[i], sq2, B_sq2, rs2, B_rs2)
                dma("sp", kT_rows(hh * 128)[:, cols], stg[i][:], [B_stg[i]], [scr_w("kT")])

        def do_ret_fm(g, h_, bh):
            cols = slice(g * 512, (g + 1) * 512)
            for pr in range(3):
                pt, bp = fm_proj(wslice(w_in, pr * 128), 128, h_, bh, B_win)
                stage_out(pt[:, :], bp, qT_s[pr * 128:(pr + 1) * 128, cols], "copy", "qT")
                stage_out(pt[:, :], bp, qdT_s[pr * 128:(pr + 1) * 128, cols], "mul", "qdT", mul_ap=C("qdtab")[:, pr * 512:(pr + 1) * 512])
            for pr in range(3):
                pt, bp = fm_proj(wslice(w_in, 384 + pr * 128), 128, h_, bh, B_win)
                stage_out(pt[:, :], bp, kT_rows(pr * 128)[:, cols], "copy", "kT")
            for hh in range(6):
                pt, bp = fm_proj(wslice(w_in, 1536 + hh * 128), 128, h_, bh, B_win)
                stage_out(pt[:, :], bp, gT_s[hh * 128:(hh + 1) * 128, cols], "silu", "gT")

        def do_qc(g, h_, bh):
            cols = slice(g * 512, (g + 1) * 512)
            for c2 in range(2):
                pt, bp = fm_proj(wslice(w_in, 2304 + c2 * 128), 128, h_, bh, B_win)
                qknorm_evac(pt, bp, 512, C("mqw%d" % layer), qcnT[:, c2, cols], B_qcn[g], sq2, B_sq2, rs2, B_rs2)

        def do_tok(g, h_, bh):
            for t4 in range(4):
                tok0 = g * 512 + t4 * 128
                i = rr("stk", 2)
                sk, bsk = stk[i], B_stk[i]
                nparts = 3 if is_ret else 2
                cbase = 384 if is_ret else 1536
                for part in range(nparts):
                    pt, bp = nextA()
                    c0 = cbase + part * 384
                    for kc in range(8):
                        mm(pt[:, 0:384], h_[:, kc, t4 * 128:(t4 + 1) * 128], w_in[:, kc, c0:c0 + 384], kc == 0, kc == 7, [bh, B_win], [bp])
                    if is_ret and part == 0:
                        tt(sk[:, 0:384], pt[:, 0:384], C("dectab"), ALU.mult, [bp, B_cst], [bsk])
                    elif part % 2 == 0:
                        vcopy(sk[:, part * 384:(part + 1) * 384], pt[:, 0:384], [bp], [bsk])
                    else:
                        acopy(sk[:, part * 384:(part + 1) * 384], pt[:, 0:384], [bp], [bsk])
                if is_ret:
                    dma("sp", kt_s[tok0:tok0 + 128, :], sk[:, 0:384], [bsk], [scr_w("kt")])
                    dma("sp", v_rows(tok0), sk[:, 384:1152], [bsk], [scr_w("v")])
                    for pr in range(3):
                        pst, bpst = nextA()
                        mm(pst[:, 0:256], sk[:, pr * 128:(pr + 1) * 128], sk[:, 384 + pr * 256:384 + (pr + 1) * 256], True, True, [bsk], [bpst])
                        if g == 0 and t4 == 0:
                            vcopy(S1[:, pr, :], pst[:, 0:256], [bpst], [B_S1[pr]])
                        else:
                            stt(S1[:, pr, :], S1[:, pr, :], C("g128_%d" % pr), pst[:, 0:256], ALU.mult, ALU.add, [B_S1[pr], bpst, B_cst], [B_S1[pr]])
                else:
                    dma("sp", v_rows(tok0), sk[:, 0:768], [bsk], [scr_w("v")])

        B_kTg = TB("kT_g")
        B_vg = TB("v_g")
        B_stl = TB("st_l")
        B_stG = TB("st_g")
        if is_ret:
            for g in range(NG):
                h_, bh = hT[g], B_hT[g]
                norm_group(g, "anw%d" % layer, h_, bh, sq, B_sq, rs, B_rs)
                do_tok(g, h_, bh)
                do_ret_fm(g, h_, bh)
                do_qc(g, h_, bh)
            for pr in range(3):
                dma("sp", st_l[pr * 128:(pr + 1) * 128, :], S1[:, pr, :], [B_S1[pr]], [B_stl])
            P.op("pool", lambda e: e.collective_compute("AllGather", ALU.bypass, replica_groups=RG, ins=[st_l], outs=[st_g]),
                 reads=[B_stl], writes=[B_stG], cc=True)
        else:
            for g in range(NG):
                h_, bh = hT[g], B_hT[g]
                norm_group(g, "anw%d" % layer, h_, bh, sq, B_sq, rs, B_rs)
                do_k_diff(g, h_, bh)
                do_tok(g, h_, bh)
            for i2 in range(2):
                P.op("pool", lambda e, i2=i2: e.collective_compute("AllGather", ALU.bypass, replica_groups=RG, ins=[kT_s2[i2]], outs=[kT_g2[i2]]),
                     reads=list(scr_lists["kT"]), writes=[B_kTg], cc=True)
            for i2 in range(2):
                P.op("pool", lambda e, i2=i2: e.collective_compute("AllGather", ALU.bypass, replica_groups=RG, ins=[v_s2[i2]], outs=[v_g2[i2]]),
                     reads=list(scr_lists["v"]), writes=[B_vg], cc=True)
            for g in range(NG):
                h_, bh = hT[g], B_hT[g]
                do_q_diff(g, h_, bh)
                do_qc(g, h_, bh)
        P.barrier()

        if KSTOP <= 1:
            break
        a2 = Alloc(nc, AR2, SB_LIMIT, "mx%d" % layer)
        A_BANKS[0] = ["pA0", "pA1", "pA2"]
        ocT = a2.t("ocT", [128, 8, TOK], BF16)
        B_oc = [[TB("oc%d_%d" % (c, g)) for g in range(NG)] for c in range(8)]
        w_out = a2.t("w_out", [128, 8, D], BF16)
        B_wout = TB("w_out")
        if layer == 0:
            dma("pool", w_out[:], w_out_d[layer].rearrange("(kc p) n -> p kc n", p=128), [], [B_wout])
        else:
            dma("sp", w_out[:], wbf_out[layer].rearrange("(kc p) n -> p kc n", p=128), list(B_wbf[layer]["out"]), [B_wout])
        if layer + 1 < NL:
            emit_precast(layer + 1)
        pm = [a2.t("pm%d" % i, [128, 512], BF16) for i in range(2)]
        B_pm = [TB("pm0"), TB("pm1")]
        omall = a2.t("omall", [128, 4, 256], BF16)
        B_omall = TB("omall")
        rl = a2.t("rl", [128, 8], F32)
        B_rl = TB("rl")
        pT_b = ps["pB"][:, :].bitcast(BF16)
        def emit_memattn():
            for g in range(NG):
                cols = slice(g * 512, (g + 1) * 512)
                for hd in range(4):
                    c2, hh = hd // 2, hd % 2
                    pacc = ps["pO%d" % (hd % 2)]
                    bacc = B_ps["pO%d" % (hd % 2)]
                    for mt in range(2):
                        sn = "pS%d" % rr("pS", 2)
                        mm(ps[sn][:, :], mkT[hh * 64:(hh + 1) * 64, c2, mt * 128:(mt + 1) * 128], qcnT[hh * 64:(hh + 1) * 64, c2, cols], True, True,
                           [B_mkT, B_qcn[g]], [B_ps[sn]])
                        i = rr("pm", 2)
                        act(pm[i][:], ps[sn][:, :], AF.Exp, [B_ps[sn]], [B_pm[i]], scale=0.125)
                        for sub in range(4):
                            mm(pacc[:, sub * 65:(sub + 1) * 65], pm[i][:, sub * 128:(sub + 1) * 128], mv[:, mt, hd, :], mt == 0 and sub == 0, mt == 1, [B_pm[i], B_mv], [bacc], skip=True)
                    acc3 = pacc[:, 0:260].rearrange("p (s e) -> p s e", s=4)
                    vrecip(rl[:, 0:4], acc3[:, :, 64], [bacc], [B_rl])
                    for sub in range(4):
                        tsc(omall[:, sub, hd * 64:(hd + 1) * 64], pacc[:, sub * 65:sub * 65 + 64], rl[:, sub:sub + 1], ALU.mult, [bacc, B_rl], [B_omall])
                for c2 in range(2):
                    for sub in range(4):
                        tpose(pT_b[:, (c2 * 4 + sub) * 128:(c2 * 4 + sub + 1) * 128], omall[:, sub, c2 * 128:(c2 + 1) * 128], id_b, [B_omall, B_tb], [B_ps["pB"]])
                for c2 in range(2):
                    vcopy(ocT[:, 6 + c2, cols], pT_b[:, c2 * 512:(c2 + 1) * 512], [B_ps["pB"]], [B_oc[6 + c2][g]])


        if not is_ret:
            emit_memattn()

        if KSTOP <= 2:
            break
        if is_ret:
            qTp = a2.t("qTp", [128, TOK], BF16)
            qdTp = a2.t("qdTp", [128, TOK], BF16)
            kTp = a2.t("kTp", [128, TOK], BF16)
            ktp = a2.t("ktp", [128, 16, 128], BF16)
            vtp = a2.t("vtp", [128, 16, 256], BF16)
            gTp = a2.t("gTp", [128, 2, TOK], BF16)
            Sst = a2.t("Sst", [128, 256], F32)
            S0 = a2.t("S0", [128, 3, 256], F32)
            Sball = a2.t("Sball", [128, 16, 256], BF16)
            Aa2 = [[a2.t("A%d_%d" % (hh, i), [128, 128], BF16) for i in range(2)] for hh in range(2)]
            sqo2 = [a2.t("sqo%d" % i, [128, 512], BF16) for i in range(2)]
            rso2 = [a2.t("rso%d" % i, [128, 512], F32) for i in range(2)]
            t12 = [a2.t("t1_%d" % i, [128, 512], F32) for i in range(2)]
            B_sqo = [TB("sqo0"), TB("sqo1")]
            B_rso = [TB("rso0"), TB("rso1")]
            B_t1 = [TB("t1_0"), TB("t1_1")]
            Bn = {n: TB(n) for n in ["qTp", "qdTp", "kTp", "ktp", "vtp", "gTp", "S", "S0", "Sb", "A0", "A1", "sqo", "rso", "t1", "st_l", "st_g"]}
            kt_v = kt_s.rearrange("(t j) c -> j t c", j=128)
            v_v2 = [v.rearrange("(t j) c -> j t c", j=128) for v in v_s2]

            def load_vtp(pr):
                for i2 in range(2):
                    dma("sp", vtp[:, i2 * 8:(i2 + 1) * 8, :], v_v2[i2][:, :, pr * 256:(pr + 1) * 256], [B_scr["v"]], [Bn["vtp"]])

            def state_step(pr, b, first):
                sb_ = "pS1" if b % 2 == 0 else "pA2"
                mm(ps[sb_][:, 0:256], ktp[:, b, :], vtp[:, b, :], True, True, [Bn["ktp"], Bn["vtp"]], [B_ps[sb_]])
                if first:
                    vcopy(Sst[:], ps[sb_][:, 0:256], [B_ps[sb_]], [Bn["S"]])
                else:
                    stt(Sst[:], Sst[:], C("g128_%d" % pr), ps[sb_][:, 0:256], ALU.mult, ALU.add, [Bn["S"], B_ps[sb_], B_cst], [Bn["S"]])

            dma("pool", S0[:], st_g[0:384, :].rearrange("(p r) c -> r p c", r=128), [B_stG], [Bn["S0"]])
            emit_memattn()
            for pr in range(3):
                dma("sp", qTp[:], qT_s[pr * 128:(pr + 1) * 128, :], [B_scr["qT"]], [Bn["qTp"]])
                dma("sp", qdTp[:], qdT_s[pr * 128:(pr + 1) * 128, :], [B_scr["qdT"]], [Bn["qdTp"]])
                dma("sp", kTp[:], kT_rows(pr * 128), [B_scr["kT"]], [Bn["kTp"]])
                dma("sp", ktp[:], kt_v[:, :, pr * 128:(pr + 1) * 128], [B_scr["kt"]], [Bn["ktp"]])
                load_vtp(pr)
                dma("sp", gTp[:], gT_s[pr * 256:(pr + 1) * 256, :].rearrange("(h p) n -> p h n", p=128), [B_scr["gT"]], [Bn["gTp"]])
                tsc(Sst[:], S0[:, pr, :], C("flag"), ALU.mult, [Bn["S0"], B_cst], [Bn["S"]])
                B_Sb = [TB("Sb%d" % b) for b in range(16)]
                B_A = [[TB("A%d_%d" % (hh, i)) for i in range(2)] for hh in range(2)]
                acopy(Sball[:, 0, :], Sst[:], [Bn["S"]], [B_Sb[0]])

                def rfront(b, pr=pr):
                    cb = slice(b * 128, (b + 1) * 128)
                    for hh in range(2):
                        h = 2 * pr + hh
                        rws = slice(hh * 64, (hh + 1) * 64)
                        psc = ps["pS0"][:, hh * 128:(hh + 1) * 128]
                        mm(psc, kTp[rws, cb], qTp[rws, cb], True, True, [Bn["kTp"], Bn["qTp"]], [B_ps["pS0"]])
                        tt(Aa2[hh][b % 2][:], psc, C("Dm")[:, h * 128:(h + 1) * 128], ALU.mult, [B_ps["pS0"], B_cst], [B_A[hh][b % 2]])

                rfront(0)
                for b in range(16):
                    cb = slice(b * 128, (b + 1) * 128)
                    if b + 1 < 16:
                        rfront(b + 1)
                    g = b // 4
                    obank = ["pO0", "pO1"] if g % 2 == 0 else ["pA0", "pA1"]
                    for hh in range(2):
                        rws = slice(hh * 64, (hh + 1) * 64)
                        po = ps[obank[hh]][:, (b % 4) * 128:(b % 4 + 1) * 128]
                        mm(po, vtp[:, b, hh * 128:(hh + 1) * 128], Aa2[hh][b % 2][:], True, False, [Bn["vtp"], B_A[hh][b % 2]], [B_ps[obank[hh]]])
                        mm(po, Sball[rws, b, hh * 128:(hh + 1) * 128], qdTp[rws, cb], False, True, [B_Sb[b], Bn["qdTp"]], [B_ps[obank[hh]]])
                    state_step(pr, b, False)
                    if b + 1 < 16:
                        acopy(Sball[:, b + 1, :], Sst[:], [Bn["S"]], [B_Sb[b + 1]])
                    if b % 4 == 3:
                        cols = slice(g * 512, (g + 1) * 512)
                        for hh in range(2):
                            h = 2 * pr + hh
                            pO, bO = ps[obank[hh]], B_ps[obank[hh]]
                            sbk = "pB"
                            act(sqo2[hh][:], pO[:, :], AF.Square, [bO], [B_sqo[hh]])
                            mm(ps[sbk][:, :], ones_b, sqo2[hh][:], True, True, [B_sqo[hh], B_tb], [B_ps[sbk]])
                            rstd_from(ps[sbk][:, :], rso2[hh][:], 128, [B_ps[sbk]], [B_rso[hh]])
                            stt(t12[hh][:], pO[:, :], C("gnw%d_%d" % (jj, h)), rso2[hh][:], ALU.mult, ALU.mult, [bO, B_rso[hh], B_cst], [B_t1[hh]])
                            tt(ocT[:, h, cols], t12[hh][:], gTp[:, hh, cols], ALU.mult, [B_t1[hh], Bn["gTp"]], [B_oc[h][g]])
        else:
            li = lambda_init(layer)
            qz = [a2.t("qz%d" % c_, [128, TOK], BF16) for c_ in range(2)]
            ko2 = [a2.t("ko%d" % i, [128, TOK], BF16) for i in range(2)]
            kp2 = [a2.t("kp%d" % i, [128, TOK], BF16) for i in range(2)]
            vo2 = [a2.t("vo%d" % i, [128, 16, 129], BF16) for i in range(2)]
            vp2 = [a2.t("vp%d" % i, [128, 16, 129], BF16) for i in range(2)]
            Th = a2.t("Th", [128, 512], F32)
            Tg = a2.t("Tg", [128, 128], F32)
            NTMP = 4
            SBOUND = 16.0
            FAR = [(104.0 + 2.0 * SBOUND) / sl for sl in SLOPES]
            SCB = ["pS0", "pS1", "pA0", "pA1"]
            tmp = [a2.t("tmp%d" % i, [128, 512], F32) for i in range(NTMP)]
            Pt = [a2.t("Pt%d" % i, [128, 512], BF16) for i in range(NTMP)]
            o0 = a2.t("o0", [128, 4, 128], F32)
            od = a2.t("od", [128, 128], F32)
            junk = a2.t("junk", [128, 128], F32)
            otok = a2.t("otok", [128, 4, 128], BF16)
            sm = a2.t("sm", [128, 16], F32)
            lt = a2.t("lt", [128, 64], F32)
            Bn = {n: TB(n) for n in ["qh", "ko", "kp", "vo", "vp", "Th", "Tg", "o0", "od", "junk", "otok", "sm", "lam", "lt"]}
            B_tmp = [TB("tmp%d" % i) for i in range(NTMP)]
            B_Pt = [TB("Pt%d" % i) for i in range(NTMP)]
            v_v2 = [v.rearrange("(t j) c -> j t c", j=128) for v in v_s2]
            vg_v2 = [v[0:1024, :].rearrange("(t j) c -> j t c", j=128) for v in v_g2]
            BnKV = [{n: TB(n + str(i)) for n in ["ko", "kp", "vo", "vp"]} for i in range(2)]
            for i in range(2):
                P.op("pool", lambda e, t_=vo2[i]: e.memset(t_[:], 1.0), reads=[], writes=[BnKV[i]["vo"]])
                P.op("pool", lambda e, t_=vp2[i]: e.memset(t_[:], 1.0), reads=[], writes=[BnKV[i]["vp"]])
            lv = C("lamv")
            for k in range(2):
                tt(lt[:], lv[:, (jj * 4 + 2 * k) * 64:(jj * 4 + 2 * k + 1) * 64], lv[:, (jj * 4 + 2 * k + 1) * 64:(jj * 4 + 2 * k + 2) * 64], ALU.mult, [B_cst], [Bn["lt"]])
                P.op("dve", lambda e, k=k: e.reduce_sum(out=sm[:, k:k + 1], in_=lt[:], axis=mybir.AxisListType.X), reads=[Bn["lt"]], writes=[Bn["sm"]])
            act(sm[:, 2:4], sm[:, 0:2], AF.Exp, [Bn["sm"]], [Bn["sm"]])
            tt(sm[:, 4:5], sm[:, 3:4], sm[:, 2:3], ALU.subtract, [Bn["sm"]], [Bn["sm"]])
            tsc(sm[:, 4:5], sm[:, 4:5], -li, ALU.add, [Bn["sm"]], [Bn["sm"]])
            neglam = sm[:, 4:5]
            accs = [(ps["pO0"], B_ps["pO0"], 0), (ps["pO0"], B_ps["pO0"], 256), (ps["pO1"], B_ps["pO1"], 0), (ps["pO1"], B_ps["pO1"], 256)]
            P.op("pool", lambda e: e.memset(qz[0][64:128, :], 0.0), reads=[], writes=[Bn["qh"]])
            P.op("pool", lambda e: e.memset(qz[1][0:64, :], 0.0), reads=[], writes=[Bn["qh"]])
            for h in range(6):
                rws_h = slice(h * 128, (h + 1) * 128)
                ko, kp, vo, vp = ko2[h % 2], kp2[h % 2], vo2[h % 2], vp2[h % 2]
                Bn.update(BnKV[h % 2])
                for c_ in range(2):
                    dma("sp", qz[c_][c_ * 64:(c_ + 1) * 64, :], qT_s[h * 128 + c_ * 64:h * 128 + (c_ + 1) * 64, :], [B_scr["qT"]], [Bn["qh"]])
                dma("sp", ko[:], kT_rows(h * 128), [B_scr["kT"]], [Bn["ko"]])
                dma("sp", kp[:], kT_g2[h // 3][(h % 3) * 128:(h % 3 + 1) * 128, :], [B_kTg], [Bn["kp"]])
                for i2 in range(2):
                    dma("sp", vo[:, i2 * 8:(i2 + 1) * 8, 0:128], v_v2[i2][:, :, rws_h], [B_scr["v"]], [Bn["vo"]])
                    dma("sp", vp[:, i2 * 8:(i2 + 1) * 8, 0:128], vg_v2[i2][:, :, rws_h], [B_vg], [Bn["vp"]])
                dma("sp", Th[:], T_d[h], [], [Bn["Th"]])
                dma("sp", Tg[:], Tg_d[h], [], [Bn["Tg"]])
                units = []
                for g in range(NG):
                    for c in range(2):
                        blocks = [("p", kb) for kb in range(16)] + [("o", kb) for kb in range(4 * g + 4)]
                        def _dist(src, kb, g=g):
                            return (2048 if src == "p" else 0) + 512 * g - (128 * kb + 127)
                        blocks = [(src, kb) for (src, kb) in blocks if _dist(src, kb) < FAR[h]]
                        for bi, (src, kb) in enumerate(blocks):
                            units.append(dict(g=g, c=c, src=src, kb=kb, bi=bi, nb=len(blocks)))

                def front(u, h=h):
                    g, c, src, kb = u["g"], u["c"], u["src"], u["kb"]
                    rws = slice(c * 64, (c + 1) * 64)
                    kk, Bk = (kp, Bn["kp"]) if src == "p" else (ko, Bn["ko"])
                    r = kb - 4 * g if src == "o" else -1
                    c_lo = 128 * r if r > 0 else 0
                    sn = SCB[rr("pS", len(SCB))]
                    psc, bsc = ps[sn], B_ps[sn]
                    mm(psc[:, c_lo:512], kk[:, kb * 128:(kb + 1) * 128], qz[c][:, g * 512 + c_lo:(g + 1) * 512], True, True, [Bk, Bn["qh"]], [bsc])
                    i = rr("tmp", NTMP)
                    tm, btm, pt_, bpt = tmp[i], B_tmp[i], Pt[i], B_Pt[i]
                    u["pt"] = (pt_, bpt)
                    u["r"] = r
                    if r < 0:
                        m = (16 + 4 * g - kb) if src == "p" else (4 * g - kb)
                        bcol = C("bprev")[:, h * 32 + m:h * 32 + m + 1] if src == "p" else C("bown")[:, h * 16 + m + 3:h * 16 + m + 4]
                        stt(tm[:], psc[:, :], 0.125, Th[:], ALU.mult, ALU.add, [bsc, Bn["Th"]], [btm])
                        act(pt_[:], tm[:], AF.Exp, [btm, B_cst], [bpt], bias=bcol)
                    else:
                        d0 = 128 * r
                        stt(tm[:, d0:d0 + 128], psc[:, d0:d0 + 128], 0.125, Tg[:], ALU.mult, ALU.add, [bsc, Bn["Tg"]], [btm])
                        if r < 3:
                            stt(tm[:, d0 + 128:512], psc[:, d0 + 128:512], 0.125, Th[:, d0 + 128:512], ALU.mult, ALU.add, [bsc, Bn["Th"]], [btm])
                        act(pt_[:, d0:d0 + 128], tm[:, d0:d0 + 128], AF.Exp, [btm], [bpt])
                        if r < 3:
                            bcol = C("bown")[:, h * 16 + (-r) + 3:h * 16 + (-r) + 4]
                            act(pt_[:, d0 + 128:512], tm[:, d0 + 128:512], AF.Exp, [btm, B_cst], [bpt], bias=bcol)

                def back(u, h=h):
                    g, c, src, kb, bi, r = u["g"], u["c"], u["src"], u["kb"], u["bi"], u["r"]
                    pt_, bpt = u["pt"]
                    vv, Bv = (vp, Bn["vp"]) if src == "p" else (vo, Bn["vo"])
                    for sub in range(4):
                        if r >= 0 and sub < r:
                            continue
                        pa_, ba_, off = accs[sub]
                        last = (src == "o" and kb == 4 * g + sub)
                        mm(pa_[:, off:off + 129], pt_[:, sub * 128:(sub + 1) * 128], vv[:, kb, :], bi == 0 and off == 0, last, [bpt, Bv], [ba_], skip=True)
                    if bi == u["nb"] - 1:
                        finalize(g, c, h)

                def finalize(g, c, h):
                    cols = slice(g * 512, (g + 1) * 512)
                    for sub in range(4):
                        pa_, ba_, off = accs[sub]
                        vrecip(sm[:, 8 + sub:9 + sub], pa_[:, off + 128:off + 129], [ba_], [Bn["sm"]])
                        if c == 0:
                            tsc(o0[:, sub, :], pa_[:, off:off + 128], sm[:, 8 + sub:9 + sub], ALU.mult, [ba_, Bn["sm"]], [Bn["o0"]])
                        else:
                            tt(sm[:, 12 + sub:13 + sub], sm[:, 8 + sub:9 + sub], neglam, ALU.mult, [Bn["sm"]], [Bn["sm"]])
                            stt(od[:], pa_[:, off:off + 128], sm[:, 12 + sub:13 + sub], o0[:, sub, :], ALU.mult, ALU.add, [ba_, Bn["sm"], Bn["o0"]], [Bn["od"]])
                            act(junk[:], od[:], AF.Square, [Bn["od"]], [Bn["junk"], Bn["sm"]], accum_out=sm[:, 5:6])
                            f = 1.0 - li
                            rstd_from(sm[:, 5:6], sm[:, 6:7], 128, [Bn["sm"]], [Bn["sm"]], extra=f)
                            stt(otok[:, sub, :], od[:], sm[:, 6:7], C("subw")[:, jj * 128:(jj + 1) * 128], ALU.mult, ALU.mult, [Bn["od"], Bn["sm"], B_cst], [Bn["otok"]])
                    if c == 1:
                        for sub in range(4):
                            tpose(pT_b[:, sub * 128:(sub + 1) * 128], otok[:, sub, :], id_b, [Bn["otok"], B_tb], [B_ps["pB"]])
                        acopy(ocT[:, h, cols], pT_b[:, 0:512], [B_ps["pB"]], [B_oc[h][g]])

                LOOK = 3
                for idx in range(len(units) + LOOK):
                    if idx < len(units):
                        front(units[idx])
                    if idx - LOOK >= 0:
                        back(units[idx - LOOK])

        if KSTOP <= 3:
            break
        for g in range(NG):
            cols = slice(g * 512, (g + 1) * 512)
            for oc in range(8):
                pt, bp = nextA()
                for kc in range(8):
                    mm(pt[:, :], w_out[:, kc, oc * 128:(oc + 1) * 128], ocT[:, kc, cols], kc == 0, kc == 7, [B_wout, B_oc[kc][g]], [bp])
                tt(xT[:, oc, cols], xT[:, oc, cols], pt[:, :], ALU.add, [bp, B_xT[oc][g]], [B_xT[oc][g]])
        P.barrier()

        if KSTOP <= 4:
            break
        a3 = Alloc(nc, ARENA, SB_LIMIT, "ff%d" % layer)
        A_BANKS[0] = ["pA0", "pA1", "pA2", "pS0", "pS1", "pO0", "pO1"]
        hT2 = [a3.t("hT%d" % i, [128, 8, 512], BF16) for i in range(2)]
        B_hT2 = [TB("hT0"), TB("hT1")]
        sq = [a3.t("sq%d" % i, [128, 512], BF16) for i in range(2)]
        B_sq = [TB("sq0"), TB("sq1")]
        rs = a3.t("rs", [128, 512], F32)
        B_rs = TB("rs")
        actT = a3.t("actT", [128, NFC, 1024], BF16)
        B_act = [TB("act0"), TB("act1")]
        NWG = 4
        wgu = [a3.t("wgu%d" % i, [128, 8, 2, 128], BF16) for i in range(NWG)]
        B_wgu = [TB("wgu%d" % i) for i in range(NWG)]
        wd = [a3.t("wd%d" % i, [128, NFC, 128], BF16) for i in range(2)]
        B_wd = [TB("wd0"), TB("wd1")]
        sg = [a3.t("sg%d" % i, [128, 512], BF16) for i in range(2)]
        B_sg = [TB("sg0"), TB("sg1")]
        wgu_v = w_gu_d[layer].rearrange("(kc p) n -> p kc n", p=128)
        wd_v = w_dn_d[layer].rearrange("(fc p) n -> p fc n", p=128)
        for hf in range(2):
            for gi in range(2):
                norm_group(2 * hf + gi, "fnw%d" % layer, hT2[gi], B_hT2[gi], sq, B_sq, rs, B_rs)
            for fc in range(NFC):
                i = rr("wgu", NWG)
                dma("pool", wgu[i][:, :, 0, :], wgu_v[:, :, fc * 128:(fc + 1) * 128], [], [B_wgu[i]])
                dma("pool", wgu[i][:, :, 1, :], wgu_v[:, :, FFN + fc * 128:FFN + (fc + 1) * 128], [], [B_wgu[i]])
                for gi in range(2):
                    pg, bg = fm_proj(lambda kc, i=i: wgu[i][:, kc, 0, :], 128, hT2[gi], B_hT2[gi], B_wgu[i])
                    pu, bu = fm_proj(lambda kc, i=i: wgu[i][:, kc, 1, :], 128, hT2[gi], B_hT2[gi], B_wgu[i])
                    k = rr("sg", 2)
                    act(sg[k][:], pg[:, :], AF.Silu, [bg], [B_sg[k]])
                    tt(actT[:, fc, gi * 512:(gi + 1) * 512], pu[:, :], sg[k][:], ALU.mult, [bu, B_sg[k]], [B_act[gi]])
            for oc in range(8):
                i = rr("wd", 2)
                dma("pool", wd[i][:], wd_v[:, :, oc * 128:(oc + 1) * 128], [], [B_wd[i]])
                for gi in range(2):
                    g = 2 * hf + gi
                    cols = slice(g * 512, (g + 1) * 512)
                    pt, bp = nextA()
                    for fc in range(NFC):
                        mm(pt[:, :], wd[i][:, fc, :], actT[:, fc, gi * 512:(gi + 1) * 512], fc == 0, fc == NFC - 1, [B_wd[i], B_act[gi]], [bp])
                    tt(xT[:, oc, cols], xT[:, oc, cols], pt[:, :], ALU.add, [bp, B_xT[oc][g]], [B_xT[oc][g]])
        P.barrier()

    a9 = Alloc(nc, ARENA, SB_LIMIT, "st")
    A_BANKS[0] = ["pA0", "pA1", "pA2"]
    xo = [a9.t("xo%d" % i, [128, D], F32) for i in range(2)]
    B_xo = [TB("xo0"), TB("xo1")]
    B_out = TB("out")
    for t in range(16):
        g = t // 4
        xo_, bxo = xo[t % 2], B_xo[t % 2]
        for hf in range(2):
            pt, bp = nextA()
            for k4 in range(4):
                kc = hf * 4 + k4
                tpose(pt[:, k4 * 128:(k4 + 1) * 128], xT[:, kc, t * 128:(t + 1) * 128], id_f, [B_xT[kc][g], B_cst], [bp])
            if hf == 0:
                vcopy(xo_[:, 0:512], pt[:, :], [bp], [bxo])
            else:
                acopy(xo_[:, 512:1024], pt[:, :], [bp], [bxo])
        dma("sp", out_d[t * 128:(t + 1) * 128, :], xo_[:], [bxo], [B_out])
    P.wait_all("sp", [B_out])

    with nc.Block() as block:
        P.emit(nc, block, es)
    es.close()
    return nc


_CACHE = {}


def kernel(**inputs):
    NL = int(os.environ.get("KNL", DEPTH))
    inputs = {k: np.asarray(v) for k, v in inputs.items()}
    if NL not in _CACHE:
        _CACHE[NL] = build_program(NL)
    nc = _CACHE[NL]
    tb, T, Tg = host_tables()
    x = inputs["x"]
    in_maps = []
    shared = {k: np.ascontiguousarray(inputs[k][:NL], dtype=np.float32) for k in ["w_in", "w_out", "w_mem_kv", "w_gate_up", "w_down"]}
    for c in range(8):
        b, half = c // 2, c % 2
        m = dict(shared)
        m["x"] = np.ascontiguousarray(x[b, half * TOK:(half + 1) * TOK, :], dtype=np.float32)
        m["mem"] = np.ascontiguousarray(inputs["mem"][b], dtype=np.float32)
        m["cst"] = host_consts(inputs, half)
        m["tb"] = tb
        m["Ttab"] = T
        m["Tgtab"] = Tg
        in_maps.append(m)
    res = run_bass_kernel_spmd(nc, in_maps, core_ids=list(range(8)))
    out = np.zeros((4, S, D), np.float32)
    for c in range(8):
        b, half = c // 2, c % 2
        out[b, half * TOK:(half + 1) * TOK, :] = res.results[c]["out"]
    return out
```
